# Optimizing a Trainium2 kernel written in Bass

```python
import jax, jax.numpy as jnp
from jax import lax
import numpy as np

D_MODEL = 1024
BATCH = 8
SEQ = 2048
DEPTH = 2
DEC_BATCH = 128
DEC_SEQ = 4
PAST_LEN = 16384
PAGE_SIZE = 128

LRU_BW = 128
D_RNN = (4 * D_MODEL // 3) // LRU_BW * LRU_BW
LRU_BLOCKS = D_RNN // LRU_BW
CONV_W = 4
LRU_C = 8.0
GLA_HEADS = 4
GLA_DK_TOTAL = D_MODEL // 2
GLA_DV_TOTAL = D_MODEL
GLA_DK = GLA_DK_TOTAL // GLA_HEADS
GLA_DV = GLA_DV_TOTAL // GLA_HEADS
GLA_RANK = 16
GLA_TAU = 16.0
GLA_CHUNK = 16
D_FF = ((8 * D_MODEL // 3 + 255) // 256) * 256
EPS = 1e-6
IN_SIZES = (D_RNN, D_RNN, GLA_DK_TOTAL, GLA_DK_TOTAL, GLA_DV_TOTAL, GLA_DV_TOTAL, GLA_RANK, D_MODEL, D_MODEL)
N_IN = sum(IN_SIZES)

kernel_name = "hybrid_rglru_gla_adaln_decode_step"


def _rmsnorm(x, g):
    xf = x.astype(jnp.float32)
    return xf * lax.rsqrt(jnp.mean(xf * xf, axis=-1, keepdims=True) + EPS) * g.astype(jnp.float32)


def _lin_combine(e1, e2):
    a1, b1 = e1
    a2, b2 = e2
    return a1 * a2, a2 * b1 + b2


def _causal_conv(u, buf, w, b):
    xa = jnp.concatenate([buf.astype(jnp.float32), u], axis=1)
    T = u.shape[1]
    out = b.astype(jnp.float32)
    for i in range(CONV_W):
        out = out + xa[:, i:i + T] * w[i].astype(jnp.float32)
    return out, xa[:, -(CONV_W - 1):]


def _rg_lru(xc, h0, wa, ba, wx, bx, lam):
    B, T, _ = xc.shape
    xb = xc.reshape(B, T, LRU_BLOCKS, LRU_BW)
    r = jax.nn.sigmoid(jnp.einsum('btnw,nwv->btnv', xb, wa.astype(jnp.float32)).reshape(B, T, D_RNN) + ba)
    i = jax.nn.sigmoid(jnp.einsum('btnw,nwv->btnv', xb, wx.astype(jnp.float32)).reshape(B, T, D_RNN) + bx)
    log_a = -LRU_C * jax.nn.softplus(-lam.astype(jnp.float32)) * r
    a = jnp.exp(log_a)
    b = jnp.sqrt(-jnp.expm1(2.0 * log_a)) * (i * xc)
    b = b.at[:, 0].add(a[:, 0] * h0.astype(jnp.float32))
    _, h = lax.associative_scan(_lin_combine, (a, b), axis=1)
    return h, h[:, -1]


def _gla_chunked(q, k, v, g, S0):
    B, T, H, K = q.shape
    V = v.shape[-1]
    C = min(GLA_CHUNK, T)
    n = -(-T // C)
    pad = n * C - T

    def blocks(z):
        z = jnp.pad(z, ((0, 0), (0, pad), (0, 0), (0, 0)))
        return z.reshape(B, n, C, H, z.shape[-1]).transpose(1, 0, 3, 2, 4)

    mask = jnp.tril(jnp.ones((C, C), dtype=bool))[:, :, None]

    def step(S, inp):
        qc, kc, vc, gc = inp
        bc = jnp.cumsum(gc, axis=2)
        decay = jnp.exp(jnp.where(mask, bc[:, :, :, None, :] - bc[:, :, None, :, :], -jnp.inf))
        att = jnp.einsum('bhik,bhjk,bhijk->bhij', qc, kc, decay)
        o = jnp.einsum('bhij,bhjv->bhiv', att, vc) + jnp.einsum('bhik,bhkv->bhiv', qc * jnp.exp(bc), S)
        b_last = bc[:, :, -1:, :]
        S = jnp.exp(b_last[:, :, 0, :])[..., None] * S + jnp.einsum('bhjk,bhjv->bhkv', kc * jnp.exp(b_last - bc), vc)
        return S, o

    S_T, o = lax.scan(step, S0.astype(jnp.float32), (blocks(q), blocks(k), blocks(v), blocks(g)))
    o = o.transpose(1, 0, 3, 2, 4).reshape(B, n * C, H, V)[:, :T]
    return o, S_T


def _layer(x, c, conv_buf, h0, S0, norm1_g, norm2_g, ada_w, ada_b, w_in, conv_w, conv_b,
           lru_wa, lru_ba, lru_wx, lru_bx, lru_lambda, gla_wa2, gla_ba, gla_norm_g,
           proj_a, proj_b, w_out, ffn_w1, ffn_w2):
    B, T, _ = x.shape
    mod = jax.nn.silu(c.astype(jnp.float32)) @ ada_w + ada_b
    sh1, sc1, gt1, sh2, sc2, gt2 = jnp.split(mod[:, None, :], 6, axis=-1)
    h = _rmsnorm(x, norm1_g) * (1.0 + sc1) + sh1
    u = h @ w_in
    idx = [int(s) for s in np.cumsum(IN_SIZES)[:-1]]
    u_x, u_g, u_q, u_k, u_v, u_r, u_lr, u_ga, u_gb = jnp.split(u, idx, axis=-1)
    xc, new_conv = _causal_conv(u_x, conv_buf, conv_w, conv_b)
    h_lru, h_T = _rg_lru(xc, h0, lru_wa, lru_ba, lru_wx, lru_bx, lru_lambda)
    y_a = h_lru * jax.nn.gelu(u_g)
    q = u_q.reshape(B, T, GLA_HEADS, GLA_DK) * (GLA_DK ** -0.5)
    k = u_k.reshape(B, T, GLA_HEADS, GLA_DK)
    v = u_v.reshape(B, T, GLA_HEADS, GLA_DV)
    g = (jax.nn.log_sigmoid(u_lr @ gla_wa2 + gla_ba) / GLA_TAU).reshape(B, T, GLA_HEADS, GLA_DK)
    o, S_T = _gla_chunked(q, k, v, g, S0)
    o = o * lax.rsqrt(jnp.mean(o * o, axis=-1, keepdims=True) + EPS) * gla_norm_g.reshape(GLA_HEADS, GLA_DV)
    y_b = o.reshape(B, T, GLA_DV_TOTAL) * jax.nn.silu(u_r)
    m = jax.nn.sigmoid(u_ga) * (y_a @ proj_a) + jax.nn.sigmoid(u_gb) * (y_b @ proj_b)
    x = (x + gt1 * (m @ w_out)).astype(x.dtype)
    h2 = _rmsnorm(x, norm2_g) * (1.0 + sc2) + sh2
    f1, f2 = jnp.split(h2 @ ffn_w1, 2, axis=-1)
    x = (x + gt2 * ((jax.nn.silu(f1) * f2) @ ffn_w2)).astype(x.dtype)
    return x, new_conv.astype(x.dtype), h_T.astype(x.dtype), S_T.astype(x.dtype)


def setup_inputs(seed: int = 0) -> dict:
    key = jax.random.key(seed)
    ks = jax.random.split(key, 32)
    nrm = jax.random.normal
    D = D_MODEL
    u = jax.random.uniform(ks[20], (DEPTH, D_RNN), minval=0.9, maxval=0.999)
    s = u ** (1.0 / LRU_C)
    lru_lambda = jnp.log(s) - jnp.log1p(-s)
    return {
        "x_prompt": nrm(ks[0], (BATCH, SEQ, D)),
        "x_sample": nrm(ks[1], (DEC_BATCH, DEC_SEQ, D)),
        "c_prompt": nrm(ks[2], (BATCH, D)),
        "c_sample": nrm(ks[3], (DEC_BATCH, D)),
        "state_conv": nrm(ks[4], (DEPTH, DEC_BATCH, CONV_W - 1, D_RNN)) * 0.5,
        "state_lru": nrm(ks[5], (DEPTH, DEC_BATCH, D_RNN)) * 0.5,
        "state_gla": nrm(ks[6], (DEPTH, DEC_BATCH, GLA_HEADS, GLA_DK, GLA_DV)) * 0.3,
        "norm1_g": 1.0 + 0.05 * nrm(ks[7], (DEPTH, D)),
        "norm2_g": 1.0 + 0.05 * nrm(ks[8], (DEPTH, D)),
        "ada_w": nrm(ks[9], (DEPTH, D, 6 * D)) * (0.3 * D ** -0.5),
        "ada_b": 0.05 * nrm(ks[10], (DEPTH, 6 * D)),
        "w_in": nrm(ks[11], (DEPTH, D, N_IN)) * D ** -0.5,
        "conv_w": nrm(ks[12], (DEPTH, CONV_W, D_RNN)) * CONV_W ** -0.5,
        "conv_b": 0.02 * nrm(ks[13], (DEPTH, D_RNN)),
        "lru_wa": nrm(ks[14], (DEPTH, LRU_BLOCKS, LRU_BW, LRU_BW)) * LRU_BW ** -0.5,
        "lru_ba": 0.02 * nrm(ks[15], (DEPTH, D_RNN)),
        "lru_wx": nrm(ks[16], (DEPTH, LRU_BLOCKS, LRU_BW, LRU_BW)) * LRU_BW ** -0.5,
        "lru_bx": 0.02 * nrm(ks[17], (DEPTH, D_RNN)),
        "lru_lambda": lru_lambda,
        "gla_wa2": nrm(ks[18], (DEPTH, GLA_RANK, GLA_DK_TOTAL)) * GLA_RANK ** -0.5,
        "gla_ba": 1.0 + 0.1 * nrm(ks[19], (DEPTH, GLA_DK_TOTAL)),
        "gla_norm_g": 1.0 + 0.05 * nrm(ks[21], (DEPTH, GLA_DV_TOTAL)),
        "proj_a": nrm(ks[22], (DEPTH, D_RNN, D)) * D_RNN ** -0.5,
        "proj_b": nrm(ks[23], (DEPTH, GLA_DV_TOTAL, D)) * GLA_DV_TOTAL ** -0.5,
        "w_out": nrm(ks[24], (DEPTH, D, D)) * D ** -0.5,
        "ffn_w1": nrm(ks[25], (DEPTH, D, 2 * D_FF)) * D ** -0.5,
        "ffn_w2": nrm(ks[26], (DEPTH, D_FF, D)) * D_FF ** -0.5,
        "final_g": 1.0 + 0.05 * nrm(ks[27], (D,)),
    }


def reference(x_prompt, x_sample, c_prompt, c_sample, state_conv, state_lru, state_gla,
              norm1_g, norm2_g, ada_w, ada_b, w_in, conv_w, conv_b, lru_wa, lru_ba, lru_wx, lru_bx,
              lru_lambda, gla_wa2, gla_ba, gla_norm_g, proj_a, proj_b, w_out, ffn_w1, ffn_w2, final_g):
    Bp = x_prompt.shape[0]
    dt = x_prompt.dtype
    xp, xs = x_prompt, x_sample
    conv_p, lru_p, gla_p, conv_s, lru_s, gla_s = [], [], [], [], [], []
    for l in range(DEPTH):
        p = (norm1_g[l], norm2_g[l], ada_w[l], ada_b[l], w_in[l], conv_w[l], conv_b[l],
             lru_wa[l], lru_ba[l], lru_wx[l], lru_bx[l], lru_lambda[l], gla_wa2[l], gla_ba[l],
             gla_norm_g[l], proj_a[l], proj_b[l], w_out[l], ffn_w1[l], ffn_w2[l])
        xp, cb, hT, ST = _layer(xp, c_prompt,
                                jnp.zeros((Bp, CONV_W - 1, D_RNN), dt),
                                jnp.zeros((Bp, D_RNN), dt),
                                jnp.zeros((Bp, GLA_HEADS, GLA_DK, GLA_DV), dt), *p)
        conv_p.append(cb); lru_p.append(hT); gla_p.append(ST)
        xs, cb, hT, ST = _layer(xs, c_sample, state_conv[l], state_lru[l], state_gla[l], *p)
        conv_s.append(cb); lru_s.append(hT); gla_s.append(ST)
    y_prompt = _rmsnorm(xp, final_g).astype(x_prompt.dtype)
    y_sample = _rmsnorm(xs, final_g).astype(x_sample.dtype)
    return (y_prompt, y_sample, jnp.stack(conv_p), jnp.stack(lru_p), jnp.stack(gla_p),
            jnp.stack(conv_s), jnp.stack(lru_s), jnp.stack(gla_s))
```

```python
import contextlib
import numpy as np
import concourse.bass as bass
import concourse.mybir as mybir
from concourse.bass_utils import run_bass_kernel_spmd

F32 = mybir.dt.float32
BF16 = mybir.dt.bfloat16
AF = mybir.ActivationFunctionType
ALU = mybir.AluOpType

NCORES = 8
D = 1024
KC = 8
SEQ = 2048
DEPTH = 2
NB = 10
DRNN = 1280
NH = 4
DK = 128
DVH = 256
DFF = 2816
FC = 22
NIN = 7696
EPS = 1e-6
NSEQ = 16
TS = 4
WS = NSEQ * TS
NTOK = SEQ + WS
TW = 512
O_X, O_G, O_Q, O_K, O_V, O_R, O_LR, O_GA, O_GB = 0, 1280, 2560, 3072, 3584, 4608, 5632, 5648, 6672
SLOT = 4096
NSLOT = 4


def _kc_pack(W):
    nk = W.shape[0] // 128
    return np.ascontiguousarray(W.reshape(nk, 128, -1).transpose(1, 0, 2).reshape(128, -1))


def _unit_list():
    u = [("Ag0", 4096), ("Ag1", 4096), ("Ag2", 2048)]
    u += [("Ax%d" % n, 1280) for n in range(NB)]
    u += [("Glr", 384), ("Gv0", 4096), ("Gv1", 4096), ("Gk", 4096), ("Gq", 4096)]
    u += [("Gr%d" % h, 2048) for h in range(NH)]
    for j in range(KC):
        u += [("Cg%d" % j, 2048), ("Cp%d" % j, 2304)]
    u += [("Wo0", 4096), ("Wo1", 4096)]
    u += [("F1_%d" % i, 2048) for i in range(FC)]
    u += [("F2_%d" % j, 2816) for j in range(KC)]
    return u


UNITS = _unit_list()
UOFF = {}
_o = 0
for _n, _s in UNITS:
    UOFF[_n] = (_o, _s)
    _o += _s
WPACK_N = _o


def _pack_layer(w_in, lru_wa, lru_wx, proj_a, proj_b, w_out, ffn_w1, ffn_w2):
    parts = {}
    parts["Ag0"] = _kc_pack(w_in[:, O_G:O_G + 512])
    parts["Ag1"] = _kc_pack(w_in[:, O_G + 512:O_G + 1024])
    parts["Ag2"] = _kc_pack(w_in[:, O_G + 1024:O_G + 1280])
    for n in range(NB):
        gw = [lru_wa[n - 1], lru_wx[n - 1]] if n > 0 else [np.zeros((128, 256), np.float32)]
        parts["Ax%d" % n] = np.concatenate([_kc_pack(w_in[:, O_X + 128 * n:O_X + 128 * n + 128])] + gw, axis=1)
    parts["Glr"] = np.concatenate([_kc_pack(w_in[:, O_LR:O_LR + 16]), lru_wa[NB - 1], lru_wx[NB - 1]], axis=1)
    parts["Gk"] = _kc_pack(w_in[:, O_K:O_K + 512])
    parts["Gq"] = _kc_pack(w_in[:, O_Q:O_Q + 512])
    parts["Gv0"] = _kc_pack(w_in[:, O_V:O_V + 512])
    parts["Gv1"] = _kc_pack(w_in[:, O_V + 512:O_V + 1024])
    for h in range(NH):
        parts["Gr%d" % h] = _kc_pack(w_in[:, O_R + 256 * h:O_R + 256 * h + 256])
    for j in range(KC):
        parts["Cg%d" % j] = _kc_pack(np.concatenate(
            [w_in[:, O_GA + 128 * j:O_GA + 128 * j + 128], w_in[:, O_GB + 128 * j:O_GB + 128 * j + 128]], axis=1))
        parts["Cp%d" % j] = np.concatenate(
            [_kc_pack(proj_a[:, 128 * j:128 * j + 128]), _kc_pack(proj_b[:, 128 * j:128 * j + 128])], axis=1)
    parts["Wo0"] = _kc_pack(w_out[:, 0:512])
    parts["Wo1"] = _kc_pack(w_out[:, 512:1024])
    for i in range(FC):
        parts["F1_%d" % i] = _kc_pack(np.concatenate(
            [ffn_w1[:, 128 * i:128 * i + 128], ffn_w1[:, DFF + 128 * i:DFF + 128 * i + 128]], axis=1))
    for j in range(KC):
        parts["F2_%d" % j] = _kc_pack(ffn_w2[:, 128 * j:128 * j + 128])
    out = np.empty((128, WPACK_N), np.float32)
    for n, s in UNITS:
        o, _ = UOFF[n]
        assert parts[n].shape == (128, s), (n, parts[n].shape, s)
        out[:, o:o + s] = parts[n]
    return out


SM = {}
_o = 0
for _l in range(DEPTH):
    for _n, _s in (("n1g", 8), ("n2g", 8), ("adab", 48), ("convw", 40), ("convb", 10), ("ba", 10),
                   ("bx", 10), ("lam", 10), ("gng", 8)):
        SM[(_n, _l)] = (_o, _s)
        _o += _s
SM["fg"] = (_o, 8)
_o += 8
SM_N = _o

C_TRI, C_U, C_MASK, C_TRIS, C_US, C_MASKS, C_SEQM = 0, 128, 256, 384, 448, 512, 576
C_N = 592


def _fm(v, nch):
    return np.ascontiguousarray(v.reshape(nch, 128).T)


def _consts():
    c = np.zeros((128, C_N), np.float32)
    j = np.arange(128)[:, None]
    i = np.arange(128)[None, :]
    c[:, C_TRI:C_TRI + 128] = (j <= i) * (-1.0 / 16.0)
    c[:, C_U:C_U + 128] = (j > i) * (-1.0 / 16.0)
    c[:, C_MASK:C_MASK + 128] = (j <= i) * 1.0
    j = np.arange(64)[:, None]
    i = np.arange(64)[None, :]
    same = (j // TS) == (i // TS)
    c[:64, C_TRIS:C_TRIS + 64] = (same & (j <= i)) * (-1.0 / 16.0)
    c[:64, C_US:C_US + 64] = (same & (j > i)) * (-1.0 / 16.0)
    c[:64, C_MASKS:C_MASKS + 64] = (same & (j <= i)) * 1.0
    c[:64, C_SEQM:C_SEQM + 16] = ((np.arange(64)[:, None] // TS) == np.arange(16)[None, :]) * 1.0
    return c


_STRICT = False


class _Op:
    __slots__ = ("eng", "fn", "deps", "dma", "sig", "sigval", "needs")

    def __init__(self, eng, fn, deps, dma):
        self.eng, self.fn, self.deps, self.dma = eng, fn, deps, dma
        self.sig = False
        self.sigval = 0
        self.needs = []


class Prog:
    def __init__(self):
        self.ops = []
        self.lastw = {}
        self.rd = {}

    def add(self, eng, fn, reads=(), writes=(), dma=None):
        idx = len(self.ops)
        deps = {}
        for k in reads:
            lw = self.lastw.get(k)
            if lw is not None:
                deps[lw] = "raw"
        for k in writes:
            lw = self.lastw.get(k)
            if lw is not None and lw not in deps:
                deps[lw] = "waw"
            r = self.rd.get(k)
            if r:
                for e, v in r.items():
                    if e == "dmas":
                        for x in v:
                            deps.setdefault(x, "war")
                    else:
                        deps.setdefault(v, "war")
        for k in reads:
            r = self.rd.setdefault(k, {})
            if dma is not None:
                r.setdefault("dmas", []).append(idx)
            else:
                r[eng] = idx
        for k in writes:
            self.lastw[k] = idx
            self.rd[k] = {}
        deps.pop(idx, None)
        self.ops.append(_Op(eng, fn, deps, dma))
        return idx

    def finalize(self):
        ops = self.ops
        for b in ops:
            for ai, kind in b.deps.items():
                a = ops[ai]
                if a.dma is not None:
                    need = True
                elif a.eng == b.eng:
                    if a.eng == "pe":
                        need = False
                    elif b.dma is not None:
                        need = True
                    else:
                        need = kind == "raw" or _STRICT
                else:
                    need = True
                if need:
                    a.sig = True
                    b.needs.append(ai)
        cnt = {}
        for o in ops:
            if o.dma is not None:
                cnt[o.dma] = cnt.get(o.dma, 0) + 16
                o.sigval = cnt[o.dma]
                o.sig = True
            elif o.sig:
                cnt[o.eng] = cnt.get(o.eng, 0) + 1
                o.sigval = cnt[o.eng]
        self.final_counts = cnt

    def emit(self, eng_name, eng, sems):
        waited = {}
        ops = self.ops
        for o in ops:
            if o.eng != eng_name:
                continue
            req = {}
            for ai in o.needs:
                a = ops[ai]
                sk = a.dma if a.dma is not None else a.eng
                if a.sigval > req.get(sk, 0):
                    req[sk] = a.sigval
            for sk, v in req.items():
                if waited.get(sk, 0) < v:
                    eng.wait_ge(sems[sk], v)
                    waited[sk] = v
            ins = o.fn(eng)
            if o.sig:
                sk = o.dma if o.dma is not None else o.eng
                ins.then_inc(sems[sk], 16 if o.dma is not None else 1)
        return waited


def _flat(*items):
    out = []
    for it in items:
        if isinstance(it, (tuple, list)):
            out.extend(_flat(*it))
        elif it is not None:
            out.append(it)
    return tuple(out)


def build_program():
    nc = bass.Bass("TRN2", target_bir_lowering=False)
    P = Prog()
    es = contextlib.ExitStack()

    def dram_in(name, shape):
        return nc.dram_tensor(name, list(shape), F32, kind="ExternalInput").ap()

    def dram_out(name, shape):
        return nc.dram_tensor(name, list(shape), F32, kind="ExternalOutput").ap()

    xT_d = dram_in("xT", [D, NTOK])
    cT_d = dram_in("cT", [D, 17])
    convs_d = dram_in("convs", [128, DEPTH * NB * NSEQ * 3])
    lrus_d = dram_in("lrus", [128, DEPTH * NB * NSEQ])
    glas_d = dram_in("glas", [DEPTH, 128, NSEQ * NH * DVH])
    small_d = dram_in("small", [128, SM_N])
    wa2e_d = dram_in("wa2e", [17, DEPTH * 512])
    consts_d = dram_in("consts", [128, C_N])
    adaw_d = dram_in("adaw", [DEPTH, 128, 12 * 4096])
    wpack_d = dram_in("wpack", [DEPTH, 128, WPACK_N])

    yT_d = dram_out("yT", [D, NTOK])
    ncp_d = dram_out("ncp", [128, DEPTH * NB * 3])
    nlp_d = dram_out("nlp", [128, DEPTH * NB])
    ngp_d = dram_out("ngp", [DEPTH, 128, NH * DVH])
    ncs_d = dram_out("ncs", [128, DEPTH * NB * NSEQ * 3])
    nls_d = dram_out("nls", [128, DEPTH * NB * NSEQ])
    ngs_d = dram_out("ngs", [DEPTH, 128, NSEQ * NH * DVH])

    dma_sems = []

    def sb(name, shape, dt=F32, dma=False):
        t = es.enter_context(nc.sbuf_tensor("sb_" + name, list(shape), dt))
        if dma:
            dma_sems.append("d_" + name)
        return t

    with es:
        wring = [sb("wr%d" % i, [128, SLOT], BF16, dma=True) for i in range(NSLOT)]
        S = sb("S", [128, DEPTH, NH, DVH], F32, dma=True)
        Sbf = sb("Sbf", [128, NH, DVH], BF16)
        Sbfc = sb("Sbfc", [128, 3, DVH], BF16)
        Sx = sb("Sx", [128, DVH], F32)
        modT = sb("modT", [128, DEPTH, 48, 17], F32)
        small = sb("small", [128, SM_N], F32, dma=True)
        slru = sb("slru", [128, DEPTH, NB], F32)
        hslru = sb("hslru", [128, DEPTH, NB], F32)
        sptmp = sb("sptmp", [128, DEPTH, NB], F32)
        hbias = sb("hbias", [128, DEPTH, 2, NB], F32)
        consts = sb("consts", [128, C_N], F32, dma=True)
        wa2e = sb("wa2e", [17, DEPTH * 512], F32, dma=True)
        ones_bf = sb("ones_bf", [128, 128], BF16)
        cT = sb("cT", [128, KC, 17], F32, dma=True)
        csb = sb("csb", [128, KC, 17], BF16)
        convc = sb("convc", [128, DEPTH, NB, 3], F32, dma=True)
        lruc = sb("lruc", [128, DEPTH, NB], F32, dma=True)
        convst = sb("convst", [128, DEPTH, NB, NSEQ, 3], F32, dma=True)
        lrust = sb("lrust", [128, DEPTH, NB, NSEQ], F32, dma=True)
        xas = sb("xas", [128, NSEQ, 7], F32)
        HS = 8
        khm = sb("khm", [64, HS, 128], BF16)
        GS = 4
        s0b = sb("s0b", [128, NSEQ, DVH], BF16, dma=True)
        sst01 = [sb("sst%d" % i, [128, GS, DVH], F32, dma=True) for i in range(2)]
        dummy = sb("dummy", [128, 2], F32)
        dpre = sb("dpre", [128, 2], F32)
        ccorr = sb("ccorr", [128, NB, 3], F32)
        ctmp = sb("ctmp", [128, NB], F32)
        swide = sb("swide", [64, 512], F32)

        psum = [es.enter_context(nc.psum_tensor("ps%d" % i, [128, TW], F32)) for i in range(8)]

        class Grp:
            pass

        def make_group(kind):
            g = Grp()
            g.kind = kind
            g.pre = kind + ":"
            W = TW if kind == "p" else WS
            g.W = W
            g.CH = 128 if kind == "p" else 64
            g.nch = W // g.CH
            n = kind
            g.x = sb(n + "x", [128, KC, W], F32)
            dma_sems.extend(["d_" + n + "x%d" % c for c in range(KC)])
            g.xsq = sb(n + "xsq", [128, KC, W], BF16)
            g.m = g.xsq
            g.h = sb(n + "h", [128, KC, W], BF16)
            g.yag = sb(n + "yag", [128, NB, W], BF16)
            g.yb = sb(n + "yb", [128, KC, W], BF16)
            words = FC * W // 2
            if kind == "s":
                words = max(words, 9 * W, 512 + 256 + 512)
            g.big = sb(n + "big", [128, words], F32)
            big = g.big
            g.ff = big[:, 0:FC * W // 2].bitcast(BF16).rearrange("p (c w) -> p c w", w=W)
            nch = g.nch
            g.v_tm = big[:, 0:nch * 512].bitcast(BF16).rearrange("p (c w) -> p c w", w=1024)
            g.khat = big[:, nch * 512:nch * 768].bitcast(BF16).rearrange("p (c w) -> p c w", w=512)
            g.gT = big[:, nch * 768:nch * 1280].rearrange("p (c w) -> p c w", w=512)
            g.lbuf = [big[:, i * W:(i + 1) * W] for i in range(9)]
            g.qt = sb(n + "qt", [128, NH, W], BF16)
            g.kt = sb(n + "kt", [128, NH, W], BF16)
            g.at_bf = sb(n + "at_bf", [g.CH, nch, g.CH], BF16)
            g.ulr = sb(n + "ulr", [32, W], F32)
            g.dec = sb(n + "dec", [128, NH, 16], F32)
            g.NF = 8
            g.fsc = [sb(n + "fs%d" % i, [128, W], F32, dma=True) for i in range(g.NF)]
            g.NBF = 6
            g.bsc = [sb(n + "bs%d" % i, [128, W], BF16) for i in range(g.NBF)]
            g.st = {"ps": 0, "fs": 0, "bs": 0, "psw": 0, "psh": 0}
            g.LKEYS = tuple(g.pre + "lb%d" % i for i in range(9))
            g.GKEYS = (g.pre + "gT", g.pre + "khat", g.pre + "vtm")
            g.FKEYS = (g.pre + "ff",)
            return g

        gp = make_group("p")
        gs = make_group("s")
        _xf = gp.xsq[:].rearrange("p c w -> p (c w)").bitcast(F32)
        sst = [sst01[0][:], sst01[1][:],
               _xf[:, 0:GS * DVH].rearrange("p (s v) -> p s v", v=DVH),
               _xf[:, GS * DVH:2 * GS * DVH].rearrange("p (s v) -> p s v", v=DVH)]
        sstk = ["sst0", "sst1", "sst2", "sst3"]
        dma_sems.extend(["d_sst2", "d_sst3"])

        eng_sems = {}
        for e in ("pe", "act", "dve", "pool", "sp"):
            eng_sems[e] = es.enter_context(nc.semaphore("sem_" + e))
        sems = dict(eng_sems)
        for k in dma_sems:
            sems[k] = es.enter_context(nc.semaphore(k))

        def PS(g):
            if g.kind == "p":
                i = g.st["ps"]
                g.st["ps"] = (i + 1) % 4
                return psum[i], ("ps%d" % i,)
            if g.st.get("restrict"):
                return psum[4], ("ps4",)
            i = g.st["ps"]
            g.st["ps"] = (i + 1) % 2
            return psum[4 + i], ("ps%d" % (4 + i),)

        PSW = PS
        PSH = PS

        def PO(g, vc):
            if g.kind == "p":
                return psum[6 + vc], ("ps%d" % (6 + vc),)
            return psum[5][:, vc * 64:(vc + 1) * 64], ("ps5",)

        def WIDE(g):
            if g.kind == "p":
                return FS(g)
            return swide, "s:swide"

        def FS(g):
            i = g.st["fs"]
            g.st["fs"] = (i + 1) % g.NF
            return g.fsc[i], g.pre + "fs%d" % i

        def BS(g):
            i = g.st["bs"]
            g.st["bs"] = (i + 1) % g.NBF
            return g.bsc[i], g.pre + "bs%d" % i

        def mm(out, lhsT, rhs, start, stop, reads, writes):
            P.add("pe", lambda e, o=out, l=lhsT, r=rhs, a=start, b=stop: e.matmul(o, lhsT=l, rhs=r, start=a, stop=b),
                  reads=_flat(reads), writes=_flat(writes))

        def act(out, in_, func, reads, writes, bias=None, scale=None):
            kw = {}
            if bias is not None:
                kw["bias"] = bias
            if scale is not None:
                kw["scale"] = scale
            P.add("act", lambda e, o=out, i=in_, f=func, kw=kw: e.activation(out=o, in_=i, func=f, **kw),
                  reads=_flat(reads), writes=_flat(writes))

        def tt(out, in0, in1, op, reads, writes):
            P.add("dve", lambda e, o=out, a=in0, b=in1, p=op: e.tensor_tensor(out=o, in0=a, in1=b, op=p),
                  reads=_flat(reads), writes=_flat(writes))

        def stt(out, in0, scalar, in1, op0, op1, reads, writes):
            P.add("dve", lambda e, o=out, a=in0, s=scalar, b=in1, p0=op0, p1=op1:
                  e.scalar_tensor_tensor(out=o, in0=a, scalar=s, in1=b, op0=p0, op1=p1),
                  reads=_flat(reads), writes=_flat(writes))

        def ts(out, in0, s1, s2, op0, op1, reads, writes):
            if s2 is None:
                P.add("dve", lambda e, o=out, a=in0, s=s1, p0=op0: e.tensor_scalar(out=o, in0=a, scalar1=s, scalar2=None, op0=p0),
                      reads=_flat(reads), writes=_flat(writes))
            else:
                P.add("dve", lambda e, o=out, a=in0, s=s1, t=s2, p0=op0, p1=op1:
                      e.tensor_scalar(out=o, in0=a, scalar1=s, scalar2=t, op0=p0, op1=p1),
                      reads=_flat(reads), writes=_flat(writes))

        def vcopy(out, in_, reads, writes):
            P.add("dve", lambda e, o=out, i=in_: e.tensor_copy(out=o, in_=i), reads=_flat(reads), writes=_flat(writes))

        def vmemset(out, val, writes):
            P.add("dve", lambda e, o=out, v=val: e.memset(o, v), reads=(), writes=_flat(writes))

        def scan(out, a, b, initial, reads, writes):
            P.add("dve", lambda e, o=out, a=a, b=b, ini=initial:
                  e.tensor_tensor_scan(out=o, data0=a, data1=b, initial=ini, op0=ALU.mult, op1=ALU.add),
                  reads=_flat(reads), writes=_flat(writes))

        def dma(q, out, in_, reads, writes, sem):
            P.add(q, lambda e, o=out, i=in_: e.dma_start(out=o, in_=i), reads=_flat(reads), writes=_flat(writes), dma=sem)

        def fence(keys):
            P.add("dve", lambda e: e.memset(dummy[:], 0.0), reads=(), writes=_flat(keys, "dummy"))

        def smc(name, l=None):
            o, s = SM[(name, l)] if l is not None else SM[name]
            return small[:, o:o + s]

        wstream = []
        ADA_SETUP = 4
        for u in range(ADA_SETUP):
            wstream.append((adaw_d[0, :, u * 4096:(u + 1) * 4096], 4096))
        ADA_AFTER = {}
        for k in range(8):
            ADA_AFTER[(0, k)] = [(0, 4 + k)]
            ADA_AFTER[(1, k)] = [(1, 4 + k)]
        ADA_AFTER[(0, 8)] = [(1, 0), (1, 1)]
        ADA_AFTER[(0, 9)] = [(1, 2), (1, 3)]
        NT = SEQ // TW
        SEPARATE = False
        unit_index = {}
        for ti in range(NT + (1 if SEPARATE else 0)):
            for l in range(DEPTH):
                for n, s in UNITS:
                    o, _ = UOFF[n]
                    unit_index[(ti, l, n)] = len(wstream)
                    wstream.append((wpack_d[l, :, o:o + s], s))
                    if ti == 0 and n.startswith("Ax"):
                        for (la, u) in ADA_AFTER.get((l, int(n[2:])), []):
                            unit_index[(0, l, "ADA_%d_%d" % (la, u))] = len(wstream)
                            wstream.append((adaw_d[la, :, u * 4096:(u + 1) * 4096], 4096))
        ws = {"loaded": 0, "cur": -1}

        def acquire(u):
            assert u == ws["cur"] + 1, (u, ws["cur"])
            ws["cur"] = u
            lim = min(len(wstream), u + NSLOT)
            while ws["loaded"] < lim:
                j = ws["loaded"]
                src, n = wstream[j]
                sl = j % NSLOT
                dma("pool", wring[sl][:, 0:n], src, reads=(), writes=("wr%d" % sl,), sem="d_wr%d" % sl)
                ws["loaded"] += 1
            sl = u % NSLOT
            return wring[sl], "wr%d" % sl

        dma("sp", small[:, :], small_d[:, :], (), ("small",), "d_small")
        dma("sp", consts[:, :], consts_d[:, :], (), ("consts",), "d_consts")
        dma("sp", wa2e[:, :], wa2e_d[:, :], (), ("wa2e",), "d_wa2e")
        dma("sp", cT[:], cT_d.rearrange("(c p) s -> p c s", p=128), (), ("cT",), "d_cT")
        dma("sp", convst[:].rearrange("p l n s t -> p (l n s t)"), convs_d[:, :], (), ("convst",), "d_convst")
        dma("sp", lrust[:].rearrange("p l n s -> p (l n s)"), lrus_d[:, :], (), ("lrust",), "d_lrust")
        vmemset(ones_bf[:], 1.0, ("ones",))
        vmemset(dpre[:], 1.0, ("dpre",))

        def preload(func):
            P.add("act", lambda e, f=func: e.activation(out=dpre[:, 1:2], in_=dpre[:, 0:1], func=f),
                  reads=("dpre",), writes=("dpre_junk",))
        vmemset(gp.ulr[:], 1.0, ("p:ulr",))
        vmemset(gs.ulr[:], 1.0, ("s:ulr",))
        vmemset(convc[:].rearrange("p l n t -> p (l n t)"), 0.0, ("convc",))
        vmemset(lruc[:].rearrange("p l n -> p (l n)"), 0.0, ("lruc",))
        vmemset(S[:].rearrange("p l h v -> p (l h v)"), 0.0, ("S0", "S1"))
        for l in range(DEPTH):
            act(sptmp[:, l, :], smc("lam", l), AF.Exp, ("small",), ("sptmp",), scale=-1.0)
        for l in range(DEPTH):
            act(sptmp[:, l, :], sptmp[:, l, :], AF.Ln, ("sptmp",), ("sptmp",), bias=1.0)
        sp2 = sptmp[:].rearrange("p l n -> p (l n)")
        ts(slru[:].rearrange("p l n -> p (l n)"), sp2, -8.0, None, ALU.mult, None, ("sptmp",), ("slru",))
        ts(hslru[:].rearrange("p l n -> p (l n)"), sp2, -4.0, None, ALU.mult, None, ("sptmp",), ("hslru",))
        for l in range(DEPTH):
            ts(hbias[:, l, 0, :], smc("ba", l), 0.5, None, ALU.mult, None, ("small",), ("hbias",))
            ts(hbias[:, l, 1, :], smc("bx", l), 0.5, None, ALU.mult, None, ("small",), ("hbias",))
        act(csb[:], cT[:], AF.Silu, ("cT",), ("csb",))
        def ada_unit(l, u, wt, wk, pst, pk):
            wv = wt[:, 0:4096].rearrange("p (k c) -> p k c", c=512)
            for jj in range(4):
                for kc in range(KC):
                    mm(pst[:, jj * 17:(jj + 1) * 17], wv[:, kc, jj * 128:(jj + 1) * 128], csb[:, kc, :],
                       kc == 0, kc == KC - 1, (wk, "csb"), pk)
            o, _ = SM[("adab", l)]
            tt(modT[:, l, u * 4:(u + 1) * 4, :], pst[:, 0:68].rearrange("p (j s) -> p j s", s=17),
               small[:, o + u * 4:o + u * 4 + 4].unsqueeze(2).to_broadcast([128, 4, 17]),
               ALU.add, (pk, "small"), ("modT%d" % l,))

        def ada_finish(l, grp, gname):
            mv = modT[:, l, grp * 8:(grp + 1) * 8, :]
            ts(mv, mv, 1.0, None, ALU.add, None, ("modT%d" % l,), ("modT%d" % l,))
            tt(mv, mv, smc(gname, l).unsqueeze(2).to_broadcast([128, 8, 17]), ALU.mult, ("modT%d" % l, "small"),
               ("modT%d" % l,))

        def ada_after(l, u):
            if u == 3:
                ada_finish(l, 1, "n1g")
            if u == 9:
                ada_finish(l, 4, "n2g")

        for u in range(ADA_SETUP):
            wt, wk = acquire(u)
            pst, pk = PS(gp)
            ada_unit(0, u, wt, wk, pst, pk)
            ada_after(0, u)

        def mod_col(l, grp, c):
            return modT[:, l, grp * 8 + c, 0:1]

        def mod_seq(l, grp, c):
            return modT[:, l, grp * 8 + c, 1:17].unsqueeze(2).to_broadcast([128, NSEQ, TS])

        def v3(ap):
            return ap.rearrange("p (s t) -> p s t", t=TS)

        def rms_stats(g, scale_div):
            W = g.W
            K = g.pre
            if g.kind == "p" and g.st.get("stats_ready"):
                flush_stat(g)
                g.st["stats_ready"] = False
                pst, pk = psum[7], ("ps7",)
                ft, fk = FS(g)
                act(ft[:, 0:W], pst[:, 0:W], AF.Ln, pk, fk, bias=EPS, scale=1.0 / scale_div)
                act(pst[:, 0:W], ft[:, 0:W], AF.Exp, fk, pk, scale=-0.5)
                nf = g.st.get("next_func")
                if nf is not None:
                    preload(nf)
                return pst, pk
            pst, pk = PS(g)
            for c in range(KC):
                act(g.xsq[:, c, 0:W], g.x[:, c, 0:W], AF.Square, K + "x%d" % c, K + "xsq")
                mm(pst[:, 0:W], ones_bf[:], g.xsq[:, c, 0:W], c == 0, c == KC - 1, ("ones", K + "xsq"), pk)
            ft, fk = FS(g)
            act(ft[:, 0:W], pst[:, 0:W], AF.Ln, pk, fk, bias=EPS, scale=1.0 / scale_div)
            act(pst[:, 0:W], ft[:, 0:W], AF.Exp, fk, pk, scale=-0.5)
            return pst, pk

        def norm_mod(g, l, g_grp, sh_grp):
            W = g.W
            K = g.pre
            pst, pk = rms_stats(g, float(D))
            for c in range(KC):
                ft, fk = FS(g)
                if g.kind == "p":
                    stt(ft[:, 0:W], g.x[:, c, 0:W], mod_col(l, g_grp, c), pst[:, 0:W], ALU.mult, ALU.mult,
                        (K + "x%d" % c, "modT%d" % l, pk), fk)
                    act(g.h[:, c, 0:W], ft[:, 0:W], AF.Identity, (fk, "modT%d" % l), K + "h%d" % c, bias=mod_col(l, sh_grp, c))
                else:
                    tt(ft[:, 0:W], g.x[:, c, 0:W], pst[:, 0:W], ALU.mult, (K + "x%d" % c, pk), fk)
                    tt(v3(ft[:, 0:W]), v3(ft[:, 0:W]), mod_seq(l, g_grp, c), ALU.mult, (fk, "modT%d" % l), fk)
                    tt(v3(g.h[:, c, 0:W]), v3(ft[:, 0:W]), mod_seq(l, sh_grp, c), ALU.add, (fk, "modT%d" % l), K + "h%d" % c)

        def dense_T(g, wv, cols, nk, rhs_t, rhs_key, wk):
            W = g.W
            pst, pk = PS(g)
            for kc in range(nk):
                rk = rhs_key[kc] if isinstance(rhs_key, list) else rhs_key
                mm(pst[:, 0:W], wv[:, kc, cols], rhs_t[:, kc, 0:W], kc == 0, kc == nk - 1, (wk, rk), pk)
            return pst, pk

        def flush_stat(g):
            j = g.st.get("pending_stat")
            if j is not None:
                W = g.W
                mm(psum[7][:, 0:W], ones_bf[:], g.h[:, j, 0:W], j == 0, j == KC - 1, ("ones", g.pre + "h%d" % j), "ps7")
                g.st["pending_stat"] = None

        def resid_add(g, l, gt_grp, j, pst, pk):
            W = g.W
            xk = g.pre + "x%d" % j
            if g.kind == "p":
                stt(g.x[:, j, 0:W], pst[:, 0:W], mod_col(l, gt_grp, j), g.x[:, j, 0:W], ALU.mult, ALU.add,
                    (pk, "modT%d" % l, xk), xk)
                flush_stat(g)
                act(g.h[:, j, 0:W], g.x[:, j, 0:W], AF.Square, xk, g.pre + "h%d" % j)
                g.st["pending_stat"] = j
                if j == KC - 1:
                    g.st["stats_ready"] = True
            else:
                ft, fk = FS(g)
                tt(v3(ft[:, 0:W]), v3(pst[:, 0:W]), mod_seq(l, gt_grp, j), ALU.mult, (pk, "modT%d" % l), fk)
                tt(g.x[:, j, 0:W], g.x[:, j, 0:W], ft[:, 0:W], ALU.add, (xk, fk), xk)

        def stage_lru(g, l):
            W = g.W
            K = g.pre
            kind = g.kind
            hk_ = [K + "h%d" % c for c in range(KC)]
            for gi, (nm, nblk) in enumerate((("Ag0", 4), ("Ag1", 4), ("Ag2", 2))):
                wt, wk = yield nm
                wv = wt[:, 0:8 * 128 * nblk].rearrange("p (k c) -> p k c", c=128 * nblk)
                for b in range(nblk):
                    n = gi * 4 + b
                    pst, pk = dense_T(g, wv, slice(b * 128, (b + 1) * 128), KC, g.h, hk_, wk)
                    act(g.yag[:, n, 0:W], pst[:, 0:W], AF.Gelu_apprx_tanh, pk, K + "yag%d" % n)
            fence(g.FKEYS + g.LKEYS)
            if kind == "s":
                dma("pool", s0b[:], glas_d[l].rearrange("p (s h v) -> p s h v", h=NH, v=DVH)[:, :, 0, :], (), "s0b", "d_s0b")
            if kind == "p":
                preload(AF.Tanh)
                for hd in range(NH):
                    act(Sbf[:, hd, :], S[:, l, hd, :], AF.Copy, "S%d" % l, "Sbf")
            cwo, _ = SM[("convw", l)]
            cbo, _ = SM[("convb", l)]
            stt_ = {}
            if kind == "p":
                def pmul(o, a, b):
                    P.add("pool", lambda e, o=o, a=a, b=b: e.tensor_tensor(out=o, in0=a, in1=b, op=ALU.mult),
                          reads=("convc", "small", "ccorr"), writes=("ccorr",))

                def padd(o, a, b):
                    P.add("pool", lambda e, o=o, a=a, b=b: e.tensor_tensor(out=o, in0=a, in1=b, op=ALU.add),
                          reads=("ccorr",), writes=("ccorr",))
                wv_ = lambda i: small[:, cwo + i * NB:cwo + (i + 1) * NB]
                cv_ = lambda j: convc[:, l, :, j]
                pmul(ccorr[:, :, 0], wv_(2), cv_(2))
                pmul(ctmp[:, :], wv_(1), cv_(1))
                padd(ccorr[:, :, 0], ccorr[:, :, 0], ctmp[:, :])
                pmul(ctmp[:, :], wv_(0), cv_(0))
                padd(ccorr[:, :, 0], ccorr[:, :, 0], ctmp[:, :])
                pmul(ccorr[:, :, 1], wv_(1), cv_(2))
                pmul(ctmp[:, :], wv_(0), cv_(1))
                padd(ccorr[:, :, 1], ccorr[:, :, 1], ctmp[:, :])
                pmul(ccorr[:, :, 2], wv_(0), cv_(2))

            def S1(n, wt, wk):
                wv = wt[:, 0:1024].rearrange("p (k c) -> p k c", c=128)
                pst, pk = dense_T(g, wv, slice(0, 128), KC, g.h, hk_, wk)
                acc, ak = g.lbuf[n % 3], K + "lb%d" % (n % 3)
                wcol = lambda i: small[:, cwo + i * NB + n:cwo + i * NB + n + 1]
                cb = small[:, cbo + n:cbo + n + 1]
                if kind == "p":
                    w3, w2, w1, w0 = wcol(3), wcol(2), wcol(1), wcol(0)
                    ops = [lambda: ts(acc[:, 0:W], pst[:, 0:W], w3, cb, ALU.mult, ALU.add, (pk, "small"), ak)]
                    for i, wi in ((2, w2), (1, w1), (0, w0)):
                        sh = 3 - i
                        ops.append(lambda sh=sh, wi=wi: stt(acc[:, sh:W], pst[:, 0:W - sh], wi, acc[:, sh:W], ALU.mult, ALU.add,
                                                            (pk, "small", ak), ak))
                    ops.append(lambda: tt(acc[:, 0:3], acc[:, 0:3], ccorr[:, n, :], ALU.add, (ak, "ccorr"), ak))
                    ops.append(lambda: vcopy(convc[:, l, n, :], pst[:, W - 3:W], pk, "convc"))
                    xcb, xk = BS(g)
                    ops.append(lambda: P.add("pool", lambda e, o=xcb[:, 0:W], i=acc[:, 0:W]: e.tensor_copy(out=o, in_=i),
                                             reads=_flat(ak), writes=_flat(xk)))
                    stt_[n] = {"acc": (acc, ak), "xcb": (xcb, xk)}
                    return ops
                else:
                    a3 = v3(acc[:, 0:W])
                    w3, w2, w1, w0 = wcol(3), wcol(2), wcol(1), wcol(0)
                    xcb, xk = BS(g)
                    ops = [lambda: act(xas[:, :, 0:3], convst[:, l, n, :, :], AF.Copy, "convst", "xas"),
                           lambda: act(xas[:, :, 3:7], v3(pst[:, 0:W]), AF.Copy, pk, "xas"),
                           lambda: act(a3, xas[:, :, 3:7], AF.Identity, ("xas", "small"), ak, bias=cb, scale=w3)]
                    for i, wi in ((2, w2), (1, w1), (0, w0)):
                        ops.append(lambda i=i, wi=wi: stt(a3, xas[:, :, i:i + 4], wi, a3, ALU.mult, ALU.add, ("xas", "small", ak), ak))
                    ops.append(lambda: act(convst[:, l, n, :, :], xas[:, :, 4:7], AF.Copy, "xas", "convst"))
                    ops.append(lambda: act(xcb[:, 0:W], acc[:, 0:W], AF.Copy, ak, xk))
                    stt_[n] = {"acc": (acc, ak), "xcb": (xcb, xk)}
                    return ops

            def S2(n, wt, wk):
                o = 1024 if n < NB - 1 else 128
                xcb, xk = stt_[n]["xcb"]
                pr, prk = PS(g)
                mm(pr[:, 0:W], wt[:, o:o + 128], xcb[:, 0:W], True, True, (wk, xk), prk)
                pi, pik = PS(g)
                mm(pi[:, 0:W], wt[:, o + 128:o + 256], xcb[:, 0:W], True, True, (wk, xk), pik)
                if kind == "p":
                    tr, trk, ti_, tik = pr, prk, pi, pik
                else:
                    tr, trk = FS(g)
                    ti_, tik = FS(g)
                act(tr[:, 0:W], pr[:, 0:W], AF.Tanh, (prk, "hbias"), trk, bias=hbias[:, l, 0, n:n + 1], scale=0.5)
                act(ti_[:, 0:W], pi[:, 0:W], AF.Tanh, (pik, "hbias"), tik, bias=hbias[:, l, 1, n:n + 1], scale=0.5)
                pr, prk, pi, pik = tr, trk, ti_, tik
                at, atk = g.lbuf[5 + n % 2], K + "lb%d" % (5 + n % 2)
                act(at[:, 0:W], pr[:, 0:W], AF.Exp, (prk, "hslru"), atk, scale=hslru[:, l, n:n + 1], bias=hslru[:, l, n:n + 1])
                act(pr[:, 0:W], pr[:, 0:W], AF.Exp, (prk, "slru"), prk, scale=slru[:, l, n:n + 1], bias=slru[:, l, n:n + 1])
                stt_[n].update({"pr": (pr, prk), "pi": (pi, pik), "at": (at, atk)})

            def S2b(n):
                pr, prk = stt_[n]["pr"]
                act(pr[:, 0:W], pr[:, 0:W], AF.Sqrt, prk, prk, scale=-0.25, bias=0.25)

            def S3(n):
                acc, ak = stt_[n]["acc"]
                pr, prk = stt_[n]["pr"]
                pi, pik = stt_[n]["pi"]
                at, atk = stt_[n]["at"]
                it, ik = g.lbuf[3 + n % 2], K + "lb%d" % (3 + n % 2)
                ht, hk = g.lbuf[7 + n % 2], K + "lb%d" % (7 + n % 2)
                if kind == "p":
                    yk = K + "yag%d" % n
                    return [
                        lambda: stt(it[:, 0:W], pi[:, 0:W], 1.0, acc[:, 0:W], ALU.add, ALU.mult, (pik, ak), ik),
                        lambda: tt(it[:, 0:W], it[:, 0:W], pr[:, 0:W], ALU.mult, (ik, prk), ik),
                        lambda: scan(ht[:, 0:W], at[:, 0:W], it[:, 0:W], lruc[:, l, n:n + 1], (atk, ik, "lruc"), hk),
                        lambda: vcopy(lruc[:, l, n:n + 1], ht[:, W - 1:W], hk, "lruc"),
                        lambda: P.add("pool", lambda e, o=g.yag[:, n, 0:W], a=ht[:, 0:W], b=g.yag[:, n, 0:W]:
                                      e.tensor_tensor(out=o, in0=a, in1=b, op=ALU.mult),
                                      reads=_flat(hk, yk), writes=_flat(yk)),
                    ]
                a3 = v3(at[:, 0:W])
                b3 = v3(it[:, 0:W])
                tmp, tk = FS(g)
                yk = K + "yag%d" % n
                return [
                    lambda: stt(it[:, 0:W], pi[:, 0:W], 1.0, acc[:, 0:W], ALU.add, ALU.mult, (pik, ak), ik),
                    lambda: tt(it[:, 0:W], it[:, 0:W], pr[:, 0:W], ALU.mult, (ik, prk), ik),
                    lambda: tt(tmp[:, 0:NSEQ], a3[:, :, 0], lrust[:, l, n, :], ALU.mult, (atk, "lrust"), tk),
                    lambda: tt(b3[:, :, 0], b3[:, :, 0], tmp[:, 0:NSEQ], ALU.add, (ik, tk), ik),
                    lambda: vmemset(a3[:, :, 0], 0.0, atk),
                    lambda: scan(ht[:, 0:W], at[:, 0:W], it[:, 0:W], 0.0, (atk, ik), hk),
                    lambda: act(lrust[:, l, n, :], v3(ht[:, 0:W])[:, :, 3], AF.Copy, hk, "lrust"),
                    lambda: tt(g.yag[:, n, 0:W], ht[:, 0:W], g.yag[:, n, 0:W], ALU.mult, (hk, yk), yk),
                ]

            glr = None
            wt = wk = None
            for k in range(NB + 2):
                if k < NB:
                    wt, wk = yield "Ax%d" % k
                elif k == NB:
                    wt, wk = yield "Glr"
                    glr = (wt, wk)
                    wvl = wt[:, 0:128].rearrange("p (k c) -> p k c", c=16)
                    psl, pslk = PS(g)
                    for kc in range(KC):
                        mm(psl[0:16, 0:W], wvl[:, kc, :], g.h[:, kc, 0:W], kc == 0, kc == KC - 1, (wk, hk_), pslk)
                    act(g.ulr[0:16, 0:W], psl[0:16, 0:W], AF.Copy, pslk, K + "ulr")
                a_ops = S3(k - 2) if 0 <= k - 2 < NB else None
                b_ops = S1(k, wt, wk) if k < NB else None
                a_ops, b_ops = list(a_ops or []), list(b_ops or [])
                while a_ops or b_ops:
                    if a_ops:
                        a_ops.pop(0)()
                    if b_ops:
                        b_ops.pop(0)()
                if 0 <= k - 1 < NB:
                    S2(k - 1, wt, wk)
                    yield ("sync", k)
                    S2b(k - 1)
                if g.st.get("ada1"):
                    for (la, u) in ADA_AFTER.get((l, k), []):
                        wt2, wk2 = yield "ADA_%d_%d" % (la, u)
                        if kind == "p":
                            ada_unit(la, u, wt2, wk2, psum[7], ("ps7",))
                            ada_after(la, u)
            return glr

        def stage_gla(g, l, glr, last_tile):
            W = g.W
            K = g.pre
            kind = g.kind
            CH, nch = g.CH, g.nch
            hk_ = [K + "h%d" % c for c in range(KC)]
            if kind == "p":
                c_tri = consts[0:CH, C_TRI:C_TRI + 128]
                c_u = consts[0:CH, C_U:C_U + 128]
                c_mask = consts[0:CH, C_MASK:C_MASK + 128]
            else:
                c_tri = consts[0:CH, C_TRIS:C_TRIS + 64]
                c_u = consts[0:CH, C_US:C_US + 64]
                c_mask = consts[0:CH, C_MASKS:C_MASKS + 64]
            gT, khat, v_tm, qt, kt, at_bf, dec, ulr = g.gT, g.khat, g.v_tm, g.qt, g.kt, g.at_bf, g.dec, g.ulr
            fence(g.LKEYS + g.GKEYS)
            if kind == "p":
                preload(AF.Ln)
            Sk, Sbk = "S%d" % l, "Sbf"
            for c in range(nch):
                pst, pk = PSW(g)
                mm(pst[0:CH, 0:512], ulr[0:17, c * CH:(c + 1) * CH], wa2e[0:17, l * 512:(l + 1) * 512], True, True,
                   (K + "ulr", "wa2e"), pk)
                ft, fk = (FS(g) if kind == "p" else (None, None))
                if kind == "p":
                    act(ft[0:CH, :], pst[0:CH, :], AF.Exp, pk, fk, scale=-1.0)
                    act(gT[0:CH, c, :], ft[0:CH, :], AF.Ln, fk, K + "gT", bias=1.0)
                else:
                    act(pst[0:CH, :], pst[0:CH, :], AF.Exp, pk, pk, scale=-1.0)
                    act(gT[0:CH, c, :], pst[0:CH, :], AF.Ln, pk, K + "gT", bias=1.0)
            for half in range(2):
                wt, wk = yield "Gv%d" % half
                wv = wt[:, 0:4096].rearrange("p (k c) -> p k c", c=512)
                for c in range(nch):
                    pst, pk = PSW(g)
                    for kc in range(KC):
                        mm(pst[0:CH, 0:512], g.h[:, kc, c * CH:(c + 1) * CH], wv[:, kc, :], kc == 0, kc == KC - 1,
                           (wk, hk_), pk)
                    act(v_tm[0:CH, c, half * 512:(half + 1) * 512], pst[0:CH, :], AF.Copy, pk, K + "vtm")
            wt, wk = yield "Gk"
            wv = wt[:, 0:4096].rearrange("p (k c) -> p k c", c=512)
            for c in range(nch):
                pkd, pkdk = PSW(g)
                mm(pkd[0:CH, 0:512], c_u, gT[0:CH, c, :], True, True, ("consts", K + "gT"), pkdk)
                kd, kdk = WIDE(g)
                act(kd[0:CH, :], pkd[0:CH, :], AF.Exp, pkdk, kdk)
                pst, pk = PSW(g)
                for kc in range(KC):
                    mm(pst[0:CH, 0:512], g.h[:, kc, c * CH:(c + 1) * CH], wv[:, kc, :], kc == 0, kc == KC - 1,
                       (wk, hk_), pk)
                tt(khat[0:CH, c, :], pst[0:CH, :], kd[0:CH, :], ALU.mult, (pk, kdk), K + "khat")
            eplus = []
            for hd in range(NH):
                pb, pbk = PS(g)
                for c in range(nch):
                    mm(pb[:, c * CH:(c + 1) * CH], gT[0:CH, c, hd * 128:(hd + 1) * 128], c_tri, True, True,
                       (K + "gT", "consts"), pbk)
                ep, epk = FS(g)
                act(ep[:, 0:W], pb[:, 0:W], AF.Exp, pbk, epk)
                em, emk = FS(g)
                act(em[:, 0:W], pb[:, 0:W], AF.Exp, pbk, emk, scale=-1.0)
                if kind == "p":
                    vcopy(dec[:, hd, 0:nch], ep[:, 0:W].rearrange("p (c t) -> p c t", t=CH)[:, :, CH - 1], epk, K + "dec")
                else:
                    vcopy(dec[:, hd, 0:NSEQ], v3(ep[:, 0:W])[:, :, TS - 1], epk, K + "dec")
                eplus.append((ep, epk))
                pst, pk = dense_T(g, wv, slice(hd * 128, (hd + 1) * 128), KC, g.h, hk_, wk)
                tt(kt[:, hd, 0:W], pst[:, 0:W], em[:, 0:W], ALU.mult, (pk, emk), K + "kt")
            wt, wk = yield "Gq"
            wv = wt[:, 0:4096].rearrange("p (k c) -> p k c", c=512)
            for hd in range(NH):
                ep, epk = eplus[hd]
                pst, pk = dense_T(g, wv, slice(hd * 128, (hd + 1) * 128), KC, g.h, hk_, wk)
                stt(qt[:, hd, 0:W], pst[:, 0:W], float(DK) ** -0.5, ep[:, 0:W], ALU.mult, ALU.mult, (pk, epk), K + "qt")
            gno, _ = SM[("gng", l)]
            if kind == "p":
                hs_ = {}

                def Xu_mm(hd, wt, wk):
                    wv = wt[:, 0:2048].rearrange("p (k c) -> p k c", c=256)
                    hs_[hd] = {"ur": [dense_T(g, wv, slice(vc * 128, (vc + 1) * 128), KC, g.h, hk_, wk) for vc in range(2)]}

                def Xu_act(hd):
                    srs = []
                    for vc in range(2):
                        pst, pk = hs_[hd]["ur"][vc]
                        sr, srk = FS(g)
                        act(sr[:, 0:W], pst[:, 0:W], AF.Silu, pk, srk)
                        srs.append((sr, srk))
                    hs_[hd]["sr"] = srs

                def Xa(hd):
                    pa, pak = PS(g)
                    for c in range(nch):
                        mm(pa[0:CH, c * CH:(c + 1) * CH], kt[:, hd, c * CH:(c + 1) * CH], qt[:, hd, c * CH:(c + 1) * CH],
                           True, True, (K + "kt", K + "qt"), pak)
                    dS = [PS(g), PS(g)]
                    for c in range(nch):
                        dst, dsk = dS[c // 2]
                        mm(dst[:, (c % 2) * 256:(c % 2 + 1) * 256], khat[0:CH, c, hd * 128:(hd + 1) * 128],
                           v_tm[0:CH, c, hd * 256:(hd + 1) * 256], True, True, (K + "khat", K + "vtm"), dsk)
                    tt(at_bf[0:CH, 0:nch, 0:CH], pa[0:CH, 0:W].rearrange("p (c t) -> p c t", t=CH),
                       c_mask.unsqueeze(1).to_broadcast([CH, nch, CH]), ALU.mult, (pak, "consts"), K + "at_bf")
                    bufs = [(S[:, l, hd, :], Sk), (Sx[:, :], "Sx")]
                    for c in range(nch):
                        dst, dsk = dS[c // 2]
                        src, srck = bufs[c % 2]
                        out, outk = bufs[(c + 1) % 2]
                        stt(out, src, dec[:, hd, c:c + 1], dst[:, (c % 2) * 256:(c % 2 + 1) * 256],
                            ALU.mult, ALU.add, (srck, K + "dec", dsk), outk)
                        if c < nch - 1:
                            act(Sbfc[:, c, :], out, AF.Copy, outk, "Sbfc%d" % c)

                def X4(hd):
                    po = [PO(g, 0), PO(g, 1)]
                    for c in range(nch):
                        for vc in range(2):
                            pot, pok = po[vc]
                            mm(pot[:, c * CH:(c + 1) * CH], v_tm[0:CH, c, hd * 256 + vc * 128:hd * 256 + (vc + 1) * 128],
                               at_bf[0:CH, c, 0:CH], True, False, (K + "vtm", K + "at_bf"), pok)
                            if c == 0:
                                mm(pot[:, 0:CH], Sbf[:, hd, vc * 128:(vc + 1) * 128], qt[:, hd, 0:CH], False, True,
                                   (Sbk, K + "qt"), pok)
                            else:
                                mm(pot[:, c * CH:(c + 1) * CH], Sbfc[:, c - 1, vc * 128:(vc + 1) * 128],
                                   qt[:, hd, c * CH:(c + 1) * CH], False, True, ("Sbfc%d" % (c - 1), K + "qt"), pok)
                    hs_[hd]["po"] = po

                def YA_evac(hd):
                    po = hs_[hd]["po"]
                    pn, pnk = PS(g)
                    for vc in range(2):
                        cidx = hd * 2 + vc
                        pot, pok = po[vc]
                        sr, srk = hs_[hd]["sr"][vc]
                        o2, o2k = BS(g)
                        act(o2[:, 0:W], pot[:, 0:W], AF.Square, pok, o2k)
                        mm(pn[:, 0:W], ones_bf[:], o2[:, 0:W], vc == 0, vc == 1, ("ones", o2k), pnk)
                        stt(sr[:, 0:W], pot[:, 0:W], small[:, gno + cidx:gno + cidx + 1], sr[:, 0:W], ALU.mult, ALU.mult,
                            (pok, "small", srk, o2k), srk)
                    hs_[hd]["pn"] = (pn, pnk)

                def YA_norm(hd):
                    pn, pnk = hs_[hd]["pn"]
                    ft, fk = FS(g)
                    act(ft[:, 0:W], pn[:, 0:W], AF.Ln, pnk, fk, bias=EPS, scale=1.0 / DVH)
                    act(ft[:, 0:W], ft[:, 0:W], AF.Exp, fk, fk, scale=-0.5)
                    hs_[hd]["rs"] = (ft, fk)
                    if hd == NH - 1:
                        preload(AF.Sigmoid)

                def YB(hd):
                    rs, rsk = hs_[hd]["rs"]
                    for vc in range(2):
                        cidx = hd * 2 + vc
                        sr, srk = hs_[hd]["sr"][vc]
                        tt(g.yb[:, cidx, 0:W], sr[:, 0:W], rs[:, 0:W], ALU.mult, (srk, rsk), K + "yb")

                wt, wk = yield "Gr0"
                Xu_mm(0, wt, wk)
                Xu_act(0)
                Xa(0)
                for hd in range(NH):
                    if hd + 1 < NH:
                        wt, wk = yield "Gr%d" % (hd + 1)
                    X4(hd)
                    yield ("h", hd, 0)
                    if hd + 1 < NH:
                        Xu_mm(hd + 1, wt, wk)
                    yield ("h", hd, 1)
                    YA_evac(hd)
                    if hd + 1 < NH:
                        Xu_act(hd + 1)
                    YA_norm(hd)
                    yield ("h", hd, 2)
                    if hd + 1 < NH:
                        Xa(hd + 1)
                    yield ("h", hd, 3)
                    YB(hd)
            else:
                gsrc = glas_d[l].rearrange("p (s h v) -> p s h v", h=NH, v=DVH)
                gdst = ngs_d[l].rearrange("p (s h v) -> p s h v", h=NH, v=DVH)
                fence(("p:xsq", "sst2", "sst3"))
                NG = NSEQ // GS

                def sst_load(gi):
                    hd_, grp_ = divmod(gi, NG)
                    bi = gi % 4
                    dma("sp", sst[bi], gsrc[:, grp_ * GS:(grp_ + 1) * GS, hd_, :], (), sstk[bi], "d_" + sstk[bi])

                sst_load(0)
                sst_load(1)
                hs_ = {}

                def Xu_s(hd, wt, wk):
                    wv = wt[:, 0:2048].rearrange("p (k c) -> p k c", c=256)
                    srs = []
                    for vc in range(2):
                        pst, pk = dense_T(g, wv, slice(vc * 128, (vc + 1) * 128), KC, g.h, hk_, wk)
                        sr, srk = FS(g)
                        act(sr[:, 0:W], pst[:, 0:W], AF.Silu, pk, srk)
                        srs.append((sr, srk))
                    hs_[hd] = {"sr": srs}

                def seg0(hd):
                    pa, pak = PS(g)
                    mm(pa[0:CH, 0:CH], kt[:, hd, 0:CH], qt[:, hd, 0:CH], True, True, (K + "kt", K + "qt"), pak)
                    tt(at_bf[0:CH, 0:1, 0:CH], pa[0:CH, 0:W].rearrange("p (c t) -> p c t", t=CH),
                       c_mask.unsqueeze(1).to_broadcast([CH, 1, CH]), ALU.mult, (pak, "consts"), K + "at_bf")

                def seg1(hd):
                    g.st["restrict"] = True
                    for vc in range(2):
                        pot, pok = PO(g, vc)
                        mm(pot[:, 0:W], v_tm[0:CH, 0, hd * 256 + vc * 128:hd * 256 + (vc + 1) * 128],
                           at_bf[0:CH, 0, 0:CH], True, False, (K + "vtm", K + "at_bf"), pok)
                        for s_ in range(NSEQ):
                            mm(pot[:, s_ * TS:(s_ + 1) * TS], s0b[:, s_, vc * 128:(vc + 1) * 128],
                               qt[:, hd, s_ * TS:(s_ + 1) * TS], False, s_ == NSEQ - 1, ("s0b", K + "qt"), pok)
                    if hd + 1 < NH:
                        dma("pool", s0b[:], gsrc[:, :, hd + 1, :], (), "s0b", "d_s0b")

                def seg2(hd):
                    pn, pnk = PS(g)
                    for vc in range(2):
                        cidx = hd * 2 + vc
                        pot, pok = PO(g, vc)
                        sr, srk = hs_[hd]["sr"][vc]
                        o2, o2k = BS(g)
                        act(o2[:, 0:W], pot[:, 0:W], AF.Square, (pok, "s:poevac"), o2k)
                        mm(pn[:, 0:W], ones_bf[:], o2[:, 0:W], vc == 0, vc == 1, ("ones", o2k), pnk)
                        stt(sr[:, 0:W], pot[:, 0:W], small[:, gno + cidx:gno + cidx + 1], sr[:, 0:W], ALU.mult, ALU.mult,
                            (pok, "small", srk, o2k), (srk, "s:poevac"))
                    ft, fk = FS(g)
                    act(ft[:, 0:W], pn[:, 0:W], AF.Ln, pnk, fk, bias=EPS, scale=1.0 / DVH)
                    act(ft[:, 0:W], ft[:, 0:W], AF.Exp, fk, fk, scale=-0.5)
                    hs_[hd]["rs"] = (ft, fk)
                    g.st["restrict"] = False

                def seg3(hd):
                    for grp in range(NG):
                        gi = hd * NG + grp
                        if gi + 2 < NH * NG:
                            sst_load(gi + 2)
                        bi = gi % 4
                        bt, bk = sst[bi], sstk[bi]
                        if (grp * GS) % HS == 0:
                            hf = (grp * GS) // HS
                            for s8 in range(HS):
                                act(khm[:, s8, :], khat[0:CH, 0, hd * 128:(hd + 1) * 128], AF.Identity, (K + "khat", "consts"), "khm",
                                    scale=consts[0:CH, C_SEQM + hf * HS + s8:C_SEQM + hf * HS + s8 + 1])
                        for pair in range(GS // 2):
                            pss, psk = PS(g)
                            for q in range(2):
                                sg = pair * 2 + q
                                s_ = grp * GS + sg
                                mm(pss[:, q * 256:(q + 1) * 256], khm[:, s_ % HS, :], v_tm[0:CH, 0, hd * 256:(hd + 1) * 256],
                                   True, True, ("khm", K + "vtm"), psk)
                            for q in range(2):
                                sg = pair * 2 + q
                                s_ = grp * GS + sg
                                stt(bt[:, sg, :], bt[:, sg, :], dec[:, hd, s_:s_ + 1], pss[:, q * 256:(q + 1) * 256],
                                    ALU.mult, ALU.add, (bk, K + "dec", psk), bk)
                        dma("sp", gdst[:, grp * GS:(grp + 1) * GS, hd, :], bt, bk, (), "d_" + bk)
                    if hd == NH - 1:
                        fence(("sst2", "sst3", "p:xsq"))

                def seg4(hd):
                    rs, rsk = hs_[hd]["rs"]
                    for vc in range(2):
                        cidx = hd * 2 + vc
                        sr, srk = hs_[hd]["sr"][vc]
                        tt(g.yb[:, cidx, 0:W], sr[:, 0:W], rs[:, 0:W], ALU.mult, (srk, rsk), K + "yb")

                wt, wk = yield "Gr0"
                Xu_s(0, wt, wk)
                for hd in range(NH):
                    if hd + 1 < NH:
                        wt, wk = yield "Gr%d" % (hd + 1)
                        Xu_s(hd + 1, wt, wk)
                    seg0(hd)
                    yield ("h", hd, 0)
                    seg1(hd)
                    yield ("h", hd, 1)
                    seg2(hd)
                    yield ("h", hd, 2)
                    seg3(hd)
                    yield ("h", hd, 3)
                    seg4(hd)
            if kind == "p" and last_tile:
                dma("sp", ngp_d[l], S[:, l, :, :].rearrange("p h v -> p (h v)"), Sk, (), "d_S")

        def stage_merge(g, l):
            W = g.W
            K = g.pre
            hk_ = [K + "h%d" % c for c in range(KC)]
            yak = tuple(K + "yag%d" % n for n in range(NB))
            for j in range(KC):
                wt, wk = yield "Cg%d" % j
                wv = wt[:, 0:2048].rearrange("p (k c) -> p k c", c=256)
                pa, pak = dense_T(g, wv, slice(0, 128), KC, g.h, hk_, wk)
                sa, sak = FS(g)
                act(sa[:, 0:W], pa[:, 0:W], AF.Sigmoid, pak, sak)
                pb, pbk = dense_T(g, wv, slice(128, 256), KC, g.h, hk_, wk)
                sbt, sbk = FS(g)
                act(sbt[:, 0:W], pb[:, 0:W], AF.Sigmoid, pbk, sbk)
                if j == KC - 1 and g.kind == "p":
                    preload(AF.Ln)
                wt, wk = yield "Cp%d" % j
                wpa = wt[:, 0:1280].rearrange("p (k c) -> p k c", c=128)
                wpb = wt[:, 1280:2304].rearrange("p (k c) -> p k c", c=128)
                ppa, ppak = dense_T(g, wpa, slice(0, 128), NB, g.yag, yak, wk)
                tt(sa[:, 0:W], sa[:, 0:W], ppa[:, 0:W], ALU.mult, (sak, ppak), sak)
                ppb, ppbk = dense_T(g, wpb, slice(0, 128), KC, g.yb, K + "yb", wk)
                tt(sbt[:, 0:W], sbt[:, 0:W], ppb[:, 0:W], ALU.mult, (sbk, ppbk), sbk)
                tt(g.m[:, j, 0:W], sa[:, 0:W], sbt[:, 0:W], ALU.add, (sak, sbk), K + "xsq")
            for jj in range(2):
                wt, wk = yield "Wo%d" % jj
                wv = wt[:, 0:4096].rearrange("p (k c) -> p k c", c=512)
                for j4 in range(4):
                    j = jj * 4 + j4
                    pst, pk = dense_T(g, wv, slice(j4 * 128, (j4 + 1) * 128), KC, g.m, K + "xsq", wk)
                    resid_add(g, l, 2, j, pst, pk)

        def stage_ffn(g, l):
            W = g.W
            K = g.pre
            hk_ = [K + "h%d" % c for c in range(KC)]
            fence(g.GKEYS + g.FKEYS)
            for i in range(FC):
                wt, wk = yield "F1_%d" % i
                wv = wt[:, 0:2048].rearrange("p (k c) -> p k c", c=256)
                p1, p1k = dense_T(g, wv, slice(0, 128), KC, g.h, hk_, wk)
                s1, s1k = FS(g)
                act(s1[:, 0:W], p1[:, 0:W], AF.Silu, p1k, s1k)
                p2, p2k = dense_T(g, wv, slice(128, 256), KC, g.h, hk_, wk)
                tt(g.ff[:, i, 0:W], s1[:, 0:W], p2[:, 0:W], ALU.mult, (s1k, p2k), K + "ff")
                if i == FC - 1 and g.kind == "p":
                    preload(AF.Ln)
            for j in range(KC):
                wt, wk = yield "F2_%d" % j
                wv = wt[:, 0:2816].rearrange("p (k c) -> p k c", c=128)
                pst, pk = dense_T(g, wv, slice(0, 128), FC, g.ff, K + "ff", wk)
                resid_add(g, l, 5, j, pst, pk)

        def final_norm(g, col0, next_col0=None):
            W = g.W
            K = g.pre
            pst, pk = rms_stats(g, float(D))
            fo, _ = SM["fg"]
            for c in range(KC):
                ft, fk = FS(g)
                stt(ft[:, 0:W], g.x[:, c, 0:W], small[:, fo + c:fo + c + 1], pst[:, 0:W], ALU.mult, ALU.mult,
                    (K + "x%d" % c, "small", pk), fk)
                dma("sp", yT_d[c * 128:(c + 1) * 128, col0:col0 + W], ft[:, 0:W], fk, (), "d_" + g.kind + fk.split(":")[1])
                if next_col0 is not None:
                    dma("sp", g.x[:, c, 0:W], xT_d[c * 128:(c + 1) * 128, next_col0:next_col0 + W], (), K + "x%d" % c,
                        "d_" + g.kind + "x%d" % c)
            if next_col0 is not None:
                g.st["x_preloaded"] = True

        def tile_prog(g, col0, last_tile, next_col0=None):
            W = g.W
            xkeys = tuple(g.pre + "x%d" % c for c in range(KC))
            g.st["stats_ready"] = False
            if g.st.get("x_preloaded"):
                g.st["x_preloaded"] = False
            else:
                for c in range(KC):
                    dma("sp", g.x[:, c, 0:W], xT_d[c * 128:(c + 1) * 128, col0:col0 + W], (), xkeys[c],
                        "d_" + g.kind + "x%d" % c)
            for l in range(DEPTH):
                g.st["next_func"] = AF.Gelu_apprx_tanh
                norm_mod(g, l, 1, 0)
                glr = yield from stage_lru(g, l)
                yield from stage_gla(g, l, glr, last_tile)
                yield from stage_merge(g, l)
                g.st["next_func"] = AF.Silu
                norm_mod(g, l, 4, 3)
                yield from stage_ffn(g, l)
                yield ("layer_end", l)
            g.st["next_func"] = None
            final_norm(g, col0, next_col0)

        def run_lockstep(ti, gens):
            reqs = [next(gen, None) for gen in gens]
            cur_l = 0
            while any(r is not None for r in reqs):
                assert all(r == reqs[0] for r in reqs), reqs
                name = reqs[0]
                if isinstance(name, tuple):
                    if name[0] == "layer_end":
                        cur_l = name[1] + 1
                    slot = None
                else:
                    slot = acquire(unit_index[(ti, cur_l, name)])
                reqs = _send_all(gens, slot)

        def _send_all(gens, slot):
            out = []
            for gen in gens:
                try:
                    out.append(gen.send(slot))
                except StopIteration:
                    out.append(None)
            return out

        for ti in range(NT):
            last_tile = ti == NT - 1
            gp.st["ada1"] = gs.st["ada1"] = (ti == 0)
            gens = [tile_prog(gp, ti * TW, last_tile, None if last_tile else (ti + 1) * TW)]
            if ti == 0 and not SEPARATE:
                gens.append(tile_prog(gs, SEQ, False))
            run_lockstep(ti, gens)
            if last_tile:
                dma("sp", ncp_d[:, :], convc[:].rearrange("p l n t -> p (l n t)"), "convc", (), "d_convc")
                dma("sp", nlp_d[:, :], lruc[:].rearrange("p l n -> p (l n)"), "lruc", (), "d_lruc")
        if SEPARATE:
            run_lockstep(NT, [tile_prog(gs, SEQ, False)])
        dma("sp", ncs_d[:, :], convst[:].rearrange("p l n s t -> p (l n s t)"), "convst", (), "d_convst")
        dma("sp", nls_d[:, :], lrust[:].rearrange("p l n s -> p (l n s)"), "lrust", (), "d_lrust")

        P.finalize()
        block = es.enter_context(nc.Block())

        @block.tensor
        def _(e):
            P.emit("pe", e, sems)

        @block.scalar
        def _(e):
            P.emit("act", e, sems)

        @block.vector
        def _(e):
            P.emit("dve", e, sems)

        @block.gpsimd
        def _(e):
            P.emit("pool", e, sems)

        @block.sync
        def _(e):
            waited = P.emit("sp", e, sems)
            for sk, v in P.final_counts.items():
                if sk.startswith("d_") and waited.get(sk, 0) < v:
                    e.wait_ge(sems[sk], v)
    return nc


_CACHE = {}


def kernel(x_prompt, x_sample, c_prompt, c_sample, state_conv, state_lru, state_gla,
           norm1_g, norm2_g, ada_w, ada_b, w_in, conv_w, conv_b, lru_wa, lru_ba, lru_wx, lru_bx,
           lru_lambda, gla_wa2, gla_ba, gla_norm_g, proj_a, proj_b, w_out, ffn_w1, ffn_w2, final_g):
    f = lambda a: np.asarray(a, dtype=np.float32)
    x_prompt, x_sample, c_prompt, c_sample = f(x_prompt), f(x_sample), f(c_prompt), f(c_sample)
    state_conv, state_lru, state_gla = f(state_conv), f(state_lru), f(state_gla)
    norm1_g, norm2_g, ada_w, ada_b, w_in = f(norm1_g), f(norm2_g), f(ada_w), f(ada_b), f(w_in)
    conv_w, conv_b, lru_wa, lru_ba, lru_wx, lru_bx = f(conv_w), f(conv_b), f(lru_wa), f(lru_ba), f(lru_wx), f(lru_bx)
    lru_lambda, gla_wa2, gla_ba, gla_norm_g = f(lru_lambda), f(gla_wa2), f(gla_ba), f(gla_norm_g)
    proj_a, proj_b, w_out, ffn_w1, ffn_w2, final_g = f(proj_a), f(proj_b), f(w_out), f(ffn_w1), f(ffn_w2), f(final_g)

    wpack = np.stack([_pack_layer(w_in[l], lru_wa[l], lru_wx[l], proj_a[l], proj_b[l], w_out[l], ffn_w1[l], ffn_w2[l])
                      for l in range(DEPTH)])
    adaw = np.stack([np.concatenate([_kc_pack(ada_w[l][:, u * 512:(u + 1) * 512]) for u in range(12)], axis=1)
                     for l in range(DEPTH)])
    small = np.zeros((128, SM_N), np.float32)
    for l in range(DEPTH):
        def put(name, arr):
            o, s = SM[(name, l)]
            small[:, o:o + s] = arr
        put("n1g", _fm(norm1_g[l], 8))
        put("n2g", _fm(norm2_g[l], 8))
        put("adab", _fm(ada_b[l], 48))
        put("convw", np.concatenate([_fm(conv_w[l, i], NB) for i in range(4)], axis=1))
        put("convb", _fm(conv_b[l], NB))
        put("ba", _fm(lru_ba[l], NB))
        put("bx", _fm(lru_bx[l], NB))
        put("lam", _fm(lru_lambda[l], NB))
        put("gng", _fm(gla_norm_g[l], 8))
    o, s = SM["fg"]
    small[:, o:o + s] = _fm(final_g, 8)
    wa2e = np.zeros((17, DEPTH * 512), np.float32)
    for l in range(DEPTH):
        wa2e[0:16, l * 512:(l + 1) * 512] = gla_wa2[l]
        wa2e[16, l * 512:(l + 1) * 512] = gla_ba[l]
    consts = _consts()

    in_maps = []
    for i in range(NCORES):
        sl = slice(i * NSEQ, (i + 1) * NSEQ)
        xT = np.empty((D, NTOK), np.float32)
        xT[:, :SEQ] = x_prompt[i].T
        xT[:, SEQ:] = x_sample[sl].reshape(WS, D).T
        cT = np.empty((D, 17), np.float32)
        cT[:, 0] = c_prompt[i]
        cT[:, 1:] = c_sample[sl].T
        convs = state_conv[:, sl].reshape(DEPTH, NSEQ, 3, NB, 128).transpose(4, 0, 3, 1, 2).reshape(128, -1)
        lrus = state_lru[:, sl].reshape(DEPTH, NSEQ, NB, 128).transpose(3, 0, 2, 1).reshape(128, -1)
        glas = state_gla[:, sl].transpose(0, 3, 1, 2, 4).reshape(DEPTH, 128, -1)
        in_maps.append({
            "xT": xT, "cT": cT, "convs": np.ascontiguousarray(convs), "lrus": np.ascontiguousarray(lrus),
            "glas": np.ascontiguousarray(glas), "small": small, "wa2e": wa2e, "consts": consts,
            "adaw": adaw, "wpack": wpack,
        })

    if "nc" not in _CACHE:
        _CACHE["nc"] = build_program()
    nc = _CACHE["nc"]
    res = run_bass_kernel_spmd(nc, in_maps, core_ids=list(range(NCORES)))
    R = res.results

    y_prompt = np.empty((NCORES, SEQ, D), np.float32)
    y_sample = np.empty((NCORES * NSEQ, TS, D), np.float32)
    ncp = np.empty((DEPTH, NCORES, 3, DRNN), np.float32)
    nlp = np.empty((DEPTH, NCORES, DRNN), np.float32)
    ngp = np.empty((DEPTH, NCORES, NH, DK, DVH), np.float32)
    ncs = np.empty((DEPTH, NCORES * NSEQ, 3, DRNN), np.float32)
    nls = np.empty((DEPTH, NCORES * NSEQ, DRNN), np.float32)
    ngs = np.empty((DEPTH, NCORES * NSEQ, NH, DK, DVH), np.float32)
    for i in range(NCORES):
        r = R[i]
        sl = slice(i * NSEQ, (i + 1) * NSEQ)
        yT = np.asarray(r["yT"])
        y_prompt[i] = yT[:, :SEQ].T
        y_sample[sl] = yT[:, SEQ:].T.reshape(NSEQ, TS, D)
        ncp[:, i] = np.asarray(r["ncp"]).reshape(128, DEPTH, NB, 3).transpose(1, 3, 2, 0).reshape(DEPTH, 3, DRNN)
        nlp[:, i] = np.asarray(r["nlp"]).reshape(128, DEPTH, NB).transpose(1, 2, 0).reshape(DEPTH, DRNN)
        ngp[:, i] = np.asarray(r["ngp"]).reshape(DEPTH, 128, NH, DVH).transpose(0, 2, 1, 3)
        ncs[:, sl] = np.asarray(r["ncs"]).reshape(128, DEPTH, NB, NSEQ, 3).transpose(1, 3, 4, 2, 0).reshape(DEPTH, NSEQ, 3, DRNN)
        nls[:, sl] = np.asarray(r["nls"]).reshape(128, DEPTH, NB, NSEQ).transpose(1, 3, 2, 0).reshape(DEPTH, NSEQ, DRNN)
        ngs[:, sl] = np.asarray(r["ngs"]).reshape(DEPTH, 128, NSEQ, NH, DVH).transpose(0, 2, 3, 1, 4)
    return (y_prompt, y_sample, ncp, nlp, ngp, ncs, nls, ngs)
```

```python
import contextlib
import numpy as np
import concourse.bass as bass
import concourse.mybir as mybir
from concourse.bass_utils import run_bass_kernel_spmd

F32 = mybir.dt.float32
BF16 = mybir.dt.bfloat16
AF = mybir.ActivationFunctionType
ALU = mybir.AluOpType

NCORES = 8
D = 1024
KC = 8
SEQ = 2048
DEPTH = 2
NB = 10
DRNN = 1280
NH = 4
DK = 128
DVH = 256
DFF = 2816
FC = 22
NIN = 7696
EPS = 1e-6
NSEQ = 16
TS = 4
WS = NSEQ * TS
NTOK = SEQ + WS
TW = 512
O_X, O_G, O_Q, O_K, O_V, O_R, O_LR, O_GA, O_GB = 0, 1280, 2560, 3072, 3584, 4608, 5632, 5648, 6672
SLOT = 4096
NSLOT = 4


def _kc_pack(W):
    nk = W.shape[0] // 128
    return np.ascontiguousarray(W.reshape(nk, 128, -1).transpose(1, 0, 2).reshape(128, -1))


def _unit_list():
    u = [("Ag0", 4096), ("Ag1", 4096), ("Ag2", 2048)]
    u += [("Ax%d" % n, 1280) for n in range(NB)]
    u += [("Glr", 384), ("Gv0", 4096), ("Gv1", 4096), ("Gk", 4096), ("Gq", 4096)]
    u += [("Gr%d" % h, 2048) for h in range(NH)]
    for j in range(KC):
        u += [("Cg%d" % j, 2048), ("Cp%d" % j, 2304)]
    u += [("Wo0", 4096), ("Wo1", 4096)]
    u += [("F1_%d" % i, 2048) for i in range(FC)]
    u += [("F2_%d" % j, 2816) for j in range(KC)]
    return u


UNITS = _unit_list()
UOFF = {}
_o = 0
for _n, _s in UNITS:
    UOFF[_n] = (_o, _s)
    _o += _s
WPACK_N = _o


def _pack_layer(w_in, lru_wa, lru_wx, proj_a, proj_b, w_out, ffn_w1, ffn_w2):
    parts = {}
    parts["Ag0"] = _kc_pack(w_in[:, O_G:O_G + 512])
    parts["Ag1"] = _kc_pack(w_in[:, O_G + 512:O_G + 1024])
    parts["Ag2"] = _kc_pack(w_in[:, O_G + 1024:O_G + 1280])
    for n in range(NB):
        gw = [lru_wa[n - 1], lru_wx[n - 1]] if n > 0 else [np.zeros((128, 256), np.float32)]
        parts["Ax%d" % n] = np.concatenate([_kc_pack(w_in[:, O_X + 128 * n:O_X + 128 * n + 128])] + gw, axis=1)
    parts["Glr"] = np.concatenate([_kc_pack(w_in[:, O_LR:O_LR + 16]), lru_wa[NB - 1], lru_wx[NB - 1]], axis=1)
    parts["Gk"] = _kc_pack(w_in[:, O_K:O_K + 512])
    parts["Gq"] = _kc_pack(w_in[:, O_Q:O_Q + 512])
    parts["Gv0"] = _kc_pack(w_in[:, O_V:O_V + 512])
    parts["Gv1"] = _kc_pack(w_in[:, O_V + 512:O_V + 1024])
    for h in range(NH):
        parts["Gr%d" % h] = _kc_pack(w_in[:, O_R + 256 * h:O_R + 256 * h + 256])
    for j in range(KC):
        parts["Cg%d" % j] = _kc_pack(np.concatenate(
            [w_in[:, O_GA + 128 * j:O_GA + 128 * j + 128], w_in[:, O_GB + 128 * j:O_GB + 128 * j + 128]], axis=1))
        parts["Cp%d" % j] = np.concatenate(
            [_kc_pack(proj_a[:, 128 * j:128 * j + 128]), _kc_pack(proj_b[:, 128 * j:128 * j + 128])], axis=1)
    parts["Wo0"] = _kc_pack(w_out[:, 0:512])
    parts["Wo1"] = _kc_pack(w_out[:, 512:1024])
    for i in range(FC):
        parts["F1_%d" % i] = _kc_pack(np.concatenate(
            [ffn_w1[:, 128 * i:128 * i + 128], ffn_w1[:, DFF + 128 * i:DFF + 128 * i + 128]], axis=1))
    for j in range(KC):
        parts["F2_%d" % j] = _kc_pack(ffn_w2[:, 128 * j:128 * j + 128])
    out = np.empty((128, WPACK_N), np.float32)
    for n, s in UNITS:
        o, _ = UOFF[n]
        assert parts[n].shape == (128, s), (n, parts[n].shape, s)
        out[:, o:o + s] = parts[n]
    return out


SM = {}
_o = 0
for _l in range(DEPTH):
    for _n, _s in (("n1g", 8), ("n2g", 8), ("adab", 48), ("convw", 40), ("convb", 10), ("ba", 10),
                   ("bx", 10), ("lam", 10), ("gng", 8)):
        SM[(_n, _l)] = (_o, _s)
        _o += _s
SM["fg"] = (_o, 8)
_o += 8
SM_N = _o

C_TRI, C_U, C_MASK, C_TRIS, C_US, C_MASKS, C_SEQM = 0, 128, 256, 384, 448, 512, 576
C_N = 592


def _fm(v, nch):
    return np.ascontiguousarray(v.reshape(nch, 128).T)


def _consts():
    c = np.zeros((128, C_N), np.float32)
    j = np.arange(128)[:, None]
    i = np.arange(128)[None, :]
    c[:, C_TRI:C_TRI + 128] = (j <= i) * (-1.0 / 16.0)
    c[:, C_U:C_U + 128] = (j > i) * (-1.0 / 16.0)
    c[:, C_MASK:C_MASK + 128] = (j <= i) * 1.0
    j = np.arange(64)[:, None]
    i = np.arange(64)[None, :]
    same = (j // TS) == (i // TS)
    c[:64, C_TRIS:C_TRIS + 64] = (same & (j <= i)) * (-1.0 / 16.0)
    c[:64, C_US:C_US + 64] = (same & (j > i)) * (-1.0 / 16.0)
    c[:64, C_MASKS:C_MASKS + 64] = (same & (j <= i)) * 1.0
    c[:64, C_SEQM:C_SEQM + 16] = ((np.arange(64)[:, None] // TS) == np.arange(16)[None, :]) * 1.0
    return c


_STRICT = False


class _Op:
    __slots__ = ("eng", "fn", "deps", "dma", "sig", "sigval", "needs")

    def __init__(self, eng, fn, deps, dma):
        self.eng, self.fn, self.deps, self.dma = eng, fn, deps, dma
        self.sig = False
        self.sigval = 0
        self.needs = []


class Prog:
    def __init__(self):
        self.ops = []
        self.lastw = {}
        self.rd = {}

    def add(self, eng, fn, reads=(), writes=(), dma=None):
        idx = len(self.ops)
        deps = {}
        for k in reads:
            lw = self.lastw.get(k)
            if lw is not None:
                deps[lw] = "raw"
        for k in writes:
            lw = self.lastw.get(k)
            if lw is not None and lw not in deps:
                deps[lw] = "waw"
            r = self.rd.get(k)
            if r:
                for e, v in r.items():
                    if e == "dmas":
                        for x in v:
                            deps.setdefault(x, "war")
                    else:
                        deps.setdefault(v, "war")
        for k in reads:
            r = self.rd.setdefault(k, {})
            if dma is not None:
                r.setdefault("dmas", []).append(idx)
            else:
                r[eng] = idx
        for k in writes:
            self.lastw[k] = idx
            self.rd[k] = {}
        deps.pop(idx, None)
        self.ops.append(_Op(eng, fn, deps, dma))
        return idx

    def finalize(self):
        ops = self.ops
        for b in ops:
            for ai, kind in b.deps.items():
                a = ops[ai]
                if a.dma is not None:
                    need = True
                elif a.eng == b.eng:
                    if a.eng == "pe":
                        need = False
                    elif b.dma is not None:
                        need = True
                    else:
                        need = kind == "raw" or _STRICT
                else:
                    need = True
                if need:
                    a.sig = True
                    b.needs.append(ai)
        cnt = {}
        for o in ops:
            if o.dma is not None:
                cnt[o.dma] = cnt.get(o.dma, 0) + 16
                o.sigval = cnt[o.dma]
                o.sig = True
            elif o.sig:
                cnt[o.eng] = cnt.get(o.eng, 0) + 1
                o.sigval = cnt[o.eng]
        self.final_counts = cnt

    def emit(self, eng_name, eng, sems):
        waited = {}
        ops = self.ops
        for o in ops:
            if o.eng != eng_name:
                continue
            req = {}
            for ai in o.needs:
                a = ops[ai]
                sk = a.dma if a.dma is not None else a.eng
                if a.sigval > req.get(sk, 0):
                    req[sk] = a.sigval
            for sk, v in req.items():
                if waited.get(sk, 0) < v:
                    eng.wait_ge(sems[sk], v)
                    waited[sk] = v
            ins = o.fn(eng)
            if o.sig:
                sk = o.dma if o.dma is not None else o.eng
                ins.then_inc(sems[sk], 16 if o.dma is not None else 1)
        return waited


def _flat(*items):
    out = []
    for it in items:
        if isinstance(it, (tuple, list)):
            out.extend(_flat(*it))
        elif it is not None:
            out.append(it)
    return tuple(out)


def build_program():
    nc = bass.Bass("TRN2", target_bir_lowering=False)
    P = Prog()
    es = contextlib.ExitStack()

    def dram_in(name, shape):
        return nc.dram_tensor(name, list(shape), F32, kind="ExternalInput").ap()

    def dram_out(name, shape):
        return nc.dram_tensor(name, list(shape), F32, kind="ExternalOutput").ap()

    xT_d = dram_in("xT", [D, NTOK])
    cT_d = dram_in("cT", [D, 17])
    convs_d = dram_in("convs", [128, DEPTH * NB * NSEQ * 3])
    lrus_d = dram_in("lrus", [128, DEPTH * NB * NSEQ])
    glas_d = dram_in("glas", [DEPTH, 128, NSEQ * NH * DVH])
    small_d = dram_in("small", [128, SM_N])
    wa2e_d = dram_in("wa2e", [17, DEPTH * 512])
    consts_d = dram_in("consts", [128, C_N])
    adaw_d = dram_in("adaw", [DEPTH, 128, 12 * 4096])
    wpack_d = dram_in("wpack", [DEPTH, 128, WPACK_N])

    yT_d = dram_out("yT", [D, NTOK])
    ncp_d = dram_out("ncp", [128, DEPTH * NB * 3])
    nlp_d = dram_out("nlp", [128, DEPTH * NB])
    ngp_d = dram_out("ngp", [DEPTH, 128, NH * DVH])
    ncs_d = dram_out("ncs", [128, DEPTH * NB * NSEQ * 3])
    nls_d = dram_out("nls", [128, DEPTH * NB * NSEQ])
    ngs_d = dram_out("ngs", [DEPTH, 128, NSEQ * NH * DVH])

    dma_sems = []

    def sb(name, shape, dt=F32, dma=False):
        t = es.enter_context(nc.sbuf_tensor("sb_" + name, list(shape), dt))
        if dma:
            dma_sems.append("d_" + name)
        return t

    with es:
        wring = [sb("wr%d" % i, [128, SLOT], BF16, dma=True) for i in range(NSLOT)]
        S = sb("S", [128, DEPTH, NH, DVH], F32, dma=True)
        Sbf = sb("Sbf", [128, NH, DVH], BF16)
        Sbfc = sb("Sbfc", [128, 3, DVH], BF16)
        Sx = sb("Sx", [128, DVH], F32)
        modT = sb("modT", [128, DEPTH, 48, 17], F32)
        small = sb("small", [128, SM_N], F32, dma=True)
        slru = sb("slru", [128, DEPTH, NB], F32)
        hslru = sb("hslru", [128, DEPTH, NB], F32)
        sptmp = sb("sptmp", [128, DEPTH, NB], F32)
        hbias = sb("hbias", [128, DEPTH, 2, NB], F32)
        consts = sb("consts", [128, C_N], F32, dma=True)
        wa2e = sb("wa2e", [17, DEPTH * 512], F32, dma=True)
        ones_bf = sb("ones_bf", [128, 128], BF16)
        cT = sb("cT", [128, KC, 17], F32, dma=True)
        csb = sb("csb", [128, KC, 17], BF16)
        convc = sb("convc", [128, DEPTH, NB, 3], F32, dma=True)
        lruc = sb("lruc", [128, DEPTH, NB], F32, dma=True)
        convst = sb("convst", [128, DEPTH, NB, NSEQ, 3], F32, dma=True)
        lrust = sb("lrust", [128, DEPTH, NB, NSEQ], F32, dma=True)
        xas = sb("xas", [128, NSEQ, 7], F32)
        HS = 8
        khm = sb("khm", [64, HS, 128], BF16)
        GS = 4
        s0b = sb("s0b", [128, NSEQ, DVH], BF16, dma=True)
        sst01 = [sb("sst%d" % i, [128, GS, DVH], F32, dma=True) for i in range(2)]
        dummy = sb("dummy", [128, 2], F32)
        dpre = sb("dpre", [128, 2], F32)
        ccorr = sb("ccorr", [128, NB, 3], F32)
        ctmp = sb("ctmp", [128, NB], F32)
        swide = sb("swide", [64, 512], F32)

        psum = [es.enter_context(nc.psum_tensor("ps%d" % i, [128, TW], F32)) for i in range(8)]

        class Grp:
            pass

        def make_group(kind):
            g = Grp()
            g.kind = kind
            g.pre = kind + ":"
            W = TW if kind == "p" else WS
            g.W = W
            g.CH = 128 if kind == "p" else 64
            g.nch = W // g.CH
            n = kind
            g.x = sb(n + "x", [128, KC, W], F32)
            dma_sems.extend(["d_" + n + "x%d" % c for c in range(KC)])
            g.xsq = sb(n + "xsq", [128, KC, W], BF16)
            g.m = g.xsq
            g.h = sb(n + "h", [128, KC, W], BF16)
            g.yag = sb(n + "yag", [128, NB, W], BF16)
            g.yb = sb(n + "yb", [128, KC, W], BF16)
            words = FC * W // 2
            if kind == "s":
                words = max(words, 9 * W, 512 + 256 + 512)
            g.big = sb(n + "big", [128, words], F32)
            big = g.big
            g.ff = big[:, 0:FC * W // 2].bitcast(BF16).rearrange("p (c w) -> p c w", w=W)
            nch = g.nch
            g.v_tm = big[:, 0:nch * 512].bitcast(BF16).rearrange("p (c w) -> p c w", w=1024)
            g.khat = big[:, nch * 512:nch * 768].bitcast(BF16).rearrange("p (c w) -> p c w", w=512)
            g.gT = big[:, nch * 768:nch * 1280].rearrange("p (c w) -> p c w", w=512)
            g.lbuf = [big[:, i * W:(i + 1) * W] for i in range(9)]
            g.qt = sb(n + "qt", [128, NH, W], BF16)
            g.kt = sb(n + "kt", [128, NH, W], BF16)
            g.at_bf = sb(n + "at_bf", [g.CH, nch, g.CH], BF16)
            g.ulr = sb(n + "ulr", [32, W], F32)
            g.dec = sb(n + "dec", [128, NH, 16], F32)
            g.NF = 8
            g.fsc = [sb(n + "fs%d" % i, [128, W], F32, dma=True) for i in range(g.NF)]
            g.NBF = 6
            g.bsc = [sb(n + "bs%d" % i, [128, W], BF16) for i in range(g.NBF)]
            g.st = {"ps": 0, "fs": 0, "bs": 0, "psw": 0, "psh": 0}
            g.LKEYS = tuple(g.pre + "lb%d" % i for i in range(9))
            g.GKEYS = (g.pre + "gT", g.pre + "khat", g.pre + "vtm")
            g.FKEYS = (g.pre + "ff",)
            return g

        gp = make_group("p")
        gs = make_group("s")
        _xf = gp.xsq[:].rearrange("p c w -> p (c w)").bitcast(F32)
        sst = [sst01[0][:], sst01[1][:],
               _xf[:, 0:GS * DVH].rearrange("p (s v) -> p s v", v=DVH),
               _xf[:, GS * DVH:2 * GS * DVH].rearrange("p (s v) -> p s v", v=DVH)]
        sstk = ["sst0", "sst1", "sst2", "sst3"]
        dma_sems.extend(["d_sst2", "d_sst3"])

        eng_sems = {}
        for e in ("pe", "act", "dve", "pool", "sp"):
            eng_sems[e] = es.enter_context(nc.semaphore("sem_" + e))
        sems = dict(eng_sems)
        for k in dma_sems:
            sems[k] = es.enter_context(nc.semaphore(k))

        def PS(g):
            if g.kind == "p":
                nps = g.st.get("nps", 4)
                i = g.st["ps"] % nps
                g.st["ps"] = (i + 1) % nps
                return psum[i], ("ps%d" % i,)
            if g.st.get("restrict"):
                return psum[4], ("ps4",)
            i = g.st["ps"]
            g.st["ps"] = (i + 1) % 2
            return psum[4 + i], ("ps%d" % (4 + i),)

        PSW = PS
        PSH = PS

        def PO(g, vc):
            if g.kind == "p":
                return psum[6 + vc], ("ps%d" % (6 + vc),)
            return psum[5][:, vc * 64:(vc + 1) * 64], ("ps5",)

        def WIDE(g):
            if g.kind == "p":
                return FS(g)
            return swide, "s:swide"

        def FS(g):
            i = g.st["fs"]
            g.st["fs"] = (i + 1) % g.NF
            return g.fsc[i], g.pre + "fs%d" % i

        def BS(g):
            i = g.st["bs"]
            g.st["bs"] = (i + 1) % g.NBF
            return g.bsc[i], g.pre + "bs%d" % i

        def mm(out, lhsT, rhs, start, stop, reads, writes):
            P.add("pe", lambda e, o=out, l=lhsT, r=rhs, a=start, b=stop: e.matmul(o, lhsT=l, rhs=r, start=a, stop=b),
                  reads=_flat(reads), writes=_flat(writes))

        def act(out, in_, func, reads, writes, bias=None, scale=None):
            kw = {}
            if bias is not None:
                kw["bias"] = bias
            if scale is not None:
                kw["scale"] = scale
            P.add("act", lambda e, o=out, i=in_, f=func, kw=kw: e.activation(out=o, in_=i, func=f, **kw),
                  reads=_flat(reads), writes=_flat(writes))

        def tt(out, in0, in1, op, reads, writes):
            P.add("dve", lambda e, o=out, a=in0, b=in1, p=op: e.tensor_tensor(out=o, in0=a, in1=b, op=p),
                  reads=_flat(reads), writes=_flat(writes))

        def stt(out, in0, scalar, in1, op0, op1, reads, writes):
            P.add("dve", lambda e, o=out, a=in0, s=scalar, b=in1, p0=op0, p1=op1:
                  e.scalar_tensor_tensor(out=o, in0=a, scalar=s, in1=b, op0=p0, op1=p1),
                  reads=_flat(reads), writes=_flat(writes))

        def ts(out, in0, s1, s2, op0, op1, reads, writes):
            if s2 is None:
                P.add("dve", lambda e, o=out, a=in0, s=s1, p0=op0: e.tensor_scalar(out=o, in0=a, scalar1=s, scalar2=None, op0=p0),
                      reads=_flat(reads), writes=_flat(writes))
            else:
                P.add("dve", lambda e, o=out, a=in0, s=s1, t=s2, p0=op0, p1=op1:
                      e.tensor_scalar(out=o, in0=a, scalar1=s, scalar2=t, op0=p0, op1=p1),
                      reads=_flat(reads), writes=_flat(writes))

        def vcopy(out, in_, reads, writes):
            P.add("dve", lambda e, o=out, i=in_: e.tensor_copy(out=o, in_=i), reads=_flat(reads), writes=_flat(writes))

        def vmemset(out, val, writes):
            P.add("dve", lambda e, o=out, v=val: e.memset(o, v), reads=(), writes=_flat(writes))

        def scan(out, a, b, initial, reads, writes):
            P.add("dve", lambda e, o=out, a=a, b=b, ini=initial:
                  e.tensor_tensor_scan(out=o, data0=a, data1=b, initial=ini, op0=ALU.mult, op1=ALU.add),
                  reads=_flat(reads), writes=_flat(writes))

        def dma(q, out, in_, reads, writes, sem):
            P.add(q, lambda e, o=out, i=in_: e.dma_start(out=o, in_=i), reads=_flat(reads), writes=_flat(writes), dma=sem)

        def fence(keys):
            P.add("dve", lambda e: e.memset(dummy[:], 0.0), reads=(), writes=_flat(keys, "dummy"))

        def smc(name, l=None):
            o, s = SM[(name, l)] if l is not None else SM[name]
            return small[:, o:o + s]

        wstream = []
        ADA_SETUP = 4
        for u in range(ADA_SETUP):
            wstream.append((adaw_d[0, :, u * 4096:(u + 1) * 4096], 4096))
        ADA_AFTER = {}
        for k in range(8):
            ADA_AFTER[(0, k)] = [(0, 4 + k)]
            ADA_AFTER[(1, k)] = [(1, 4 + k)]
        ADA_AFTER[(0, 8)] = [(1, 0), (1, 1)]
        ADA_AFTER[(0, 9)] = [(1, 2), (1, 3)]
        NT = SEQ // TW
        SEPARATE = False
        unit_index = {}
        for ti in range(NT + (1 if SEPARATE else 0)):
            for l in range(DEPTH):
                for n, s in UNITS:
                    o, _ = UOFF[n]
                    unit_index[(ti, l, n)] = len(wstream)
                    wstream.append((wpack_d[l, :, o:o + s], s))
                    if ti == 0 and n.startswith("Ax"):
                        for (la, u) in ADA_AFTER.get((l, int(n[2:])), []):
                            unit_index[(0, l, "ADA_%d_%d" % (la, u))] = len(wstream)
                            wstream.append((adaw_d[la, :, u * 4096:(u + 1) * 4096], 4096))
        ws = {"loaded": 0, "cur": -1}

        def acquire(u):
            assert u == ws["cur"] + 1, (u, ws["cur"])
            ws["cur"] = u
            lim = min(len(wstream), u + NSLOT)
            while ws["loaded"] < lim:
                j = ws["loaded"]
                src, n = wstream[j]
                sl = j % NSLOT
                dma("pool", wring[sl][:, 0:n], src, reads=(), writes=("wr%d" % sl,), sem="d_wr%d" % sl)
                ws["loaded"] += 1
            sl = u % NSLOT
            return wring[sl], "wr%d" % sl

        dma("sp", small[:, :], small_d[:, :], (), ("small",), "d_small")
        dma("sp", consts[:, :], consts_d[:, :], (), ("consts",), "d_consts")
        dma("sp", wa2e[:, :], wa2e_d[:, :], (), ("wa2e",), "d_wa2e")
        dma("sp", cT[:], cT_d.rearrange("(c p) s -> p c s", p=128), (), ("cT",), "d_cT")
        dma("sp", convst[:].rearrange("p l n s t -> p (l n s t)"), convs_d[:, :], (), ("convst",), "d_convst")
        dma("sp", lrust[:].rearrange("p l n s -> p (l n s)"), lrus_d[:, :], (), ("lrust",), "d_lrust")
        vmemset(ones_bf[:], 1.0, ("ones",))
        vmemset(dpre[:], 1.0, ("dpre",))

        def preload(func):
            P.add("act", lambda e, f=func: e.activation(out=dpre[:, 1:2], in_=dpre[:, 0:1], func=f),
                  reads=("dpre",), writes=("dpre_junk",))
        vmemset(gp.ulr[:], 1.0, ("p:ulr",))
        vmemset(gs.ulr[:], 1.0, ("s:ulr",))
        vmemset(convc[:].rearrange("p l n t -> p (l n t)"), 0.0, ("convc",))
        vmemset(lruc[:].rearrange("p l n -> p (l n)"), 0.0, ("lruc",))
        vmemset(S[:].rearrange("p l h v -> p (l h v)"), 0.0, ("S0", "S1"))
        for l in range(DEPTH):
            act(sptmp[:, l, :], smc("lam", l), AF.Exp, ("small",), ("sptmp",), scale=-1.0)
        for l in range(DEPTH):
            act(sptmp[:, l, :], sptmp[:, l, :], AF.Ln, ("sptmp",), ("sptmp",), bias=1.0)
        sp2 = sptmp[:].rearrange("p l n -> p (l n)")
        ts(slru[:].rearrange("p l n -> p (l n)"), sp2, -8.0, None, ALU.mult, None, ("sptmp",), ("slru",))
        ts(hslru[:].rearrange("p l n -> p (l n)"), sp2, -4.0, None, ALU.mult, None, ("sptmp",), ("hslru",))
        for l in range(DEPTH):
            ts(hbias[:, l, 0, :], smc("ba", l), 0.5, None, ALU.mult, None, ("small",), ("hbias",))
            ts(hbias[:, l, 1, :], smc("bx", l), 0.5, None, ALU.mult, None, ("small",), ("hbias",))
        act(csb[:], cT[:], AF.Silu, ("cT",), ("csb",))
        def ada_unit(l, u, wt, wk, pst, pk):
            wv = wt[:, 0:4096].rearrange("p (k c) -> p k c", c=512)
            for jj in range(4):
                for kc in range(KC):
                    mm(pst[:, jj * 17:(jj + 1) * 17], wv[:, kc, jj * 128:(jj + 1) * 128], csb[:, kc, :],
                       kc == 0, kc == KC - 1, (wk, "csb"), pk)
            o, _ = SM[("adab", l)]
            tt(modT[:, l, u * 4:(u + 1) * 4, :], pst[:, 0:68].rearrange("p (j s) -> p j s", s=17),
               small[:, o + u * 4:o + u * 4 + 4].unsqueeze(2).to_broadcast([128, 4, 17]),
               ALU.add, (pk, "small"), ("modT%d" % l,))

        def ada_finish(l, grp, gname):
            mv = modT[:, l, grp * 8:(grp + 1) * 8, :]
            ts(mv, mv, 1.0, None, ALU.add, None, ("modT%d" % l,), ("modT%d" % l,))
            tt(mv, mv, smc(gname, l).unsqueeze(2).to_broadcast([128, 8, 17]), ALU.mult, ("modT%d" % l, "small"),
               ("modT%d" % l,))

        def ada_after(l, u):
            if u == 3:
                ada_finish(l, 1, "n1g")
            if u == 9:
                ada_finish(l, 4, "n2g")

        for u in range(ADA_SETUP):
            wt, wk = acquire(u)
            pst, pk = PS(gp)
            ada_unit(0, u, wt, wk, pst, pk)
            ada_after(0, u)

        def mod_col(l, grp, c):
            return modT[:, l, grp * 8 + c, 0:1]

        def mod_seq(l, grp, c):
            return modT[:, l, grp * 8 + c, 1:17].unsqueeze(2).to_broadcast([128, NSEQ, TS])

        def v3(ap):
            return ap.rearrange("p (s t) -> p s t", t=TS)

        def rms_stats(g, scale_div):
            W = g.W
            K = g.pre
            if g.kind == "p" and g.st.get("stats_ready"):
                flush_stat(g)
                g.st["stats_ready"] = False
                pst, pk = psum[7], ("ps7",)
                ft, fk = FS(g)
                act(ft[:, 0:W], pst[:, 0:W], AF.Ln, pk, fk, bias=EPS, scale=1.0 / scale_div)
                act(pst[:, 0:W], ft[:, 0:W], AF.Exp, fk, pk, scale=-0.5)
                nf = g.st.get("next_func")
                if nf is not None:
                    preload(nf)
                return pst, pk
            pst, pk = PS(g)
            for c in range(KC):
                act(g.xsq[:, c, 0:W], g.x[:, c, 0:W], AF.Square, K + "x%d" % c, K + "xsq")
                mm(pst[:, 0:W], ones_bf[:], g.xsq[:, c, 0:W], c == 0, c == KC - 1, ("ones", K + "xsq"), pk)
            ft, fk = FS(g)
            act(ft[:, 0:W], pst[:, 0:W], AF.Ln, pk, fk, bias=EPS, scale=1.0 / scale_div)
            act(pst[:, 0:W], ft[:, 0:W], AF.Exp, fk, pk, scale=-0.5)
            return pst, pk

        def norm_mod(g, l, g_grp, sh_grp):
            W = g.W
            K = g.pre
            pst, pk = rms_stats(g, float(D))
            for c in range(KC):
                ft, fk = FS(g)
                if g.kind == "p":
                    stt(ft[:, 0:W], g.x[:, c, 0:W], mod_col(l, g_grp, c), pst[:, 0:W], ALU.mult, ALU.mult,
                        (K + "x%d" % c, "modT%d" % l, pk), fk)
                    act(g.h[:, c, 0:W], ft[:, 0:W], AF.Identity, (fk, "modT%d" % l), K + "h%d" % c, bias=mod_col(l, sh_grp, c))
                else:
                    tt(ft[:, 0:W], g.x[:, c, 0:W], pst[:, 0:W], ALU.mult, (K + "x%d" % c, pk), fk)
                    tt(v3(ft[:, 0:W]), v3(ft[:, 0:W]), mod_seq(l, g_grp, c), ALU.mult, (fk, "modT%d" % l), fk)
                    tt(v3(g.h[:, c, 0:W]), v3(ft[:, 0:W]), mod_seq(l, sh_grp, c), ALU.add, (fk, "modT%d" % l), K + "h%d" % c)

        def dense_T(g, wv, cols, nk, rhs_t, rhs_key, wk):
            W = g.W
            pst, pk = PS(g)
            for kc in range(nk):
                rk = rhs_key[kc] if isinstance(rhs_key, list) else rhs_key
                mm(pst[:, 0:W], wv[:, kc, cols], rhs_t[:, kc, 0:W], kc == 0, kc == nk - 1, (wk, rk), pk)
            return pst, pk

        def flush_stat(g):
            j = g.st.get("pending_stat")
            if j is not None:
                W = g.W
                mm(psum[7][:, 0:W], ones_bf[:], g.h[:, j, 0:W], j == 0, j == KC - 1, ("ones", g.pre + "h%d" % j), "ps7")
                g.st["pending_stat"] = None

        def resid_add(g, l, gt_grp, j, pst, pk):
            W = g.W
            xk = g.pre + "x%d" % j
            if g.kind == "p":
                stt(g.x[:, j, 0:W], pst[:, 0:W], mod_col(l, gt_grp, j), g.x[:, j, 0:W], ALU.mult, ALU.add,
                    (pk, "modT%d" % l, xk), xk)
                flush_stat(g)
                act(g.h[:, j, 0:W], g.x[:, j, 0:W], AF.Square, xk, g.pre + "h%d" % j)
                g.st["pending_stat"] = j
                if j == KC - 1:
                    g.st["stats_ready"] = True
            else:
                ft, fk = FS(g)
                tt(v3(ft[:, 0:W]), v3(pst[:, 0:W]), mod_seq(l, gt_grp, j), ALU.mult, (pk, "modT%d" % l), fk)
                tt(g.x[:, j, 0:W], g.x[:, j, 0:W], ft[:, 0:W], ALU.add, (xk, fk), xk)

        def stage_lru(g, l):
            W = g.W
            K = g.pre
            kind = g.kind
            hk_ = [K + "h%d" % c for c in range(KC)]
            for gi, (nm, nblk) in enumerate((("Ag0", 4), ("Ag1", 4), ("Ag2", 2))):
                wt, wk = yield nm
                wv = wt[:, 0:8 * 128 * nblk].rearrange("p (k c) -> p k c", c=128 * nblk)
                for b in range(nblk):
                    n = gi * 4 + b
                    pst, pk = dense_T(g, wv, slice(b * 128, (b + 1) * 128), KC, g.h, hk_, wk)
                    act(g.yag[:, n, 0:W], pst[:, 0:W], AF.Gelu_apprx_tanh, pk, K + "yag%d" % n)
            fence(g.FKEYS + g.LKEYS)
            if kind == "s":
                dma("pool", s0b[:], glas_d[l].rearrange("p (s h v) -> p s h v", h=NH, v=DVH)[:, :, 0, :], (), "s0b", "d_s0b")
            if kind == "p":
                preload(AF.Tanh)
                for hd in range(NH):
                    act(Sbf[:, hd, :], S[:, l, hd, :], AF.Copy, "S%d" % l, "Sbf")
            cwo, _ = SM[("convw", l)]
            cbo, _ = SM[("convb", l)]
            stt_ = {}
            if kind == "p":
                def pmul(o, a, b):
                    P.add("pool", lambda e, o=o, a=a, b=b: e.tensor_tensor(out=o, in0=a, in1=b, op=ALU.mult),
                          reads=("convc", "small", "ccorr"), writes=("ccorr",))

                def padd(o, a, b):
                    P.add("pool", lambda e, o=o, a=a, b=b: e.tensor_tensor(out=o, in0=a, in1=b, op=ALU.add),
                          reads=("ccorr",), writes=("ccorr",))
                wv_ = lambda i: small[:, cwo + i * NB:cwo + (i + 1) * NB]
                cv_ = lambda j: convc[:, l, :, j]
                pmul(ccorr[:, :, 0], wv_(2), cv_(2))
                pmul(ctmp[:, :], wv_(1), cv_(1))
                padd(ccorr[:, :, 0], ccorr[:, :, 0], ctmp[:, :])
                pmul(ctmp[:, :], wv_(0), cv_(0))
                padd(ccorr[:, :, 0], ccorr[:, :, 0], ctmp[:, :])
                pmul(ccorr[:, :, 1], wv_(1), cv_(2))
                pmul(ctmp[:, :], wv_(0), cv_(1))
                padd(ccorr[:, :, 1], ccorr[:, :, 1], ctmp[:, :])
                pmul(ccorr[:, :, 2], wv_(0), cv_(2))

            def S1(n, wt, wk):
                wv = wt[:, 0:1024].rearrange("p (k c) -> p k c", c=128)
                pst, pk = dense_T(g, wv, slice(0, 128), KC, g.h, hk_, wk)
                acc, ak = g.lbuf[n % 3], K + "lb%d" % (n % 3)
                wcol = lambda i: small[:, cwo + i * NB + n:cwo + i * NB + n + 1]
                cb = small[:, cbo + n:cbo + n + 1]
                if kind == "p":
                    w3, w2, w1, w0 = wcol(3), wcol(2), wcol(1), wcol(0)
                    ops = [lambda: ts(acc[:, 0:W], pst[:, 0:W], w3, cb, ALU.mult, ALU.add, (pk, "small"), ak)]
                    for i, wi in ((2, w2), (1, w1), (0, w0)):
                        sh = 3 - i
                        ops.append(lambda sh=sh, wi=wi: stt(acc[:, sh:W], pst[:, 0:W - sh], wi, acc[:, sh:W], ALU.mult, ALU.add,
                                                            (pk, "small", ak), ak))
                    ops.append(lambda: tt(acc[:, 0:3], acc[:, 0:3], ccorr[:, n, :], ALU.add, (ak, "ccorr"), ak))
                    ops.append(lambda: vcopy(convc[:, l, n, :], pst[:, W - 3:W], pk, "convc"))
                    xcb, xk = BS(g)
                    ops.append(lambda: P.add("pool", lambda e, o=xcb[:, 0:W], i=acc[:, 0:W]: e.tensor_copy(out=o, in_=i),
                                             reads=_flat(ak), writes=_flat(xk)))
                    stt_[n] = {"acc": (acc, ak), "xcb": (xcb, xk)}
                    return ops
                else:
                    a3 = v3(acc[:, 0:W])
                    w3, w2, w1, w0 = wcol(3), wcol(2), wcol(1), wcol(0)
                    xcb, xk = BS(g)
                    ops = [lambda: vcopy(xas[:, :, 0:3], convst[:, l, n, :, :], "convst", "xas"),
                           lambda: act(xas[:, :, 3:7], v3(pst[:, 0:W]), AF.Copy, pk, "xas"),
                           lambda: ts(a3, xas[:, :, 3:7], w3, cb, ALU.mult, ALU.add, ("xas", "small"), ak)]
                    for i, wi in ((2, w2), (1, w1), (0, w0)):
                        ops.append(lambda i=i, wi=wi: stt(a3, xas[:, :, i:i + 4], wi, a3, ALU.mult, ALU.add, ("xas", "small", ak), ak))
                    ops.append(lambda: vcopy(convst[:, l, n, :, :], xas[:, :, 4:7], "xas", "convst"))
                    ops.append(lambda: act(xcb[:, 0:W], acc[:, 0:W], AF.Copy, ak, xk))
                    stt_[n] = {"acc": (acc, ak), "xcb": (xcb, xk)}
                    return ops

            def S2(n, wt, wk):
                o = 1024 if n < NB - 1 else 128
                xcb, xk = stt_[n]["xcb"]
                pr, prk = PS(g)
                mm(pr[:, 0:W], wt[:, o:o + 128], xcb[:, 0:W], True, True, (wk, xk), prk)
                pi, pik = PS(g)
                mm(pi[:, 0:W], wt[:, o + 128:o + 256], xcb[:, 0:W], True, True, (wk, xk), pik)
                if kind == "p":
                    tr, trk, ti_, tik = pr, prk, pi, pik
                else:
                    tr, trk = FS(g)
                    ti_, tik = FS(g)
                act(tr[:, 0:W], pr[:, 0:W], AF.Tanh, (prk, "hbias"), trk, bias=hbias[:, l, 0, n:n + 1], scale=0.5)
                act(ti_[:, 0:W], pi[:, 0:W], AF.Tanh, (pik, "hbias"), tik, bias=hbias[:, l, 1, n:n + 1], scale=0.5)
                pr, prk, pi, pik = tr, trk, ti_, tik
                at, atk = g.lbuf[5 + n % 2], K + "lb%d" % (5 + n % 2)
                act(at[:, 0:W], pr[:, 0:W], AF.Exp, (prk, "hslru"), atk, scale=hslru[:, l, n:n + 1], bias=hslru[:, l, n:n + 1])
                act(pr[:, 0:W], pr[:, 0:W], AF.Exp, (prk, "slru"), prk, scale=slru[:, l, n:n + 1], bias=slru[:, l, n:n + 1])
                stt_[n].update({"pr": (pr, prk), "pi": (pi, pik), "at": (at, atk)})

            def S2b(n):
                pr, prk = stt_[n]["pr"]
                act(pr[:, 0:W], pr[:, 0:W], AF.Sqrt, prk, prk, scale=-0.25, bias=0.25)

            def S3(n):
                acc, ak = stt_[n]["acc"]
                pr, prk = stt_[n]["pr"]
                pi, pik = stt_[n]["pi"]
                at, atk = stt_[n]["at"]
                it, ik = g.lbuf[3 + n % 2], K + "lb%d" % (3 + n % 2)
                ht, hk = g.lbuf[7 + n % 2], K + "lb%d" % (7 + n % 2)
                if kind == "p":
                    yk = K + "yag%d" % n
                    return [
                        lambda: stt(it[:, 0:W], pi[:, 0:W], 1.0, acc[:, 0:W], ALU.add, ALU.mult, (pik, ak), ik),
                        lambda: tt(it[:, 0:W], it[:, 0:W], pr[:, 0:W], ALU.mult, (ik, prk), ik),
                        lambda: scan(ht[:, 0:W], at[:, 0:W], it[:, 0:W], lruc[:, l, n:n + 1], (atk, ik, "lruc"), hk),
                        lambda: vcopy(lruc[:, l, n:n + 1], ht[:, W - 1:W], hk, "lruc"),
                        lambda: P.add("pool", lambda e, o=g.yag[:, n, 0:W], a=ht[:, 0:W], b=g.yag[:, n, 0:W]:
                                      e.tensor_tensor(out=o, in0=a, in1=b, op=ALU.mult),
                                      reads=_flat(hk, yk), writes=_flat(yk)),
                    ]
                a3 = v3(at[:, 0:W])
                b3 = v3(it[:, 0:W])
                tmp, tk = FS(g)
                yk = K + "yag%d" % n
                return [
                    lambda: stt(it[:, 0:W], pi[:, 0:W], 1.0, acc[:, 0:W], ALU.add, ALU.mult, (pik, ak), ik),
                    lambda: tt(it[:, 0:W], it[:, 0:W], pr[:, 0:W], ALU.mult, (ik, prk), ik),
                    lambda: tt(tmp[:, 0:NSEQ], a3[:, :, 0], lrust[:, l, n, :], ALU.mult, (atk, "lrust"), tk),
                    lambda: tt(b3[:, :, 0], b3[:, :, 0], tmp[:, 0:NSEQ], ALU.add, (ik, tk), ik),
                    lambda: vmemset(a3[:, :, 0], 0.0, atk),
                    lambda: scan(ht[:, 0:W], at[:, 0:W], it[:, 0:W], 0.0, (atk, ik), hk),
                    lambda: vcopy(lrust[:, l, n, :], v3(ht[:, 0:W])[:, :, 3], hk, "lrust"),
                    lambda: tt(g.yag[:, n, 0:W], ht[:, 0:W], g.yag[:, n, 0:W], ALU.mult, (hk, yk), yk),
                ]

            glr = None
            wt = wk = None
            for k in range(NB + 2):
                if k < NB:
                    wt, wk = yield "Ax%d" % k
                elif k == NB:
                    wt, wk = yield "Glr"
                    glr = (wt, wk)
                    wvl = wt[:, 0:128].rearrange("p (k c) -> p k c", c=16)
                    psl, pslk = PS(g)
                    for kc in range(KC):
                        mm(psl[0:16, 0:W], wvl[:, kc, :], g.h[:, kc, 0:W], kc == 0, kc == KC - 1, (wk, hk_), pslk)
                    act(g.ulr[0:16, 0:W], psl[0:16, 0:W], AF.Copy, pslk, K + "ulr")
                a_ops = S3(k - 2) if 0 <= k - 2 < NB else None
                b_ops = S1(k, wt, wk) if k < NB else None
                a_ops, b_ops = list(a_ops or []), list(b_ops or [])
                while a_ops or b_ops:
                    if a_ops:
                        a_ops.pop(0)()
                    if b_ops:
                        b_ops.pop(0)()
                if 0 <= k - 1 < NB:
                    S2(k - 1, wt, wk)
                    yield ("sync", k)
                    S2b(k - 1)
                if g.st.get("ada1"):
                    for (la, u) in ADA_AFTER.get((l, k), []):
                        wt2, wk2 = yield "ADA_%d_%d" % (la, u)
                        if kind == "p":
                            ada_unit(la, u, wt2, wk2, psum[7], ("ps7",))
                            ada_after(la, u)
            return glr

        def stage_gla(g, l, glr, last_tile):
            W = g.W
            K = g.pre
            kind = g.kind
            CH, nch = g.CH, g.nch
            hk_ = [K + "h%d" % c for c in range(KC)]
            if kind == "p":
                c_tri = consts[0:CH, C_TRI:C_TRI + 128]
                c_u = consts[0:CH, C_U:C_U + 128]
                c_mask = consts[0:CH, C_MASK:C_MASK + 128]
            else:
                c_tri = consts[0:CH, C_TRIS:C_TRIS + 64]
                c_u = consts[0:CH, C_US:C_US + 64]
                c_mask = consts[0:CH, C_MASKS:C_MASKS + 64]
            gT, khat, v_tm, qt, kt, at_bf, dec, ulr = g.gT, g.khat, g.v_tm, g.qt, g.kt, g.at_bf, g.dec, g.ulr
            fence(g.LKEYS + g.GKEYS)
            if kind == "p":
                preload(AF.Ln)
            Sk, Sbk = "S%d" % l, "Sbf"
            for c in range(nch):
                pst, pk = PSW(g)
                mm(pst[0:CH, 0:512], ulr[0:17, c * CH:(c + 1) * CH], wa2e[0:17, l * 512:(l + 1) * 512], True, True,
                   (K + "ulr", "wa2e"), pk)
                ft, fk = (FS(g) if kind == "p" else (None, None))
                if kind == "p":
                    act(ft[0:CH, :], pst[0:CH, :], AF.Exp, pk, fk, scale=-1.0)
                    act(gT[0:CH, c, :], ft[0:CH, :], AF.Ln, fk, K + "gT", bias=1.0)
                else:
                    act(pst[0:CH, :], pst[0:CH, :], AF.Exp, pk, pk, scale=-1.0)
                    act(gT[0:CH, c, :], pst[0:CH, :], AF.Ln, pk, K + "gT", bias=1.0)
            for half in range(2):
                wt, wk = yield "Gv%d" % half
                wv = wt[:, 0:4096].rearrange("p (k c) -> p k c", c=512)
                for c in range(nch):
                    pst, pk = PSW(g)
                    for kc in range(KC):
                        mm(pst[0:CH, 0:512], g.h[:, kc, c * CH:(c + 1) * CH], wv[:, kc, :], kc == 0, kc == KC - 1,
                           (wk, hk_), pk)
                    act(v_tm[0:CH, c, half * 512:(half + 1) * 512], pst[0:CH, :], AF.Copy, pk, K + "vtm")
            wt, wk = yield "Gk"
            wv = wt[:, 0:4096].rearrange("p (k c) -> p k c", c=512)
            for c in range(nch):
                pkd, pkdk = PSW(g)
                mm(pkd[0:CH, 0:512], c_u, gT[0:CH, c, :], True, True, ("consts", K + "gT"), pkdk)
                kd, kdk = WIDE(g)
                act(kd[0:CH, :], pkd[0:CH, :], AF.Exp, pkdk, kdk)
                pst, pk = PSW(g)
                for kc in range(KC):
                    mm(pst[0:CH, 0:512], g.h[:, kc, c * CH:(c + 1) * CH], wv[:, kc, :], kc == 0, kc == KC - 1,
                       (wk, hk_), pk)
                tt(khat[0:CH, c, :], pst[0:CH, :], kd[0:CH, :], ALU.mult, (pk, kdk), K + "khat")
            eplus = []
            for hd in range(NH):
                pb, pbk = PS(g)
                for c in range(nch):
                    mm(pb[:, c * CH:(c + 1) * CH], gT[0:CH, c, hd * 128:(hd + 1) * 128], c_tri, True, True,
                       (K + "gT", "consts"), pbk)
                ep, epk = FS(g)
                act(ep[:, 0:W], pb[:, 0:W], AF.Exp, pbk, epk)
                em, emk = FS(g)
                act(em[:, 0:W], pb[:, 0:W], AF.Exp, pbk, emk, scale=-1.0)
                if kind == "p":
                    vcopy(dec[:, hd, 0:nch], ep[:, 0:W].rearrange("p (c t) -> p c t", t=CH)[:, :, CH - 1], epk, K + "dec")
                else:
                    vcopy(dec[:, hd, 0:NSEQ], v3(ep[:, 0:W])[:, :, TS - 1], epk, K + "dec")
                eplus.append((ep, epk))
                pst, pk = dense_T(g, wv, slice(hd * 128, (hd + 1) * 128), KC, g.h, hk_, wk)
                tt(kt[:, hd, 0:W], pst[:, 0:W], em[:, 0:W], ALU.mult, (pk, emk), K + "kt")
            wt, wk = yield "Gq"
            wv = wt[:, 0:4096].rearrange("p (k c) -> p k c", c=512)
            for hd in range(NH):
                ep, epk = eplus[hd]
                pst, pk = dense_T(g, wv, slice(hd * 128, (hd + 1) * 128), KC, g.h, hk_, wk)
                stt(qt[:, hd, 0:W], pst[:, 0:W], float(DK) ** -0.5, ep[:, 0:W], ALU.mult, ALU.mult, (pk, epk), K + "qt")
            gno, _ = SM[("gng", l)]
            if kind == "p":
                hs_ = {}

                def Xu_mm(hd, wt, wk):
                    wv = wt[:, 0:2048].rearrange("p (k c) -> p k c", c=256)
                    hs_[hd] = {"ur": [dense_T(g, wv, slice(vc * 128, (vc + 1) * 128), KC, g.h, hk_, wk) for vc in range(2)]}

                def Xu_act(hd):
                    srs = []
                    for vc in range(2):
                        pst, pk = hs_[hd]["ur"][vc]
                        sr, srk = FS(g)
                        act(sr[:, 0:W], pst[:, 0:W], AF.Silu, pk, srk)
                        srs.append((sr, srk))
                    hs_[hd]["sr"] = srs

                def Xa(hd):
                    pa, pak = PS(g)
                    for c in range(nch):
                        mm(pa[0:CH, c * CH:(c + 1) * CH], kt[:, hd, c * CH:(c + 1) * CH], qt[:, hd, c * CH:(c + 1) * CH],
                           True, True, (K + "kt", K + "qt"), pak)
                    dS = [PS(g), PS(g)]
                    for c in range(nch):
                        dst, dsk = dS[c // 2]
                        mm(dst[:, (c % 2) * 256:(c % 2 + 1) * 256], khat[0:CH, c, hd * 128:(hd + 1) * 128],
                           v_tm[0:CH, c, hd * 256:(hd + 1) * 256], True, True, (K + "khat", K + "vtm"), dsk)
                    tt(at_bf[0:CH, 0:nch, 0:CH], pa[0:CH, 0:W].rearrange("p (c t) -> p c t", t=CH),
                       c_mask.unsqueeze(1).to_broadcast([CH, nch, CH]), ALU.mult, (pak, "consts"), K + "at_bf")
                    bufs = [(S[:, l, hd, :], Sk), (Sx[:, :], "Sx")]
                    for c in range(nch):
                        dst, dsk = dS[c // 2]
                        src, srck = bufs[c % 2]
                        out, outk = bufs[(c + 1) % 2]
                        stt(out, src, dec[:, hd, c:c + 1], dst[:, (c % 2) * 256:(c % 2 + 1) * 256],
                            ALU.mult, ALU.add, (srck, K + "dec", dsk), outk)
                        if c < nch - 1:
                            act(Sbfc[:, c, :], out, AF.Copy, outk, "Sbfc%d" % c)

                def X4(hd):
                    po = [PO(g, 0), PO(g, 1)]
                    for c in range(nch):
                        for vc in range(2):
                            pot, pok = po[vc]
                            mm(pot[:, c * CH:(c + 1) * CH], v_tm[0:CH, c, hd * 256 + vc * 128:hd * 256 + (vc + 1) * 128],
                               at_bf[0:CH, c, 0:CH], True, False, (K + "vtm", K + "at_bf"), pok)
                            if c == 0:
                                mm(pot[:, 0:CH], Sbf[:, hd, vc * 128:(vc + 1) * 128], qt[:, hd, 0:CH], False, True,
                                   (Sbk, K + "qt"), pok)
                            else:
                                mm(pot[:, c * CH:(c + 1) * CH], Sbfc[:, c - 1, vc * 128:(vc + 1) * 128],
                                   qt[:, hd, c * CH:(c + 1) * CH], False, True, ("Sbfc%d" % (c - 1), K + "qt"), pok)
                    hs_[hd]["po"] = po

                def YA_evac(hd):
                    po = hs_[hd]["po"]
                    pn, pnk = PS(g)
                    for vc in range(2):
                        cidx = hd * 2 + vc
                        pot, pok = po[vc]
                        sr, srk = hs_[hd]["sr"][vc]
                        o2, o2k = BS(g)
                        act(o2[:, 0:W], pot[:, 0:W], AF.Square, pok, o2k)
                        mm(pn[:, 0:W], ones_bf[:], o2[:, 0:W], vc == 0, vc == 1, ("ones", o2k), pnk)
                        stt(sr[:, 0:W], pot[:, 0:W], small[:, gno + cidx:gno + cidx + 1], sr[:, 0:W], ALU.mult, ALU.mult,
                            (pok, "small", srk, o2k), srk)
                    hs_[hd]["pn"] = (pn, pnk)

                def YA_norm(hd):
                    pn, pnk = hs_[hd]["pn"]
                    ft, fk = FS(g)
                    act(ft[:, 0:W], pn[:, 0:W], AF.Ln, pnk, fk, bias=EPS, scale=1.0 / DVH)
                    act(ft[:, 0:W], ft[:, 0:W], AF.Exp, fk, fk, scale=-0.5)
                    hs_[hd]["rs"] = (ft, fk)
                    if hd == NH - 1:
                        preload(AF.Sigmoid)

                def YB(hd):
                    rs, rsk = hs_[hd]["rs"]
                    for vc in range(2):
                        cidx = hd * 2 + vc
                        sr, srk = hs_[hd]["sr"][vc]
                        tt(g.yb[:, cidx, 0:W], sr[:, 0:W], rs[:, 0:W], ALU.mult, (srk, rsk), K + "yb")

                wt, wk = yield "Gr0"
                Xu_mm(0, wt, wk)
                Xu_act(0)
                Xa(0)
                for hd in range(NH):
                    if hd + 1 < NH:
                        wt, wk = yield "Gr%d" % (hd + 1)
                    X4(hd)
                    yield ("h", hd, 0)
                    if hd + 1 < NH:
                        Xu_mm(hd + 1, wt, wk)
                    yield ("h", hd, 1)
                    YA_evac(hd)
                    if hd + 1 < NH:
                        Xu_act(hd + 1)
                    YA_norm(hd)
                    yield ("h", hd, 2)
                    if hd + 1 < NH:
                        Xa(hd + 1)
                    yield ("h", hd, 3)
                    YB(hd)
            else:
                gsrc = glas_d[l].rearrange("p (s h v) -> p s h v", h=NH, v=DVH)
                gdst = ngs_d[l].rearrange("p (s h v) -> p s h v", h=NH, v=DVH)
                fence(("p:xsq", "sst2", "sst3"))
                NG = NSEQ // GS

                def sst_load(gi):
                    hd_, grp_ = divmod(gi, NG)
                    bi = gi % 4
                    dma("sp", sst[bi], gsrc[:, grp_ * GS:(grp_ + 1) * GS, hd_, :], (), sstk[bi], "d_" + sstk[bi])

                sst_load(0)
                sst_load(1)
                hs_ = {}

                def Xu_s(hd, wt, wk):
                    wv = wt[:, 0:2048].rearrange("p (k c) -> p k c", c=256)
                    srs = []
                    for vc in range(2):
                        pst, pk = dense_T(g, wv, slice(vc * 128, (vc + 1) * 128), KC, g.h, hk_, wk)
                        sr, srk = FS(g)
                        act(sr[:, 0:W], pst[:, 0:W], AF.Silu, pk, srk)
                        srs.append((sr, srk))
                    hs_[hd] = {"sr": srs}

                def seg0(hd):
                    pa, pak = PS(g)
                    mm(pa[0:CH, 0:CH], kt[:, hd, 0:CH], qt[:, hd, 0:CH], True, True, (K + "kt", K + "qt"), pak)
                    tt(at_bf[0:CH, 0:1, 0:CH], pa[0:CH, 0:W].rearrange("p (c t) -> p c t", t=CH),
                       c_mask.unsqueeze(1).to_broadcast([CH, 1, CH]), ALU.mult, (pak, "consts"), K + "at_bf")

                def seg1(hd):
                    g.st["restrict"] = True
                    for vc in range(2):
                        pot, pok = PO(g, vc)
                        mm(pot[:, 0:W], v_tm[0:CH, 0, hd * 256 + vc * 128:hd * 256 + (vc + 1) * 128],
                           at_bf[0:CH, 0, 0:CH], True, False, (K + "vtm", K + "at_bf"), pok)
                        for s_ in range(NSEQ):
                            mm(pot[:, s_ * TS:(s_ + 1) * TS], s0b[:, s_, vc * 128:(vc + 1) * 128],
                               qt[:, hd, s_ * TS:(s_ + 1) * TS], False, s_ == NSEQ - 1, ("s0b", K + "qt"), pok)
                    if hd + 1 < NH:
                        dma("pool", s0b[:], gsrc[:, :, hd + 1, :], (), "s0b", "d_s0b")

                def seg2(hd):
                    pn, pnk = PS(g)
                    for vc in range(2):
                        cidx = hd * 2 + vc
                        pot, pok = PO(g, vc)
                        sr, srk = hs_[hd]["sr"][vc]
                        o2, o2k = BS(g)
                        act(o2[:, 0:W], pot[:, 0:W], AF.Square, (pok, "s:poevac"), o2k)
                        mm(pn[:, 0:W], ones_bf[:], o2[:, 0:W], vc == 0, vc == 1, ("ones", o2k), pnk)
                        stt(sr[:, 0:W], pot[:, 0:W], small[:, gno + cidx:gno + cidx + 1], sr[:, 0:W], ALU.mult, ALU.mult,
                            (pok, "small", srk, o2k), (srk, "s:poevac"))
                    ft, fk = FS(g)
                    act(ft[:, 0:W], pn[:, 0:W], AF.Ln, pnk, fk, bias=EPS, scale=1.0 / DVH)
                    act(ft[:, 0:W], ft[:, 0:W], AF.Exp, fk, fk, scale=-0.5)
                    hs_[hd]["rs"] = (ft, fk)
                    g.st["restrict"] = False

                def seg3(hd):
                    for grp in range(NG):
                        gi = hd * NG + grp
                        if gi + 2 < NH * NG:
                            sst_load(gi + 2)
                        bi = gi % 4
                        bt, bk = sst[bi], sstk[bi]
                        if (grp * GS) % HS == 0:
                            hf = (grp * GS) // HS
                            tt(khm[:, :, :], khat[0:CH, 0, hd * 128:(hd + 1) * 128].unsqueeze(1).to_broadcast([CH, HS, 128]),
                               consts[0:CH, C_SEQM + hf * HS:C_SEQM + (hf + 1) * HS].unsqueeze(2).to_broadcast([CH, HS, 128]),
                               ALU.mult, (K + "khat", "consts"), "khm")
                        for pair in range(GS // 2):
                            pss, psk = PS(g)
                            for q in range(2):
                                sg = pair * 2 + q
                                s_ = grp * GS + sg
                                mm(pss[:, q * 256:(q + 1) * 256], khm[:, s_ % HS, :], v_tm[0:CH, 0, hd * 256:(hd + 1) * 256],
                                   True, True, ("khm", K + "vtm"), psk)
                            for q in range(2):
                                sg = pair * 2 + q
                                s_ = grp * GS + sg
                                stt(bt[:, sg, :], bt[:, sg, :], dec[:, hd, s_:s_ + 1], pss[:, q * 256:(q + 1) * 256],
                                    ALU.mult, ALU.add, (bk, K + "dec", psk), bk)
                        dma("sp", gdst[:, grp * GS:(grp + 1) * GS, hd, :], bt, bk, (), "d_" + bk)
                    if hd == NH - 1:
                        fence(("sst2", "sst3", "p:xsq"))

                def seg4(hd):
                    rs, rsk = hs_[hd]["rs"]
                    for vc in range(2):
                        cidx = hd * 2 + vc
                        sr, srk = hs_[hd]["sr"][vc]
                        tt(g.yb[:, cidx, 0:W], sr[:, 0:W], rs[:, 0:W], ALU.mult, (srk, rsk), K + "yb")

                wt, wk = yield "Gr0"
                Xu_s(0, wt, wk)
                for hd in range(NH):
                    if hd + 1 < NH:
                        wt, wk = yield "Gr%d" % (hd + 1)
                        Xu_s(hd + 1, wt, wk)
                    seg0(hd)
                    yield ("h", hd, 0)
                    seg1(hd)
                    yield ("h", hd, 1)
                    seg2(hd)
                    yield ("h", hd, 2)
                    seg3(hd)
                    yield ("h", hd, 3)
                    seg4(hd)
            if kind == "p" and last_tile:
                dma("sp", ngp_d[l], S[:, l, :, :].rearrange("p h v -> p (h v)"), Sk, (), "d_S")

        def stage_merge(g, l):
            W = g.W
            K = g.pre
            hk_ = [K + "h%d" % c for c in range(KC)]
            yak = tuple(K + "yag%d" % n for n in range(NB))
            for j in range(KC):
                wt, wk = yield "Cg%d" % j
                wv = wt[:, 0:2048].rearrange("p (k c) -> p k c", c=256)
                pa, pak = dense_T(g, wv, slice(0, 128), KC, g.h, hk_, wk)
                sa, sak = FS(g)
                act(sa[:, 0:W], pa[:, 0:W], AF.Sigmoid, pak, sak)
                pb, pbk = dense_T(g, wv, slice(128, 256), KC, g.h, hk_, wk)
                sbt, sbk = FS(g)
                act(sbt[:, 0:W], pb[:, 0:W], AF.Sigmoid, pbk, sbk)
                if j == KC - 1 and g.kind == "p":
                    preload(AF.Ln)
                wt, wk = yield "Cp%d" % j
                wpa = wt[:, 0:1280].rearrange("p (k c) -> p k c", c=128)
                wpb = wt[:, 1280:2304].rearrange("p (k c) -> p k c", c=128)
                ppa, ppak = dense_T(g, wpa, slice(0, 128), NB, g.yag, yak, wk)
                tt(sa[:, 0:W], sa[:, 0:W], ppa[:, 0:W], ALU.mult, (sak, ppak), sak)
                ppb, ppbk = dense_T(g, wpb, slice(0, 128), KC, g.yb, K + "yb", wk)
                tt(sbt[:, 0:W], sbt[:, 0:W], ppb[:, 0:W], ALU.mult, (sbk, ppbk), sbk)
                tt(g.m[:, j, 0:W], sa[:, 0:W], sbt[:, 0:W], ALU.add, (sak, sbk), K + "xsq")
            for jj in range(2):
                wt, wk = yield "Wo%d" % jj
                wv = wt[:, 0:4096].rearrange("p (k c) -> p k c", c=512)
                for j4 in range(4):
                    j = jj * 4 + j4
                    pst, pk = dense_T(g, wv, slice(j4 * 128, (j4 + 1) * 128), KC, g.m, K + "xsq", wk)
                    resid_add(g, l, 2, j, pst, pk)

        def stage_ffn(g, l):
            W = g.W
            K = g.pre
            hk_ = [K + "h%d" % c for c in range(KC)]
            fence(g.GKEYS + g.FKEYS)
            for i in range(FC):
                wt, wk = yield "F1_%d" % i
                wv = wt[:, 0:2048].rearrange("p (k c) -> p k c", c=256)
                p1, p1k = dense_T(g, wv, slice(0, 128), KC, g.h, hk_, wk)
                s1, s1k = FS(g)
                act(s1[:, 0:W], p1[:, 0:W], AF.Silu, p1k, s1k)
                p2, p2k = dense_T(g, wv, slice(128, 256), KC, g.h, hk_, wk)
                tt(g.ff[:, i, 0:W], s1[:, 0:W], p2[:, 0:W], ALU.mult, (s1k, p2k), K + "ff")
                if i == FC - 1 and g.kind == "p":
                    preload(AF.Ln)
            for j in range(KC):
                wt, wk = yield "F2_%d" % j
                wv = wt[:, 0:2816].rearrange("p (k c) -> p k c", c=128)
                pst, pk = dense_T(g, wv, slice(0, 128), FC, g.ff, K + "ff", wk)
                resid_add(g, l, 5, j, pst, pk)

        def final_norm(g, col0, next_col0=None):
            W = g.W
            K = g.pre
            pst, pk = rms_stats(g, float(D))
            fo, _ = SM["fg"]
            for c in range(KC):
                ft, fk = FS(g)
                stt(ft[:, 0:W], g.x[:, c, 0:W], small[:, fo + c:fo + c + 1], pst[:, 0:W], ALU.mult, ALU.mult,
                    (K + "x%d" % c, "small", pk), fk)
                dma("sp", yT_d[c * 128:(c + 1) * 128, col0:col0 + W], ft[:, 0:W], fk, (), "d_" + g.kind + fk.split(":")[1])
                if next_col0 is not None:
                    dma("sp", g.x[:, c, 0:W], xT_d[c * 128:(c + 1) * 128, next_col0:next_col0 + W], (), K + "x%d" % c,
                        "d_" + g.kind + "x%d" % c)
            if next_col0 is not None:
                g.st["x_preloaded"] = True

        def tile_prog(g, col0, last_tile, next_col0=None):
            W = g.W
            xkeys = tuple(g.pre + "x%d" % c for c in range(KC))
            g.st["stats_ready"] = False
            if g.st.get("x_preloaded"):
                g.st["x_preloaded"] = False
            else:
                for c in range(KC):
                    dma("sp", g.x[:, c, 0:W], xT_d[c * 128:(c + 1) * 128, col0:col0 + W], (), xkeys[c],
                        "d_" + g.kind + "x%d" % c)
            for l in range(DEPTH):
                g.st["next_func"] = AF.Gelu_apprx_tanh
                norm_mod(g, l, 1, 0)
                glr = yield from stage_lru(g, l)
                yield from stage_gla(g, l, glr, last_tile)
                yield from stage_merge(g, l)
                g.st["next_func"] = AF.Silu
                norm_mod(g, l, 4, 3)
                yield from stage_ffn(g, l)
                yield ("layer_end", l)
            g.st["next_func"] = None
            final_norm(g, col0, next_col0)

        def run_lockstep(ti, gens):
            reqs = [next(gen, None) for gen in gens]
            cur_l = 0
            while any(r is not None for r in reqs):
                assert all(r == reqs[0] for r in reqs), reqs
                name = reqs[0]
                if isinstance(name, tuple):
                    if name[0] == "layer_end":
                        cur_l = name[1] + 1
                    slot = None
                else:
                    slot = acquire(unit_index[(ti, cur_l, name)])
                reqs = _send_all(gens, slot)

        def _send_all(gens, slot):
            out = []
            for gen in gens:
                try:
                    out.append(gen.send(slot))
                except StopIteration:
                    out.append(None)
            return out

        for ti in range(NT):
            last_tile = ti == NT - 1
            gp.st["ada1"] = gs.st["ada1"] = (ti == 0)
            gp.st["nps"] = 4 if ti == 0 else 6
            gens = [tile_prog(gp, ti * TW, last_tile, None if last_tile else (ti + 1) * TW)]
            if ti == 0 and not SEPARATE:
                gens.append(tile_prog(gs, SEQ, False))
            run_lockstep(ti, gens)
            if last_tile:
                dma("sp", ncp_d[:, :], convc[:].rearrange("p l n t -> p (l n t)"), "convc", (), "d_convc")
                dma("sp", nlp_d[:, :], lruc[:].rearrange("p l n -> p (l n)"), "lruc", (), "d_lruc")
        if SEPARATE:
            run_lockstep(NT, [tile_prog(gs, SEQ, False)])
        dma("sp", ncs_d[:, :], convst[:].rearrange("p l n s t -> p (l n s t)"), "convst", (), "d_convst")
        dma("sp", nls_d[:, :], lrust[:].rearrange("p l n s -> p (l n s)"), "lrust", (), "d_lrust")

        P.finalize()
        block = es.enter_context(nc.Block())

        @block.tensor
        def _(e):
            P.emit("pe", e, sems)

        @block.scalar
        def _(e):
            P.emit("act", e, sems)

        @block.vector
        def _(e):
            P.emit("dve", e, sems)

        @block.gpsimd
        def _(e):
            P.emit("pool", e, sems)

        @block.sync
        def _(e):
            waited = P.emit("sp", e, sems)
            for sk, v in P.final_counts.items():
                if sk.startswith("d_") and waited.get(sk, 0) < v:
                    e.wait_ge(sems[sk], v)
    return nc


_CACHE = {}


def kernel(x_prompt, x_sample, c_prompt, c_sample, state_conv, state_lru, state_gla,
           norm1_g, norm2_g, ada_w, ada_b, w_in, conv_w, conv_b, lru_wa, lru_ba, lru_wx, lru_bx,
           lru_lambda, gla_wa2, gla_ba, gla_norm_g, proj_a, proj_b, w_out, ffn_w1, ffn_w2, final_g):
    f = lambda a: np.asarray(a, dtype=np.float32)
    x_prompt, x_sample, c_prompt, c_sample = f(x_prompt), f(x_sample), f(c_prompt), f(c_sample)
    state_conv, state_lru, state_gla = f(state_conv), f(state_lru), f(state_gla)
    norm1_g, norm2_g, ada_w, ada_b, w_in = f(norm1_g), f(norm2_g), f(ada_w), f(ada_b), f(w_in)
    conv_w, conv_b, lru_wa, lru_ba, lru_wx, lru_bx = f(conv_w), f(conv_b), f(lru_wa), f(lru_ba), f(lru_wx), f(lru_bx)
    lru_lambda, gla_wa2, gla_ba, gla_norm_g = f(lru_lambda), f(gla_wa2), f(gla_ba), f(gla_norm_g)
    proj_a, proj_b, w_out, ffn_w1, ffn_w2, final_g = f(proj_a), f(proj_b), f(w_out), f(ffn_w1), f(ffn_w2), f(final_g)

    wpack = np.stack([_pack_layer(w_in[l], lru_wa[l], lru_wx[l], proj_a[l], proj_b[l], w_out[l], ffn_w1[l], ffn_w2[l])
                      for l in range(DEPTH)])
    adaw = np.stack([np.concatenate([_kc_pack(ada_w[l][:, u * 512:(u + 1) * 512]) for u in range(12)], axis=1)
                     for l in range(DEPTH)])
    small = np.zeros((128, SM_N), np.float32)
    for l in range(DEPTH):
        def put(name, arr):
            o, s = SM[(name, l)]
            small[:, o:o + s] = arr
        put("n1g", _fm(norm1_g[l], 8))
        put("n2g", _fm(norm2_g[l], 8))
        put("adab", _fm(ada_b[l], 48))
        put("convw", np.concatenate([_fm(conv_w[l, i], NB) for i in range(4)], axis=1))
        put("convb", _fm(conv_b[l], NB))
        put("ba", _fm(lru_ba[l], NB))
        put("bx", _fm(lru_bx[l], NB))
        put("lam", _fm(lru_lambda[l], NB))
        put("gng", _fm(gla_norm_g[l], 8))
    o, s = SM["fg"]
    small[:, o:o + s] = _fm(final_g, 8)
    wa2e = np.zeros((17, DEPTH * 512), np.float32)
    for l in range(DEPTH):
        wa2e[0:16, l * 512:(l + 1) * 512] = gla_wa2[l]
        wa2e[16, l * 512:(l + 1) * 512] = gla_ba[l]
    consts = _consts()

    in_maps = []
    for i in range(NCORES):
        sl = slice(i * NSEQ, (i + 1) * NSEQ)
        xT = np.empty((D, NTOK), np.float32)
        xT[:, :SEQ] = x_prompt[i].T
        xT[:, SEQ:] = x_sample[sl].reshape(WS, D).T
        cT = np.empty((D, 17), np.float32)
        cT[:, 0] = c_prompt[i]
        cT[:, 1:] = c_sample[sl].T
        convs = state_conv[:, sl].reshape(DEPTH, NSEQ, 3, NB, 128).transpose(4, 0, 3, 1, 2).reshape(128, -1)
        lrus = state_lru[:, sl].reshape(DEPTH, NSEQ, NB, 128).transpose(3, 0, 2, 1).reshape(128, -1)
        glas = state_gla[:, sl].transpose(0, 3, 1, 2, 4).reshape(DEPTH, 128, -1)
        in_maps.append({
            "xT": xT, "cT": cT, "convs": np.ascontiguousarray(convs), "lrus": np.ascontiguousarray(lrus),
            "glas": np.ascontiguousarray(glas), "small": small, "wa2e": wa2e, "consts": consts,
            "adaw": adaw, "wpack": wpack,
        })

    if "nc" not in _CACHE:
        _CACHE["nc"] = build_program()
    nc = _CACHE["nc"]
    res = run_bass_kernel_spmd(nc, in_maps, core_ids=list(range(NCORES)))
    R = res.results

    y_prompt = np.empty((NCORES, SEQ, D), np.float32)
    y_sample = np.empty((NCORES * NSEQ, TS, D), np.float32)
    ncp = np.empty((DEPTH, NCORES, 3, DRNN), np.float32)
    nlp = np.empty((DEPTH, NCORES, DRNN), np.float32)
    ngp = np.empty((DEPTH, NCORES, NH, DK, DVH), np.float32)
    ncs = np.empty((DEPTH, NCORES * NSEQ, 3, DRNN), np.float32)
    nls = np.empty((DEPTH, NCORES * NSEQ, DRNN), np.float32)
    ngs = np.empty((DEPTH, NCORES * NSEQ, NH, DK, DVH), np.float32)
    for i in range(NCORES):
        r = R[i]
        sl = slice(i * NSEQ, (i + 1) * NSEQ)
        yT = np.asarray(r["yT"])
        y_prompt[i] = yT[:, :SEQ].T
        y_sample[sl] = yT[:, SEQ:].T.reshape(NSEQ, TS, D)
        ncp[:, i] = np.asarray(r["ncp"]).reshape(128, DEPTH, NB, 3).transpose(1, 3, 2, 0).reshape(DEPTH, 3, DRNN)
        nlp[:, i] = np.asarray(r["nlp"]).reshape(128, DEPTH, NB).transpose(1, 2, 0).reshape(DEPTH, DRNN)
        ngp[:, i] = np.asarray(r["ngp"]).reshape(DEPTH, 128, NH, DVH).transpose(0, 2, 1, 3)
        ncs[:, sl] = np.asarray(r["ncs"]).reshape(128, DEPTH, NB, NSEQ, 3).transpose(1, 3, 4, 2, 0).reshape(DEPTH, NSEQ, 3, DRNN)
        nls[:, sl] = np.asarray(r["nls"]).reshape(128, DEPTH, NB, NSEQ).transpose(1, 3, 2, 0).reshape(DEPTH, NSEQ, DRNN)
        ngs[:, sl] = np.asarray(r["ngs"]).reshape(DEPTH, 128, NSEQ, NH, DVH).transpose(0, 2, 3, 1, 4)
    return (y_prompt, y_sample, ncp, nlp, ngp, ncs, nls, ngs)
```

```python
import contextlib
import numpy as np
import concourse.bass as bass
import concourse.mybir as mybir
from concourse.bass_utils import run_bass_kernel_spmd

F32 = mybir.dt.float32
BF16 = mybir.dt.bfloat16
AF = mybir.ActivationFunctionType
ALU = mybir.AluOpType

NCORES = 8
D = 1024
KC = 8
SEQ = 2048
DEPTH = 2
NB = 10
DRNN = 1280
NH = 4
DK = 128
DVH = 256
DFF = 2816
FC = 22
NIN = 7696
EPS = 1e-6
NSEQ = 16
TS = 4
WS = NSEQ * TS
NTOK = SEQ + WS
TW = 512
O_X, O_G, O_Q, O_K, O_V, O_R, O_LR, O_GA, O_GB = 0, 1280, 2560, 3072, 3584, 4608, 5632, 5648, 6672
SLOT = 4096
NSLOT = 4


def _kc_pack(W):
    nk = W.shape[0] // 128
    return np.ascontiguousarray(W.reshape(nk, 128, -1).transpose(1, 0, 2).reshape(128, -1))


def _unit_list():
    u = [("Ag0", 4096), ("Ag1", 4096), ("Ag2", 2048)]
    u += [("Ax%d" % n, 1280) for n in range(NB)]
    u += [("Glr", 384), ("Gv0", 4096), ("Gv1", 4096), ("Gk", 4096), ("Gq", 4096)]
    u += [("Gr%d" % h, 2048) for h in range(NH)]
    for j in range(KC):
        u += [("Cg%d" % j, 2048), ("Cp%d" % j, 2304)]
    u += [("Wo0", 4096), ("Wo1", 4096)]
    u += [("F1_%d" % i, 2048) for i in range(FC)]
    u += [("F2_%d" % j, 2816) for j in range(KC)]
    return u


UNITS = _unit_list()
UOFF = {}
_o = 0
for _n, _s in UNITS:
    UOFF[_n] = (_o, _s)
    _o += _s
WPACK_N = _o


def _pack_layer(w_in, lru_wa, lru_wx, proj_a, proj_b, w_out, ffn_w1, ffn_w2):
    parts = {}
    parts["Ag0"] = _kc_pack(w_in[:, O_G:O_G + 512])
    parts["Ag1"] = _kc_pack(w_in[:, O_G + 512:O_G + 1024])
    parts["Ag2"] = _kc_pack(w_in[:, O_G + 1024:O_G + 1280])
    for n in range(NB):
        gw = [lru_wa[n - 1], lru_wx[n - 1]] if n > 0 else [np.zeros((128, 256), np.float32)]
        parts["Ax%d" % n] = np.concatenate([_kc_pack(w_in[:, O_X + 128 * n:O_X + 128 * n + 128])] + gw, axis=1)
    parts["Glr"] = np.concatenate([_kc_pack(w_in[:, O_LR:O_LR + 16]), lru_wa[NB - 1], lru_wx[NB - 1]], axis=1)
    parts["Gk"] = _kc_pack(w_in[:, O_K:O_K + 512])
    parts["Gq"] = _kc_pack(w_in[:, O_Q:O_Q + 512])
    parts["Gv0"] = _kc_pack(w_in[:, O_V:O_V + 512])
    parts["Gv1"] = _kc_pack(w_in[:, O_V + 512:O_V + 1024])
    for h in range(NH):
        parts["Gr%d" % h] = _kc_pack(w_in[:, O_R + 256 * h:O_R + 256 * h + 256])
    for j in range(KC):
        parts["Cg%d" % j] = _kc_pack(np.concatenate(
            [w_in[:, O_GA + 128 * j:O_GA + 128 * j + 128], w_in[:, O_GB + 128 * j:O_GB + 128 * j + 128]], axis=1))
        parts["Cp%d" % j] = np.concatenate(
            [_kc_pack(proj_a[:, 128 * j:128 * j + 128]), _kc_pack(proj_b[:, 128 * j:128 * j + 128])], axis=1)
    parts["Wo0"] = _kc_pack(w_out[:, 0:512])
    parts["Wo1"] = _kc_pack(w_out[:, 512:1024])
    for i in range(FC):
        parts["F1_%d" % i] = _kc_pack(np.concatenate(
            [ffn_w1[:, 128 * i:128 * i + 128], ffn_w1[:, DFF + 128 * i:DFF + 128 * i + 128]], axis=1))
    for j in range(KC):
        parts["F2_%d" % j] = _kc_pack(ffn_w2[:, 128 * j:128 * j + 128])
    out = np.empty((128, WPACK_N), np.float32)
    for n, s in UNITS:
        o, _ = UOFF[n]
        assert parts[n].shape == (128, s), (n, parts[n].shape, s)
        out[:, o:o + s] = parts[n]
    return out


SM = {}
_o = 0
for _l in range(DEPTH):
    for _n, _s in (("n1g", 8), ("n2g", 8), ("adab", 48), ("convw", 40), ("convb", 10), ("ba", 10),
                   ("bx", 10), ("lam", 10), ("gng", 8)):
        SM[(_n, _l)] = (_o, _s)
        _o += _s
SM["fg"] = (_o, 8)
_o += 8
SM_N = _o

C_TRI, C_U, C_MASK, C_TRIS, C_US, C_MASKS, C_SEQM = 0, 128, 256, 384, 448, 512, 576
C_N = 592


def _fm(v, nch):
    return np.ascontiguousarray(v.reshape(nch, 128).T)


def _consts():
    c = np.zeros((128, C_N), np.float32)
    j = np.arange(128)[:, None]
    i = np.arange(128)[None, :]
    c[:, C_TRI:C_TRI + 128] = (j <= i) * (-1.0 / 16.0)
    c[:, C_U:C_U + 128] = (j > i) * (-1.0 / 16.0)
    c[:, C_MASK:C_MASK + 128] = (j <= i) * 1.0
    j = np.arange(64)[:, None]
    i = np.arange(64)[None, :]
    same = (j // TS) == (i // TS)
    c[:64, C_TRIS:C_TRIS + 64] = (same & (j <= i)) * (-1.0 / 16.0)
    c[:64, C_US:C_US + 64] = (same & (j > i)) * (-1.0 / 16.0)
    c[:64, C_MASKS:C_MASKS + 64] = (same & (j <= i)) * 1.0
    c[:64, C_SEQM:C_SEQM + 16] = ((np.arange(64)[:, None] // TS) == np.arange(16)[None, :]) * 1.0
    return c


_STRICT = False


class _Op:
    __slots__ = ("eng", "fn", "deps", "dma", "sig", "sigval", "needs")

    def __init__(self, eng, fn, deps, dma):
        self.eng, self.fn, self.deps, self.dma = eng, fn, deps, dma
        self.sig = False
        self.sigval = 0
        self.needs = []


class Prog:
    def __init__(self):
        self.ops = []
        self.lastw = {}
        self.rd = {}

    def add(self, eng, fn, reads=(), writes=(), dma=None):
        idx = len(self.ops)
        deps = {}
        for k in reads:
            lw = self.lastw.get(k)
            if lw is not None:
                deps[lw] = "raw"
        for k in writes:
            lw = self.lastw.get(k)
            if lw is not None and lw not in deps:
                deps[lw] = "waw"
            r = self.rd.get(k)
            if r:
                for e, v in r.items():
                    if e == "dmas":
                        for x in v:
                            deps.setdefault(x, "war")
                    else:
                        deps.setdefault(v, "war")
        for k in reads:
            r = self.rd.setdefault(k, {})
            if dma is not None:
                r.setdefault("dmas", []).append(idx)
            else:
                r[eng] = idx
        for k in writes:
            self.lastw[k] = idx
            self.rd[k] = {}
        deps.pop(idx, None)
        self.ops.append(_Op(eng, fn, deps, dma))
        return idx

    def finalize(self):
        ops = self.ops
        for b in ops:
            for ai, kind in b.deps.items():
                a = ops[ai]
                if a.dma is not None:
                    need = True
                elif a.eng == b.eng:
                    if a.eng == "pe":
                        need = False
                    elif b.dma is not None:
                        need = True
                    else:
                        need = kind == "raw" or _STRICT
                else:
                    need = True
                if need:
                    a.sig = True
                    b.needs.append(ai)
        cnt = {}
        for o in ops:
            if o.dma is not None:
                cnt[o.dma] = cnt.get(o.dma, 0) + 16
                o.sigval = cnt[o.dma]
                o.sig = True
            elif o.sig:
                cnt[o.eng] = cnt.get(o.eng, 0) + 1
                o.sigval = cnt[o.eng]
        self.final_counts = cnt

    def emit(self, eng_name, eng, sems):
        waited = {}
        ops = self.ops
        for o in ops:
            if o.eng != eng_name:
                continue
            req = {}
            for ai in o.needs:
                a = ops[ai]
                sk = a.dma if a.dma is not None else a.eng
                if a.sigval > req.get(sk, 0):
                    req[sk] = a.sigval
            for sk, v in req.items():
                if waited.get(sk, 0) < v:
                    eng.wait_ge(sems[sk], v)
                    waited[sk] = v
            ins = o.fn(eng)
            if o.sig:
                sk = o.dma if o.dma is not None else o.eng
                ins.then_inc(sems[sk], 16 if o.dma is not None else 1)
        return waited


def _flat(*items):
    out = []
    for it in items:
        if isinstance(it, (tuple, list)):
            out.extend(_flat(*it))
        elif it is not None:
            out.append(it)
    return tuple(out)


def build_program():
    nc = bass.Bass("TRN2", target_bir_lowering=False)
    P = Prog()
    es = contextlib.ExitStack()

    def dram_in(name, shape):
        return nc.dram_tensor(name, list(shape), F32, kind="ExternalInput").ap()

    def dram_out(name, shape):
        return nc.dram_tensor(name, list(shape), F32, kind="ExternalOutput").ap()

    xT_d = dram_in("xT", [D, NTOK])
    cT_d = dram_in("cT", [D, 17])
    convs_d = dram_in("convs", [128, DEPTH * NB * NSEQ * 3])
    lrus_d = dram_in("lrus", [128, DEPTH * NB * NSEQ])
    glas_d = dram_in("glas", [DEPTH, 128, NSEQ * NH * DVH])
    small_d = dram_in("small", [128, SM_N])
    wa2e_d = dram_in("wa2e", [17, DEPTH * 512])
    consts_d = dram_in("consts", [128, C_N])
    adaw_d = dram_in("adaw", [DEPTH, 128, 12 * 4096])
    wpack_d = dram_in("wpack", [DEPTH, 128, WPACK_N])

    yT_d = dram_out("yT", [D, NTOK])
    ncp_d = dram_out("ncp", [128, DEPTH * NB * 3])
    nlp_d = dram_out("nlp", [128, DEPTH * NB])
    ngp_d = dram_out("ngp", [DEPTH, 128, NH * DVH])
    ncs_d = dram_out("ncs", [128, DEPTH * NB * NSEQ * 3])
    nls_d = dram_out("nls", [128, DEPTH * NB * NSEQ])
    ngs_d = dram_out("ngs", [DEPTH, 128, NSEQ * NH * DVH])

    dma_sems = []

    def sb(name, shape, dt=F32, dma=False):
        t = es.enter_context(nc.sbuf_tensor("sb_" + name, list(shape), dt))
        if dma:
            dma_sems.append("d_" + name)
        return t

    with es:
        wring = [sb("wr%d" % i, [128, SLOT], BF16, dma=True) for i in range(NSLOT)]
        S = sb("S", [128, DEPTH, NH, DVH], F32, dma=True)
        Sbf = sb("Sbf", [128, NH, DVH], BF16)
        Sbfc = sb("Sbfc", [128, 3, DVH], BF16)
        Sx = sb("Sx", [128, DVH], F32)
        modT = sb("modT", [128, DEPTH, 48, 17], F32)
        small = sb("small", [128, SM_N], F32, dma=True)
        slru = sb("slru", [128, DEPTH, NB], F32)
        hslru = sb("hslru", [128, DEPTH, NB], F32)
        sptmp = sb("sptmp", [128, DEPTH, NB], F32)
        hbias = sb("hbias", [128, DEPTH, 2, NB], F32)
        consts = sb("consts", [128, C_N], F32, dma=True)
        wa2e = sb("wa2e", [17, DEPTH * 512], F32, dma=True)
        ones_bf = sb("ones_bf", [128, 128], BF16)
        cT = sb("cT", [128, KC, 17], F32, dma=True)
        csb = sb("csb", [128, KC, 17], BF16)
        convc = sb("convc", [128, DEPTH, NB, 3], F32, dma=True)
        lruc = sb("lruc", [128, DEPTH, NB], F32, dma=True)
        convst = sb("convst", [128, DEPTH, NB, NSEQ, 3], F32, dma=True)
        lrust = sb("lrust", [128, DEPTH, NB, NSEQ], F32, dma=True)
        xas = sb("xas", [128, NSEQ, 7], F32)
        HS = 8
        khm = sb("khm", [64, HS, 128], BF16)
        GS = 4
        s0b = sb("s0b", [128, NSEQ, DVH], BF16, dma=True)
        sst01 = [sb("sst%d" % i, [128, GS, DVH], F32, dma=True) for i in range(2)]
        dummy = sb("dummy", [128, 2], F32)
        dpre = sb("dpre", [128, 2], F32)
        ccorr = sb("ccorr", [128, NB, 3], F32)
        ctmp = sb("ctmp", [128, NB], F32)
        swide = sb("swide", [64, 512], F32)

        psum = [es.enter_context(nc.psum_tensor("ps%d" % i, [128, TW], F32)) for i in range(8)]

        class Grp:
            pass

        def make_group(kind):
            g = Grp()
            g.kind = kind
            g.pre = kind + ":"
            W = TW if kind == "p" else WS
            g.W = W
            g.CH = 128 if kind == "p" else 64
            g.nch = W // g.CH
            n = kind
            g.x = sb(n + "x", [128, KC, W], F32)
            dma_sems.extend(["d_" + n + "x%d" % c for c in range(KC)])
            g.xsq = sb(n + "xsq", [128, KC, W], BF16)
            g.m = g.xsq
            g.h = sb(n + "h", [128, KC, W], BF16)
            g.yag = sb(n + "yag", [128, NB, W], BF16)
            g.yb = sb(n + "yb", [128, KC, W], BF16)
            words = FC * W // 2
            if kind == "s":
                words = max(words, 9 * W, 512 + 256 + 512)
            g.big = sb(n + "big", [128, words], F32)
            big = g.big
            g.ff = big[:, 0:FC * W // 2].bitcast(BF16).rearrange("p (c w) -> p c w", w=W)
            nch = g.nch
            g.v_tm = big[:, 0:nch * 512].bitcast(BF16).rearrange("p (c w) -> p c w", w=1024)
            g.khat = big[:, nch * 512:nch * 768].bitcast(BF16).rearrange("p (c w) -> p c w", w=512)
            g.gT = big[:, nch * 768:nch * 1280].rearrange("p (c w) -> p c w", w=512)
            g.lbuf = [big[:, i * W:(i + 1) * W] for i in range(9)]
            g.qt = sb(n + "qt", [128, NH, W], BF16)
            g.kt = sb(n + "kt", [128, NH, W], BF16)
            g.at_bf = sb(n + "at_bf", [g.CH, nch, g.CH], BF16)
            g.ulr = sb(n + "ulr", [32, W], F32)
            g.dec = sb(n + "dec", [128, NH, 16], F32)
            g.NF = 8
            g.fsc = [sb(n + "fs%d" % i, [128, W], F32, dma=True) for i in range(g.NF)]
            g.NBF = 6
            g.bsc = [sb(n + "bs%d" % i, [128, W], BF16) for i in range(g.NBF)]
            g.st = {"ps": 0, "fs": 0, "bs": 0, "psw": 0, "psh": 0}
            g.LKEYS = tuple(g.pre + "lb%d" % i for i in range(9))
            g.GKEYS = (g.pre + "gT", g.pre + "khat", g.pre + "vtm")
            g.FKEYS = (g.pre + "ff",)
            return g

        gp = make_group("p")
        gs = make_group("s")
        _xf = gp.xsq[:].rearrange("p c w -> p (c w)").bitcast(F32)
        sst = [sst01[0][:], sst01[1][:],
               _xf[:, 0:GS * DVH].rearrange("p (s v) -> p s v", v=DVH),
               _xf[:, GS * DVH:2 * GS * DVH].rearrange("p (s v) -> p s v", v=DVH)]
        sstk = ["sst0", "sst1", "sst2", "sst3"]
        dma_sems.extend(["d_sst2", "d_sst3"])

        eng_sems = {}
        for e in ("pe", "act", "dve", "pool", "sp"):
            eng_sems[e] = es.enter_context(nc.semaphore("sem_" + e))
        sems = dict(eng_sems)
        for k in dma_sems:
            sems[k] = es.enter_context(nc.semaphore(k))

        def PS(g):
            if g.kind == "p":
                nps = g.st.get("nps", 4)
                banks = list(range(nps))
                if not g.st.get("in_heads"):
                    banks.append(6)
                i = g.st["ps"] % len(banks)
                g.st["ps"] = (i + 1) % len(banks)
                b = banks[i]
                return psum[b], ("ps%d" % b,)
            if g.st.get("restrict"):
                return psum[4], ("ps4",)
            i = g.st["ps"]
            g.st["ps"] = (i + 1) % 2
            return psum[4 + i], ("ps%d" % (4 + i),)

        PSW = PS
        PSH = PS

        def PO(g, vc):
            if g.kind == "p":
                return psum[6 + vc], ("ps%d" % (6 + vc),)
            return psum[5][:, vc * 64:(vc + 1) * 64], ("ps5",)

        def WIDE(g):
            if g.kind == "p":
                return FS(g)
            return swide, "s:swide"

        def FS(g):
            i = g.st["fs"]
            g.st["fs"] = (i + 1) % g.NF
            return g.fsc[i], g.pre + "fs%d" % i

        def BS(g):
            i = g.st["bs"]
            g.st["bs"] = (i + 1) % g.NBF
            return g.bsc[i], g.pre + "bs%d" % i

        def mm(out, lhsT, rhs, start, stop, reads, writes):
            P.add("pe", lambda e, o=out, l=lhsT, r=rhs, a=start, b=stop: e.matmul(o, lhsT=l, rhs=r, start=a, stop=b),
                  reads=_flat(reads), writes=_flat(writes))

        def act(out, in_, func, reads, writes, bias=None, scale=None):
            kw = {}
            if bias is not None:
                kw["bias"] = bias
            if scale is not None:
                kw["scale"] = scale
            P.add("act", lambda e, o=out, i=in_, f=func, kw=kw: e.activation(out=o, in_=i, func=f, **kw),
                  reads=_flat(reads), writes=_flat(writes))

        def tt(out, in0, in1, op, reads, writes):
            P.add("dve", lambda e, o=out, a=in0, b=in1, p=op: e.tensor_tensor(out=o, in0=a, in1=b, op=p),
                  reads=_flat(reads), writes=_flat(writes))

        def stt(out, in0, scalar, in1, op0, op1, reads, writes):
            P.add("dve", lambda e, o=out, a=in0, s=scalar, b=in1, p0=op0, p1=op1:
                  e.scalar_tensor_tensor(out=o, in0=a, scalar=s, in1=b, op0=p0, op1=p1),
                  reads=_flat(reads), writes=_flat(writes))

        def ts(out, in0, s1, s2, op0, op1, reads, writes):
            if s2 is None:
                P.add("dve", lambda e, o=out, a=in0, s=s1, p0=op0: e.tensor_scalar(out=o, in0=a, scalar1=s, scalar2=None, op0=p0),
                      reads=_flat(reads), writes=_flat(writes))
            else:
                P.add("dve", lambda e, o=out, a=in0, s=s1, t=s2, p0=op0, p1=op1:
                      e.tensor_scalar(out=o, in0=a, scalar1=s, scalar2=t, op0=p0, op1=p1),
                      reads=_flat(reads), writes=_flat(writes))

        def vcopy(out, in_, reads, writes):
            P.add("dve", lambda e, o=out, i=in_: e.tensor_copy(out=o, in_=i), reads=_flat(reads), writes=_flat(writes))

        def vmemset(out, val, writes):
            P.add("dve", lambda e, o=out, v=val: e.memset(o, v), reads=(), writes=_flat(writes))

        def scan(out, a, b, initial, reads, writes):
            P.add("dve", lambda e, o=out, a=a, b=b, ini=initial:
                  e.tensor_tensor_scan(out=o, data0=a, data1=b, initial=ini, op0=ALU.mult, op1=ALU.add),
                  reads=_flat(reads), writes=_flat(writes))

        def dma(q, out, in_, reads, writes, sem):
            P.add(q, lambda e, o=out, i=in_: e.dma_start(out=o, in_=i), reads=_flat(reads), writes=_flat(writes), dma=sem)

        def fence(keys):
            P.add("dve", lambda e: e.memset(dummy[:], 0.0), reads=(), writes=_flat(keys, "dummy"))

        def smc(name, l=None):
            o, s = SM[(name, l)] if l is not None else SM[name]
            return small[:, o:o + s]

        wstream = []
        ADA_SETUP = 4
        for u in range(ADA_SETUP):
            wstream.append((adaw_d[0, :, u * 4096:(u + 1) * 4096], 4096))
        ADA_AFTER = {}
        for k in range(8):
            ADA_AFTER[(0, k)] = [(0, 4 + k)]
            ADA_AFTER[(1, k)] = [(1, 4 + k)]
        ADA_AFTER[(0, 8)] = [(1, 0), (1, 1)]
        ADA_AFTER[(0, 9)] = [(1, 2), (1, 3)]
        NT = SEQ // TW
        SEPARATE = False
        unit_index = {}
        for ti in range(NT + (1 if SEPARATE else 0)):
            for l in range(DEPTH):
                for n, s in UNITS:
                    o, _ = UOFF[n]
                    unit_index[(ti, l, n)] = len(wstream)
                    wstream.append((wpack_d[l, :, o:o + s], s))
                    if ti == 0 and n.startswith("Ax"):
                        for (la, u) in ADA_AFTER.get((l, int(n[2:])), []):
                            unit_index[(0, l, "ADA_%d_%d" % (la, u))] = len(wstream)
                            wstream.append((adaw_d[la, :, u * 4096:(u + 1) * 4096], 4096))
        ws = {"loaded": 0, "cur": -1}

        def acquire(u):
            assert u == ws["cur"] + 1, (u, ws["cur"])
            ws["cur"] = u
            lim = min(len(wstream), u + NSLOT)
            while ws["loaded"] < lim:
                j = ws["loaded"]
                src, n = wstream[j]
                sl = j % NSLOT
                dma("pool", wring[sl][:, 0:n], src, reads=(), writes=("wr%d" % sl,), sem="d_wr%d" % sl)
                ws["loaded"] += 1
            sl = u % NSLOT
            return wring[sl], "wr%d" % sl

        dma("sp", small[:, :], small_d[:, :], (), ("small",), "d_small")
        dma("sp", consts[:, :], consts_d[:, :], (), ("consts",), "d_consts")
        dma("sp", wa2e[:, :], wa2e_d[:, :], (), ("wa2e",), "d_wa2e")
        dma("sp", cT[:], cT_d.rearrange("(c p) s -> p c s", p=128), (), ("cT",), "d_cT")
        dma("sp", convst[:].rearrange("p l n s t -> p (l n s t)"), convs_d[:, :], (), ("convst",), "d_convst")
        dma("sp", lrust[:].rearrange("p l n s -> p (l n s)"), lrus_d[:, :], (), ("lrust",), "d_lrust")
        vmemset(ones_bf[:], 1.0, ("ones",))
        vmemset(dpre[:], 1.0, ("dpre",))

        def preload(func):
            P.add("act", lambda e, f=func: e.activation(out=dpre[:, 1:2], in_=dpre[:, 0:1], func=f),
                  reads=("dpre",), writes=("dpre_junk",))
        vmemset(gp.ulr[:], 1.0, ("p:ulr",))
        vmemset(gs.ulr[:], 1.0, ("s:ulr",))
        vmemset(convc[:].rearrange("p l n t -> p (l n t)"), 0.0, ("convc",))
        vmemset(lruc[:].rearrange("p l n -> p (l n)"), 0.0, ("lruc",))
        vmemset(S[:].rearrange("p l h v -> p (l h v)"), 0.0, ("S0", "S1"))
        for l in range(DEPTH):
            act(sptmp[:, l, :], smc("lam", l), AF.Exp, ("small",), ("sptmp",), scale=-1.0)
        for l in range(DEPTH):
            act(sptmp[:, l, :], sptmp[:, l, :], AF.Ln, ("sptmp",), ("sptmp",), bias=1.0)
        sp2 = sptmp[:].rearrange("p l n -> p (l n)")
        ts(slru[:].rearrange("p l n -> p (l n)"), sp2, -8.0, None, ALU.mult, None, ("sptmp",), ("slru",))
        ts(hslru[:].rearrange("p l n -> p (l n)"), sp2, -4.0, None, ALU.mult, None, ("sptmp",), ("hslru",))
        for l in range(DEPTH):
            ts(hbias[:, l, 0, :], smc("ba", l), 0.5, None, ALU.mult, None, ("small",), ("hbias",))
            ts(hbias[:, l, 1, :], smc("bx", l), 0.5, None, ALU.mult, None, ("small",), ("hbias",))
        act(csb[:], cT[:], AF.Silu, ("cT",), ("csb",))
        def ada_unit(l, u, wt, wk, pst, pk):
            wv = wt[:, 0:4096].rearrange("p (k c) -> p k c", c=512)
            for jj in range(4):
                for kc in range(KC):
                    mm(pst[:, jj * 17:(jj + 1) * 17], wv[:, kc, jj * 128:(jj + 1) * 128], csb[:, kc, :],
                       kc == 0, kc == KC - 1, (wk, "csb"), pk)
            o, _ = SM[("adab", l)]
            tt(modT[:, l, u * 4:(u + 1) * 4, :], pst[:, 0:68].rearrange("p (j s) -> p j s", s=17),
               small[:, o + u * 4:o + u * 4 + 4].unsqueeze(2).to_broadcast([128, 4, 17]),
               ALU.add, (pk, "small"), ("modT%d" % l,))

        def ada_finish(l, grp, gname):
            mv = modT[:, l, grp * 8:(grp + 1) * 8, :]
            ts(mv, mv, 1.0, None, ALU.add, None, ("modT%d" % l,), ("modT%d" % l,))
            tt(mv, mv, smc(gname, l).unsqueeze(2).to_broadcast([128, 8, 17]), ALU.mult, ("modT%d" % l, "small"),
               ("modT%d" % l,))

        def ada_after(l, u):
            if u == 3:
                ada_finish(l, 1, "n1g")
            if u == 9:
                ada_finish(l, 4, "n2g")

        for u in range(ADA_SETUP):
            wt, wk = acquire(u)
            pst, pk = PS(gp)
            ada_unit(0, u, wt, wk, pst, pk)
            ada_after(0, u)

        def mod_col(l, grp, c):
            return modT[:, l, grp * 8 + c, 0:1]

        def mod_seq(l, grp, c):
            return modT[:, l, grp * 8 + c, 1:17].unsqueeze(2).to_broadcast([128, NSEQ, TS])

        def v3(ap):
            return ap.rearrange("p (s t) -> p s t", t=TS)

        def rms_stats(g, scale_div):
            W = g.W
            K = g.pre
            if g.kind == "p" and g.st.get("stats_ready"):
                flush_stat(g)
                g.st["stats_ready"] = False
                pst, pk = psum[7], ("ps7",)
                ft, fk = FS(g)
                act(ft[:, 0:W], pst[:, 0:W], AF.Ln, pk, fk, bias=EPS, scale=1.0 / scale_div)
                act(pst[:, 0:W], ft[:, 0:W], AF.Exp, fk, pk, scale=-0.5)
                nf = g.st.get("next_func")
                if nf is not None:
                    preload(nf)
                return pst, pk
            pst, pk = PS(g)
            for c in range(KC):
                act(g.xsq[:, c, 0:W], g.x[:, c, 0:W], AF.Square, K + "x%d" % c, K + "xsq")
                mm(pst[:, 0:W], ones_bf[:], g.xsq[:, c, 0:W], c == 0, c == KC - 1, ("ones", K + "xsq"), pk)
            ft, fk = FS(g)
            act(ft[:, 0:W], pst[:, 0:W], AF.Ln, pk, fk, bias=EPS, scale=1.0 / scale_div)
            act(pst[:, 0:W], ft[:, 0:W], AF.Exp, fk, pk, scale=-0.5)
            return pst, pk

        def norm_mod(g, l, g_grp, sh_grp):
            W = g.W
            K = g.pre
            pst, pk = rms_stats(g, float(D))
            for c in range(KC):
                ft, fk = FS(g)
                if g.kind == "p":
                    stt(ft[:, 0:W], g.x[:, c, 0:W], mod_col(l, g_grp, c), pst[:, 0:W], ALU.mult, ALU.mult,
                        (K + "x%d" % c, "modT%d" % l, pk), fk)
                    act(g.h[:, c, 0:W], ft[:, 0:W], AF.Identity, (fk, "modT%d" % l), K + "h%d" % c, bias=mod_col(l, sh_grp, c))
                else:
                    tt(ft[:, 0:W], g.x[:, c, 0:W], pst[:, 0:W], ALU.mult, (K + "x%d" % c, pk), fk)
                    tt(v3(ft[:, 0:W]), v3(ft[:, 0:W]), mod_seq(l, g_grp, c), ALU.mult, (fk, "modT%d" % l), fk)
                    tt(v3(g.h[:, c, 0:W]), v3(ft[:, 0:W]), mod_seq(l, sh_grp, c), ALU.add, (fk, "modT%d" % l), K + "h%d" % c)

        def dense_T(g, wv, cols, nk, rhs_t, rhs_key, wk):
            W = g.W
            pst, pk = PS(g)
            for kc in range(nk):
                rk = rhs_key[kc] if isinstance(rhs_key, list) else rhs_key
                mm(pst[:, 0:W], wv[:, kc, cols], rhs_t[:, kc, 0:W], kc == 0, kc == nk - 1, (wk, rk), pk)
            return pst, pk

        def flush_stat(g):
            j = g.st.get("pending_stat")
            if j is not None:
                W = g.W
                mm(psum[7][:, 0:W], ones_bf[:], g.h[:, j, 0:W], j == 0, j == KC - 1, ("ones", g.pre + "h%d" % j), "ps7")
                g.st["pending_stat"] = None

        def resid_add(g, l, gt_grp, j, pst, pk):
            W = g.W
            xk = g.pre + "x%d" % j
            if g.kind == "p":
                stt(g.x[:, j, 0:W], pst[:, 0:W], mod_col(l, gt_grp, j), g.x[:, j, 0:W], ALU.mult, ALU.add,
                    (pk, "modT%d" % l, xk), xk)
                flush_stat(g)
                act(g.h[:, j, 0:W], g.x[:, j, 0:W], AF.Square, xk, g.pre + "h%d" % j)
                g.st["pending_stat"] = j
                if j == KC - 1:
                    g.st["stats_ready"] = True
            else:
                ft, fk = FS(g)
                tt(v3(ft[:, 0:W]), v3(pst[:, 0:W]), mod_seq(l, gt_grp, j), ALU.mult, (pk, "modT%d" % l), fk)
                tt(g.x[:, j, 0:W], g.x[:, j, 0:W], ft[:, 0:W], ALU.add, (xk, fk), xk)

        def stage_lru(g, l):
            W = g.W
            K = g.pre
            kind = g.kind
            hk_ = [K + "h%d" % c for c in range(KC)]
            for gi, (nm, nblk) in enumerate((("Ag0", 4), ("Ag1", 4), ("Ag2", 2))):
                wt, wk = yield nm
                wv = wt[:, 0:8 * 128 * nblk].rearrange("p (k c) -> p k c", c=128 * nblk)
                for b in range(nblk):
                    n = gi * 4 + b
                    pst, pk = dense_T(g, wv, slice(b * 128, (b + 1) * 128), KC, g.h, hk_, wk)
                    act(g.yag[:, n, 0:W], pst[:, 0:W], AF.Gelu_apprx_tanh, pk, K + "yag%d" % n)
            fence(g.FKEYS + g.LKEYS)
            if kind == "s":
                dma("pool", s0b[:], glas_d[l].rearrange("p (s h v) -> p s h v", h=NH, v=DVH)[:, :, 0, :], (), "s0b", "d_s0b")
            if kind == "p":
                preload(AF.Tanh)
                for hd in range(NH):
                    act(Sbf[:, hd, :], S[:, l, hd, :], AF.Copy, "S%d" % l, "Sbf")
            cwo, _ = SM[("convw", l)]
            cbo, _ = SM[("convb", l)]
            stt_ = {}
            if kind == "p":
                def pmul(o, a, b):
                    P.add("pool", lambda e, o=o, a=a, b=b: e.tensor_tensor(out=o, in0=a, in1=b, op=ALU.mult),
                          reads=("convc", "small", "ccorr"), writes=("ccorr",))

                def padd(o, a, b):
                    P.add("pool", lambda e, o=o, a=a, b=b: e.tensor_tensor(out=o, in0=a, in1=b, op=ALU.add),
                          reads=("ccorr",), writes=("ccorr",))
                wv_ = lambda i: small[:, cwo + i * NB:cwo + (i + 1) * NB]
                cv_ = lambda j: convc[:, l, :, j]
                pmul(ccorr[:, :, 0], wv_(2), cv_(2))
                pmul(ctmp[:, :], wv_(1), cv_(1))
                padd(ccorr[:, :, 0], ccorr[:, :, 0], ctmp[:, :])
                pmul(ctmp[:, :], wv_(0), cv_(0))
                padd(ccorr[:, :, 0], ccorr[:, :, 0], ctmp[:, :])
                pmul(ccorr[:, :, 1], wv_(1), cv_(2))
                pmul(ctmp[:, :], wv_(0), cv_(1))
                padd(ccorr[:, :, 1], ccorr[:, :, 1], ctmp[:, :])
                pmul(ccorr[:, :, 2], wv_(0), cv_(2))

            def S1(n, wt, wk):
                wv = wt[:, 0:1024].rearrange("p (k c) -> p k c", c=128)
                pst, pk = dense_T(g, wv, slice(0, 128), KC, g.h, hk_, wk)
                acc, ak = g.lbuf[n % 3], K + "lb%d" % (n % 3)
                wcol = lambda i: small[:, cwo + i * NB + n:cwo + i * NB + n + 1]
                cb = small[:, cbo + n:cbo + n + 1]
                if kind == "p":
                    w3, w2, w1, w0 = wcol(3), wcol(2), wcol(1), wcol(0)
                    ops = [lambda: ts(acc[:, 0:W], pst[:, 0:W], w3, cb, ALU.mult, ALU.add, (pk, "small"), ak)]
                    for i, wi in ((2, w2), (1, w1), (0, w0)):
                        sh = 3 - i
                        ops.append(lambda sh=sh, wi=wi: stt(acc[:, sh:W], pst[:, 0:W - sh], wi, acc[:, sh:W], ALU.mult, ALU.add,
                                                            (pk, "small", ak), ak))
                    ops.append(lambda: tt(acc[:, 0:3], acc[:, 0:3], ccorr[:, n, :], ALU.add, (ak, "ccorr"), ak))
                    ops.append(lambda: vcopy(convc[:, l, n, :], pst[:, W - 3:W], pk, "convc"))
                    xcb, xk = BS(g)
                    ops.append(lambda: P.add("pool", lambda e, o=xcb[:, 0:W], i=acc[:, 0:W]: e.tensor_copy(out=o, in_=i),
                                             reads=_flat(ak), writes=_flat(xk)))
                    stt_[n] = {"acc": (acc, ak), "xcb": (xcb, xk)}
                    return ops
                else:
                    a3 = v3(acc[:, 0:W])
                    w3, w2, w1, w0 = wcol(3), wcol(2), wcol(1), wcol(0)
                    xcb, xk = BS(g)
                    ops = [lambda: vcopy(xas[:, :, 0:3], convst[:, l, n, :, :], "convst", "xas"),
                           lambda: act(xas[:, :, 3:7], v3(pst[:, 0:W]), AF.Copy, pk, "xas"),
                           lambda: ts(a3, xas[:, :, 3:7], w3, cb, ALU.mult, ALU.add, ("xas", "small"), ak)]
                    for i, wi in ((2, w2), (1, w1), (0, w0)):
                        ops.append(lambda i=i, wi=wi: stt(a3, xas[:, :, i:i + 4], wi, a3, ALU.mult, ALU.add, ("xas", "small", ak), ak))
                    ops.append(lambda: vcopy(convst[:, l, n, :, :], xas[:, :, 4:7], "xas", "convst"))
                    ops.append(lambda: act(xcb[:, 0:W], acc[:, 0:W], AF.Copy, ak, xk))
                    stt_[n] = {"acc": (acc, ak), "xcb": (xcb, xk)}
                    return ops

            def S2(n, wt, wk):
                o = 1024 if n < NB - 1 else 128
                xcb, xk = stt_[n]["xcb"]
                pr, prk = PS(g)
                mm(pr[:, 0:W], wt[:, o:o + 128], xcb[:, 0:W], True, True, (wk, xk), prk)
                pi, pik = PS(g)
                mm(pi[:, 0:W], wt[:, o + 128:o + 256], xcb[:, 0:W], True, True, (wk, xk), pik)
                if kind == "p":
                    tr, trk, ti_, tik = pr, prk, pi, pik
                else:
                    tr, trk = FS(g)
                    ti_, tik = FS(g)
                act(tr[:, 0:W], pr[:, 0:W], AF.Tanh, (prk, "hbias"), trk, bias=hbias[:, l, 0, n:n + 1], scale=0.5)
                act(ti_[:, 0:W], pi[:, 0:W], AF.Tanh, (pik, "hbias"), tik, bias=hbias[:, l, 1, n:n + 1], scale=0.5)
                pr, prk, pi, pik = tr, trk, ti_, tik
                at, atk = g.lbuf[5 + n % 2], K + "lb%d" % (5 + n % 2)
                act(at[:, 0:W], pr[:, 0:W], AF.Exp, (prk, "hslru"), atk, scale=hslru[:, l, n:n + 1], bias=hslru[:, l, n:n + 1])
                act(pr[:, 0:W], pr[:, 0:W], AF.Exp, (prk, "slru"), prk, scale=slru[:, l, n:n + 1], bias=slru[:, l, n:n + 1])
                stt_[n].update({"pr": (pr, prk), "pi": (pi, pik), "at": (at, atk)})

            def S2b(n):
                pr, prk = stt_[n]["pr"]
                act(pr[:, 0:W], pr[:, 0:W], AF.Sqrt, prk, prk, scale=-0.25, bias=0.25)

            def S3(n):
                acc, ak = stt_[n]["acc"]
                pr, prk = stt_[n]["pr"]
                pi, pik = stt_[n]["pi"]
                at, atk = stt_[n]["at"]
                it, ik = g.lbuf[3 + n % 2], K + "lb%d" % (3 + n % 2)
                ht, hk = g.lbuf[7 + n % 2], K + "lb%d" % (7 + n % 2)
                if kind == "p":
                    yk = K + "yag%d" % n
                    return [
                        lambda: stt(it[:, 0:W], pi[:, 0:W], 1.0, acc[:, 0:W], ALU.add, ALU.mult, (pik, ak), ik),
                        lambda: tt(it[:, 0:W], it[:, 0:W], pr[:, 0:W], ALU.mult, (ik, prk), ik),
                        lambda: scan(ht[:, 0:W], at[:, 0:W], it[:, 0:W], lruc[:, l, n:n + 1], (atk, ik, "lruc"), hk),
                        lambda: vcopy(lruc[:, l, n:n + 1], ht[:, W - 1:W], hk, "lruc"),
                        lambda: P.add("pool", lambda e, o=g.yag[:, n, 0:W], a=ht[:, 0:W], b=g.yag[:, n, 0:W]:
                                      e.tensor_tensor(out=o, in0=a, in1=b, op=ALU.mult),
                                      reads=_flat(hk, yk), writes=_flat(yk)),
                    ]
                a3 = v3(at[:, 0:W])
                b3 = v3(it[:, 0:W])
                tmp, tk = FS(g)
                yk = K + "yag%d" % n
                return [
                    lambda: stt(it[:, 0:W], pi[:, 0:W], 1.0, acc[:, 0:W], ALU.add, ALU.mult, (pik, ak), ik),
                    lambda: tt(it[:, 0:W], it[:, 0:W], pr[:, 0:W], ALU.mult, (ik, prk), ik),
                    lambda: tt(tmp[:, 0:NSEQ], a3[:, :, 0], lrust[:, l, n, :], ALU.mult, (atk, "lrust"), tk),
                    lambda: tt(b3[:, :, 0], b3[:, :, 0], tmp[:, 0:NSEQ], ALU.add, (ik, tk), ik),
                    lambda: vmemset(a3[:, :, 0], 0.0, atk),
                    lambda: scan(ht[:, 0:W], at[:, 0:W], it[:, 0:W], 0.0, (atk, ik), hk),
                    lambda: vcopy(lrust[:, l, n, :], v3(ht[:, 0:W])[:, :, 3], hk, "lrust"),
                    lambda: tt(g.yag[:, n, 0:W], ht[:, 0:W], g.yag[:, n, 0:W], ALU.mult, (hk, yk), yk),
                ]

            glr = None
            wt = wk = None
            for k in range(NB + 2):
                if k < NB:
                    wt, wk = yield "Ax%d" % k
                elif k == NB:
                    wt, wk = yield "Glr"
                    glr = (wt, wk)
                    wvl = wt[:, 0:128].rearrange("p (k c) -> p k c", c=16)
                    psl, pslk = PS(g)
                    for kc in range(KC):
                        mm(psl[0:16, 0:W], wvl[:, kc, :], g.h[:, kc, 0:W], kc == 0, kc == KC - 1, (wk, hk_), pslk)
                    act(g.ulr[0:16, 0:W], psl[0:16, 0:W], AF.Copy, pslk, K + "ulr")
                a_ops = S3(k - 2) if 0 <= k - 2 < NB else None
                b_ops = S1(k, wt, wk) if k < NB else None
                a_ops, b_ops = list(a_ops or []), list(b_ops or [])
                while a_ops or b_ops:
                    if a_ops:
                        a_ops.pop(0)()
                    if b_ops:
                        b_ops.pop(0)()
                if 0 <= k - 1 < NB:
                    S2(k - 1, wt, wk)
                    yield ("sync", k)
                    S2b(k - 1)
                if g.st.get("ada1"):
                    for (la, u) in ADA_AFTER.get((l, k), []):
                        wt2, wk2 = yield "ADA_%d_%d" % (la, u)
                        if kind == "p":
                            ada_unit(la, u, wt2, wk2, psum[7], ("ps7",))
                            ada_after(la, u)
            return glr

        def stage_gla(g, l, glr, last_tile):
            W = g.W
            K = g.pre
            kind = g.kind
            CH, nch = g.CH, g.nch
            hk_ = [K + "h%d" % c for c in range(KC)]
            if kind == "p":
                c_tri = consts[0:CH, C_TRI:C_TRI + 128]
                c_u = consts[0:CH, C_U:C_U + 128]
                c_mask = consts[0:CH, C_MASK:C_MASK + 128]
            else:
                c_tri = consts[0:CH, C_TRIS:C_TRIS + 64]
                c_u = consts[0:CH, C_US:C_US + 64]
                c_mask = consts[0:CH, C_MASKS:C_MASKS + 64]
            gT, khat, v_tm, qt, kt, at_bf, dec, ulr = g.gT, g.khat, g.v_tm, g.qt, g.kt, g.at_bf, g.dec, g.ulr
            fence(g.LKEYS + g.GKEYS)
            if kind == "p":
                preload(AF.Ln)
            Sk, Sbk = "S%d" % l, "Sbf"
            for c in range(nch):
                pst, pk = PSW(g)
                mm(pst[0:CH, 0:512], ulr[0:17, c * CH:(c + 1) * CH], wa2e[0:17, l * 512:(l + 1) * 512], True, True,
                   (K + "ulr", "wa2e"), pk)
                ft, fk = (FS(g) if kind == "p" else (None, None))
                if kind == "p":
                    act(ft[0:CH, :], pst[0:CH, :], AF.Exp, pk, fk, scale=-1.0)
                    act(gT[0:CH, c, :], ft[0:CH, :], AF.Ln, fk, K + "gT", bias=1.0)
                else:
                    act(pst[0:CH, :], pst[0:CH, :], AF.Exp, pk, pk, scale=-1.0)
                    act(gT[0:CH, c, :], pst[0:CH, :], AF.Ln, pk, K + "gT", bias=1.0)
            for half in range(2):
                wt, wk = yield "Gv%d" % half
                wv = wt[:, 0:4096].rearrange("p (k c) -> p k c", c=512)
                for c in range(nch):
                    pst, pk = PSW(g)
                    for kc in range(KC):
                        mm(pst[0:CH, 0:512], g.h[:, kc, c * CH:(c + 1) * CH], wv[:, kc, :], kc == 0, kc == KC - 1,
                           (wk, hk_), pk)
                    act(v_tm[0:CH, c, half * 512:(half + 1) * 512], pst[0:CH, :], AF.Copy, pk, K + "vtm")
            wt, wk = yield "Gk"
            wv = wt[:, 0:4096].rearrange("p (k c) -> p k c", c=512)
            for c in range(nch):
                pkd, pkdk = PSW(g)
                mm(pkd[0:CH, 0:512], c_u, gT[0:CH, c, :], True, True, ("consts", K + "gT"), pkdk)
                kd, kdk = WIDE(g)
                act(kd[0:CH, :], pkd[0:CH, :], AF.Exp, pkdk, kdk)
                pst, pk = PSW(g)
                for kc in range(KC):
                    mm(pst[0:CH, 0:512], g.h[:, kc, c * CH:(c + 1) * CH], wv[:, kc, :], kc == 0, kc == KC - 1,
                       (wk, hk_), pk)
                tt(khat[0:CH, c, :], pst[0:CH, :], kd[0:CH, :], ALU.mult, (pk, kdk), K + "khat")
            eplus = []
            for hd in range(NH):
                pb, pbk = PS(g)
                for c in range(nch):
                    mm(pb[:, c * CH:(c + 1) * CH], gT[0:CH, c, hd * 128:(hd + 1) * 128], c_tri, True, True,
                       (K + "gT", "consts"), pbk)
                ep, epk = FS(g)
                act(ep[:, 0:W], pb[:, 0:W], AF.Exp, pbk, epk)
                em, emk = FS(g)
                act(em[:, 0:W], pb[:, 0:W], AF.Exp, pbk, emk, scale=-1.0)
                if kind == "p":
                    vcopy(dec[:, hd, 0:nch], ep[:, 0:W].rearrange("p (c t) -> p c t", t=CH)[:, :, CH - 1], epk, K + "dec")
                else:
                    vcopy(dec[:, hd, 0:NSEQ], v3(ep[:, 0:W])[:, :, TS - 1], epk, K + "dec")
                eplus.append((ep, epk))
                pst, pk = dense_T(g, wv, slice(hd * 128, (hd + 1) * 128), KC, g.h, hk_, wk)
                tt(kt[:, hd, 0:W], pst[:, 0:W], em[:, 0:W], ALU.mult, (pk, emk), K + "kt")
            wt, wk = yield "Gq"
            wv = wt[:, 0:4096].rearrange("p (k c) -> p k c", c=512)
            for hd in range(NH):
                ep, epk = eplus[hd]
                pst, pk = dense_T(g, wv, slice(hd * 128, (hd + 1) * 128), KC, g.h, hk_, wk)
                stt(qt[:, hd, 0:W], pst[:, 0:W], float(DK) ** -0.5, ep[:, 0:W], ALU.mult, ALU.mult, (pk, epk), K + "qt")
            gno, _ = SM[("gng", l)]
            if kind == "p":
                hs_ = {}

                def Xu_mm(hd, wt, wk):
                    wv = wt[:, 0:2048].rearrange("p (k c) -> p k c", c=256)
                    hs_[hd] = {"ur": [dense_T(g, wv, slice(vc * 128, (vc + 1) * 128), KC, g.h, hk_, wk) for vc in range(2)]}

                def Xu_act(hd):
                    srs = []
                    for vc in range(2):
                        pst, pk = hs_[hd]["ur"][vc]
                        sr, srk = FS(g)
                        act(sr[:, 0:W], pst[:, 0:W], AF.Silu, pk, srk)
                        srs.append((sr, srk))
                    hs_[hd]["sr"] = srs

                def Xa(hd):
                    pa, pak = PS(g)
                    for c in range(nch):
                        mm(pa[0:CH, c * CH:(c + 1) * CH], kt[:, hd, c * CH:(c + 1) * CH], qt[:, hd, c * CH:(c + 1) * CH],
                           True, True, (K + "kt", K + "qt"), pak)
                    dS = [PS(g), PS(g)]
                    for c in range(nch):
                        dst, dsk = dS[c // 2]
                        mm(dst[:, (c % 2) * 256:(c % 2 + 1) * 256], khat[0:CH, c, hd * 128:(hd + 1) * 128],
                           v_tm[0:CH, c, hd * 256:(hd + 1) * 256], True, True, (K + "khat", K + "vtm"), dsk)
                    tt(at_bf[0:CH, 0:nch, 0:CH], pa[0:CH, 0:W].rearrange("p (c t) -> p c t", t=CH),
                       c_mask.unsqueeze(1).to_broadcast([CH, nch, CH]), ALU.mult, (pak, "consts"), K + "at_bf")
                    bufs = [(S[:, l, hd, :], Sk), (Sx[:, :], "Sx")]
                    for c in range(nch):
                        dst, dsk = dS[c // 2]
                        src, srck = bufs[c % 2]
                        out, outk = bufs[(c + 1) % 2]
                        stt(out, src, dec[:, hd, c:c + 1], dst[:, (c % 2) * 256:(c % 2 + 1) * 256],
                            ALU.mult, ALU.add, (srck, K + "dec", dsk), outk)
                        if c < nch - 1:
                            act(Sbfc[:, c, :], out, AF.Copy, outk, "Sbfc%d" % c)

                def X4(hd):
                    po = [PO(g, 0), PO(g, 1)]
                    for c in range(nch):
                        for vc in range(2):
                            pot, pok = po[vc]
                            mm(pot[:, c * CH:(c + 1) * CH], v_tm[0:CH, c, hd * 256 + vc * 128:hd * 256 + (vc + 1) * 128],
                               at_bf[0:CH, c, 0:CH], True, False, (K + "vtm", K + "at_bf"), pok)
                            if c == 0:
                                mm(pot[:, 0:CH], Sbf[:, hd, vc * 128:(vc + 1) * 128], qt[:, hd, 0:CH], False, True,
                                   (Sbk, K + "qt"), pok)
                            else:
                                mm(pot[:, c * CH:(c + 1) * CH], Sbfc[:, c - 1, vc * 128:(vc + 1) * 128],
                                   qt[:, hd, c * CH:(c + 1) * CH], False, True, ("Sbfc%d" % (c - 1), K + "qt"), pok)
                    hs_[hd]["po"] = po

                def YA_evac(hd):
                    po = hs_[hd]["po"]
                    pn, pnk = PS(g)
                    for vc in range(2):
                        cidx = hd * 2 + vc
                        pot, pok = po[vc]
                        sr, srk = hs_[hd]["sr"][vc]
                        o2, o2k = BS(g)
                        act(o2[:, 0:W], pot[:, 0:W], AF.Square, pok, o2k)
                        mm(pn[:, 0:W], ones_bf[:], o2[:, 0:W], vc == 0, vc == 1, ("ones", o2k), pnk)
                        stt(sr[:, 0:W], pot[:, 0:W], small[:, gno + cidx:gno + cidx + 1], sr[:, 0:W], ALU.mult, ALU.mult,
                            (pok, "small", srk, o2k), srk)
                    hs_[hd]["pn"] = (pn, pnk)

                def YA_norm(hd):
                    pn, pnk = hs_[hd]["pn"]
                    ft, fk = FS(g)
                    act(ft[:, 0:W], pn[:, 0:W], AF.Ln, pnk, fk, bias=EPS, scale=1.0 / DVH)
                    act(ft[:, 0:W], ft[:, 0:W], AF.Exp, fk, fk, scale=-0.5)
                    hs_[hd]["rs"] = (ft, fk)
                    if hd == NH - 1:
                        preload(AF.Sigmoid)

                def YB(hd):
                    rs, rsk = hs_[hd]["rs"]
                    for vc in range(2):
                        cidx = hd * 2 + vc
                        sr, srk = hs_[hd]["sr"][vc]
                        tt(g.yb[:, cidx, 0:W], sr[:, 0:W], rs[:, 0:W], ALU.mult, (srk, rsk), K + "yb")

                g.st["in_heads"] = True
                wt, wk = yield "Gr0"
                Xu_mm(0, wt, wk)
                Xu_act(0)
                Xa(0)
                for hd in range(NH):
                    if hd + 1 < NH:
                        wt, wk = yield "Gr%d" % (hd + 1)
                    X4(hd)
                    yield ("h", hd, 0)
                    if hd + 1 < NH:
                        Xu_mm(hd + 1, wt, wk)
                    yield ("h", hd, 1)
                    YA_evac(hd)
                    if hd + 1 < NH:
                        Xu_act(hd + 1)
                    YA_norm(hd)
                    yield ("h", hd, 2)
                    if hd + 1 < NH:
                        Xa(hd + 1)
                    yield ("h", hd, 3)
                    YB(hd)
                g.st["in_heads"] = False
            else:
                gsrc = glas_d[l].rearrange("p (s h v) -> p s h v", h=NH, v=DVH)
                gdst = ngs_d[l].rearrange("p (s h v) -> p s h v", h=NH, v=DVH)
                fence(("p:xsq", "sst2", "sst3"))
                NG = NSEQ // GS

                def sst_load(gi):
                    hd_, grp_ = divmod(gi, NG)
                    bi = gi % 4
                    dma("sp", sst[bi], gsrc[:, grp_ * GS:(grp_ + 1) * GS, hd_, :], (), sstk[bi], "d_" + sstk[bi])

                sst_load(0)
                sst_load(1)
                hs_ = {}

                def Xu_s(hd, wt, wk):
                    wv = wt[:, 0:2048].rearrange("p (k c) -> p k c", c=256)
                    srs = []
                    for vc in range(2):
                        pst, pk = dense_T(g, wv, slice(vc * 128, (vc + 1) * 128), KC, g.h, hk_, wk)
                        sr, srk = FS(g)
                        act(sr[:, 0:W], pst[:, 0:W], AF.Silu, pk, srk)
                        srs.append((sr, srk))
                    hs_[hd] = {"sr": srs}

                def seg0(hd):
                    pa, pak = PS(g)
                    mm(pa[0:CH, 0:CH], kt[:, hd, 0:CH], qt[:, hd, 0:CH], True, True, (K + "kt", K + "qt"), pak)
                    tt(at_bf[0:CH, 0:1, 0:CH], pa[0:CH, 0:W].rearrange("p (c t) -> p c t", t=CH),
                       c_mask.unsqueeze(1).to_broadcast([CH, 1, CH]), ALU.mult, (pak, "consts"), K + "at_bf")

                def seg1(hd):
                    g.st["restrict"] = True
                    for vc in range(2):
                        pot, pok = PO(g, vc)
                        mm(pot[:, 0:W], v_tm[0:CH, 0, hd * 256 + vc * 128:hd * 256 + (vc + 1) * 128],
                           at_bf[0:CH, 0, 0:CH], True, False, (K + "vtm", K + "at_bf"), pok)
                        for s_ in range(NSEQ):
                            mm(pot[:, s_ * TS:(s_ + 1) * TS], s0b[:, s_, vc * 128:(vc + 1) * 128],
                               qt[:, hd, s_ * TS:(s_ + 1) * TS], False, s_ == NSEQ - 1, ("s0b", K + "qt"), pok)
                    if hd + 1 < NH:
                        dma("pool", s0b[:], gsrc[:, :, hd + 1, :], (), "s0b", "d_s0b")

                def seg2(hd):
                    pn, pnk = PS(g)
                    for vc in range(2):
                        cidx = hd * 2 + vc
                        pot, pok = PO(g, vc)
                        sr, srk = hs_[hd]["sr"][vc]
                        o2, o2k = BS(g)
                        act(o2[:, 0:W], pot[:, 0:W], AF.Square, (pok, "s:poevac"), o2k)
                        mm(pn[:, 0:W], ones_bf[:], o2[:, 0:W], vc == 0, vc == 1, ("ones", o2k), pnk)
                        stt(sr[:, 0:W], pot[:, 0:W], small[:, gno + cidx:gno + cidx + 1], sr[:, 0:W], ALU.mult, ALU.mult,
                            (pok, "small", srk, o2k), (srk, "s:poevac"))
                    ft, fk = FS(g)
                    act(ft[:, 0:W], pn[:, 0:W], AF.Ln, pnk, fk, bias=EPS, scale=1.0 / DVH)
                    act(ft[:, 0:W], ft[:, 0:W], AF.Exp, fk, fk, scale=-0.5)
                    hs_[hd]["rs"] = (ft, fk)
                    g.st["restrict"] = False

                def seg3(hd):
                    for grp in range(NG):
                        gi = hd * NG + grp
                        if gi + 2 < NH * NG:
                            sst_load(gi + 2)
                        bi = gi % 4
                        bt, bk = sst[bi], sstk[bi]
                        if (grp * GS) % HS == 0:
                            hf = (grp * GS) // HS
                            tt(khm[:, :, :], khat[0:CH, 0, hd * 128:(hd + 1) * 128].unsqueeze(1).to_broadcast([CH, HS, 128]),
                               consts[0:CH, C_SEQM + hf * HS:C_SEQM + (hf + 1) * HS].unsqueeze(2).to_broadcast([CH, HS, 128]),
                               ALU.mult, (K + "khat", "consts"), "khm")
                        for pair in range(GS // 2):
                            pss, psk = PS(g)
                            for q in range(2):
                                sg = pair * 2 + q
                                s_ = grp * GS + sg
                                mm(pss[:, q * 256:(q + 1) * 256], khm[:, s_ % HS, :], v_tm[0:CH, 0, hd * 256:(hd + 1) * 256],
                                   True, True, ("khm", K + "vtm"), psk)
                            for q in range(2):
                                sg = pair * 2 + q
                                s_ = grp * GS + sg
                                stt(bt[:, sg, :], bt[:, sg, :], dec[:, hd, s_:s_ + 1], pss[:, q * 256:(q + 1) * 256],
                                    ALU.mult, ALU.add, (bk, K + "dec", psk), bk)
                        dma("sp", gdst[:, grp * GS:(grp + 1) * GS, hd, :], bt, bk, (), "d_" + bk)
                    if hd == NH - 1:
                        fence(("sst2", "sst3", "p:xsq"))

                def seg4(hd):
                    rs, rsk = hs_[hd]["rs"]
                    for vc in range(2):
                        cidx = hd * 2 + vc
                        sr, srk = hs_[hd]["sr"][vc]
                        tt(g.yb[:, cidx, 0:W], sr[:, 0:W], rs[:, 0:W], ALU.mult, (srk, rsk), K + "yb")

                wt, wk = yield "Gr0"
                Xu_s(0, wt, wk)
                for hd in range(NH):
                    if hd + 1 < NH:
                        wt, wk = yield "Gr%d" % (hd + 1)
                        Xu_s(hd + 1, wt, wk)
                    seg0(hd)
                    yield ("h", hd, 0)
                    seg1(hd)
                    yield ("h", hd, 1)
                    seg2(hd)
                    yield ("h", hd, 2)
                    seg3(hd)
                    yield ("h", hd, 3)
                    seg4(hd)
            if kind == "p" and last_tile:
                dma("sp", ngp_d[l], S[:, l, :, :].rearrange("p h v -> p (h v)"), Sk, (), "d_S")

        def stage_merge(g, l):
            W = g.W
            K = g.pre
            hk_ = [K + "h%d" % c for c in range(KC)]
            yak = tuple(K + "yag%d" % n for n in range(NB))
            for j in range(KC):
                wt, wk = yield "Cg%d" % j
                wv = wt[:, 0:2048].rearrange("p (k c) -> p k c", c=256)
                pa, pak = dense_T(g, wv, slice(0, 128), KC, g.h, hk_, wk)
                sa, sak = FS(g)
                act(sa[:, 0:W], pa[:, 0:W], AF.Sigmoid, pak, sak)
                pb, pbk = dense_T(g, wv, slice(128, 256), KC, g.h, hk_, wk)
                sbt, sbk = FS(g)
                act(sbt[:, 0:W], pb[:, 0:W], AF.Sigmoid, pbk, sbk)
                if j == KC - 1 and g.kind == "p":
                    preload(AF.Ln)
                wt, wk = yield "Cp%d" % j
                wpa = wt[:, 0:1280].rearrange("p (k c) -> p k c", c=128)
                wpb = wt[:, 1280:2304].rearrange("p (k c) -> p k c", c=128)
                ppa, ppak = dense_T(g, wpa, slice(0, 128), NB, g.yag, yak, wk)
                tt(sa[:, 0:W], sa[:, 0:W], ppa[:, 0:W], ALU.mult, (sak, ppak), sak)
                ppb, ppbk = dense_T(g, wpb, slice(0, 128), KC, g.yb, K + "yb", wk)
                tt(sbt[:, 0:W], sbt[:, 0:W], ppb[:, 0:W], ALU.mult, (sbk, ppbk), sbk)
                tt(g.m[:, j, 0:W], sa[:, 0:W], sbt[:, 0:W], ALU.add, (sak, sbk), K + "xsq")
            for jj in range(2):
                wt, wk = yield "Wo%d" % jj
                wv = wt[:, 0:4096].rearrange("p (k c) -> p k c", c=512)
                for j4 in range(4):
                    j = jj * 4 + j4
                    pst, pk = dense_T(g, wv, slice(j4 * 128, (j4 + 1) * 128), KC, g.m, K + "xsq", wk)
                    resid_add(g, l, 2, j, pst, pk)

        def stage_ffn(g, l):
            W = g.W
            K = g.pre
            hk_ = [K + "h%d" % c for c in range(KC)]
            fence(g.GKEYS + g.FKEYS)
            for i in range(FC):
                wt, wk = yield "F1_%d" % i
                wv = wt[:, 0:2048].rearrange("p (k c) -> p k c", c=256)
                p1, p1k = dense_T(g, wv, slice(0, 128), KC, g.h, hk_, wk)
                s1, s1k = FS(g)
                act(s1[:, 0:W], p1[:, 0:W], AF.Silu, p1k, s1k)
                p2, p2k = dense_T(g, wv, slice(128, 256), KC, g.h, hk_, wk)
                tt(g.ff[:, i, 0:W], s1[:, 0:W], p2[:, 0:W], ALU.mult, (s1k, p2k), K + "ff")
                if i == FC - 1 and g.kind == "p":
                    preload(AF.Ln)
            for j in range(KC):
                wt, wk = yield "F2_%d" % j
                wv = wt[:, 0:2816].rearrange("p (k c) -> p k c", c=128)
                pst, pk = dense_T(g, wv, slice(0, 128), FC, g.ff, K + "ff", wk)
                resid_add(g, l, 5, j, pst, pk)

        def final_norm(g, col0, next_col0=None):
            W = g.W
            K = g.pre
            pst, pk = rms_stats(g, float(D))
            fo, _ = SM["fg"]
            for c in range(KC):
                ft, fk = FS(g)
                stt(ft[:, 0:W], g.x[:, c, 0:W], small[:, fo + c:fo + c + 1], pst[:, 0:W], ALU.mult, ALU.mult,
                    (K + "x%d" % c, "small", pk), fk)
                dma("sp", yT_d[c * 128:(c + 1) * 128, col0:col0 + W], ft[:, 0:W], fk, (), "d_" + g.kind + fk.split(":")[1])
                if next_col0 is not None:
                    dma("sp", g.x[:, c, 0:W], xT_d[c * 128:(c + 1) * 128, next_col0:next_col0 + W], (), K + "x%d" % c,
                        "d_" + g.kind + "x%d" % c)
            if next_col0 is not None:
                g.st["x_preloaded"] = True

        def tile_prog(g, col0, last_tile, next_col0=None):
            W = g.W
            xkeys = tuple(g.pre + "x%d" % c for c in range(KC))
            g.st["stats_ready"] = False
            if g.st.get("x_preloaded"):
                g.st["x_preloaded"] = False
            else:
                for c in range(KC):
                    dma("sp", g.x[:, c, 0:W], xT_d[c * 128:(c + 1) * 128, col0:col0 + W], (), xkeys[c],
                        "d_" + g.kind + "x%d" % c)
            for l in range(DEPTH):
                g.st["next_func"] = AF.Gelu_apprx_tanh
                norm_mod(g, l, 1, 0)
                glr = yield from stage_lru(g, l)
                yield from stage_gla(g, l, glr, last_tile)
                yield from stage_merge(g, l)
                g.st["next_func"] = AF.Silu
                norm_mod(g, l, 4, 3)
                yield from stage_ffn(g, l)
                yield ("layer_end", l)
            g.st["next_func"] = None
            final_norm(g, col0, next_col0)

        def run_lockstep(ti, gens):
            reqs = [next(gen, None) for gen in gens]
            cur_l = 0
            while any(r is not None for r in reqs):
                assert all(r == reqs[0] for r in reqs), reqs
                name = reqs[0]
                if isinstance(name, tuple):
                    if name[0] == "layer_end":
                        cur_l = name[1] + 1
                    slot = None
                else:
                    slot = acquire(unit_index[(ti, cur_l, name)])
                reqs = _send_all(gens, slot)

        def _send_all(gens, slot):
            out = []
            for gen in gens:
                try:
                    out.append(gen.send(slot))
                except StopIteration:
                    out.append(None)
            return out

        for ti in range(NT):
            last_tile = ti == NT - 1
            gp.st["ada1"] = gs.st["ada1"] = (ti == 0)
            gp.st["nps"] = 4 if ti == 0 else 6
            gens = [tile_prog(gp, ti * TW, last_tile, None if last_tile else (ti + 1) * TW)]
            if ti == 0 and not SEPARATE:
                gens.append(tile_prog(gs, SEQ, False))
            run_lockstep(ti, gens)
            if last_tile:
                dma("sp", ncp_d[:, :], convc[:].rearrange("p l n t -> p (l n t)"), "convc", (), "d_convc")
                dma("sp", nlp_d[:, :], lruc[:].rearrange("p l n -> p (l n)"), "lruc", (), "d_lruc")
        if SEPARATE:
            run_lockstep(NT, [tile_prog(gs, SEQ, False)])
        dma("sp", ncs_d[:, :], convst[:].rearrange("p l n s t -> p (l n s t)"), "convst", (), "d_convst")
        dma("sp", nls_d[:, :], lrust[:].rearrange("p l n s -> p (l n s)"), "lrust", (), "d_lrust")

        P.finalize()
        block = es.enter_context(nc.Block())

        @block.tensor
        def _(e):
            P.emit("pe", e, sems)

        @block.scalar
        def _(e):
            P.emit("act", e, sems)

        @block.vector
        def _(e):
            P.emit("dve", e, sems)

        @block.gpsimd
        def _(e):
            P.emit("pool", e, sems)

        @block.sync
        def _(e):
            waited = P.emit("sp", e, sems)
            for sk, v in P.final_counts.items():
                if sk.startswith("d_") and waited.get(sk, 0) < v:
                    e.wait_ge(sems[sk], v)
    return nc


_CACHE = {}


def kernel(x_prompt, x_sample, c_prompt, c_sample, state_conv, state_lru, state_gla,
           norm1_g, norm2_g, ada_w, ada_b, w_in, conv_w, conv_b, lru_wa, lru_ba, lru_wx, lru_bx,
           lru_lambda, gla_wa2, gla_ba, gla_norm_g, proj_a, proj_b, w_out, ffn_w1, ffn_w2, final_g):
    f = lambda a: np.asarray(a, dtype=np.float32)
    x_prompt, x_sample, c_prompt, c_sample = f(x_prompt), f(x_sample), f(c_prompt), f(c_sample)
    state_conv, state_lru, state_gla = f(state_conv), f(state_lru), f(state_gla)
    norm1_g, norm2_g, ada_w, ada_b, w_in = f(norm1_g), f(norm2_g), f(ada_w), f(ada_b), f(w_in)
    conv_w, conv_b, lru_wa, lru_ba, lru_wx, lru_bx = f(conv_w), f(conv_b), f(lru_wa), f(lru_ba), f(lru_wx), f(lru_bx)
    lru_lambda, gla_wa2, gla_ba, gla_norm_g = f(lru_lambda), f(gla_wa2), f(gla_ba), f(gla_norm_g)
    proj_a, proj_b, w_out, ffn_w1, ffn_w2, final_g = f(proj_a), f(proj_b), f(w_out), f(ffn_w1), f(ffn_w2), f(final_g)

    wpack = np.stack([_pack_layer(w_in[l], lru_wa[l], lru_wx[l], proj_a[l], proj_b[l], w_out[l], ffn_w1[l], ffn_w2[l])
                      for l in range(DEPTH)])
    adaw = np.stack([np.concatenate([_kc_pack(ada_w[l][:, u * 512:(u + 1) * 512]) for u in range(12)], axis=1)
                     for l in range(DEPTH)])
    small = np.zeros((128, SM_N), np.float32)
    for l in range(DEPTH):
        def put(name, arr):
            o, s = SM[(name, l)]
            small[:, o:o + s] = arr
        put("n1g", _fm(norm1_g[l], 8))
        put("n2g", _fm(norm2_g[l], 8))
        put("adab", _fm(ada_b[l], 48))
        put("convw", np.concatenate([_fm(conv_w[l, i], NB) for i in range(4)], axis=1))
        put("convb", _fm(conv_b[l], NB))
        put("ba", _fm(lru_ba[l], NB))
        put("bx", _fm(lru_bx[l], NB))
        put("lam", _fm(lru_lambda[l], NB))
        put("gng", _fm(gla_norm_g[l], 8))
    o, s = SM["fg"]
    small[:, o:o + s] = _fm(final_g, 8)
    wa2e = np.zeros((17, DEPTH * 512), np.float32)
    for l in range(DEPTH):
        wa2e[0:16, l * 512:(l + 1) * 512] = gla_wa2[l]
        wa2e[16, l * 512:(l + 1) * 512] = gla_ba[l]
    consts = _consts()

    in_maps = []
    for i in range(NCORES):
        sl = slice(i * NSEQ, (i + 1) * NSEQ)
        xT = np.empty((D, NTOK), np.float32)
        xT[:, :SEQ] = x_prompt[i].T
        xT[:, SEQ:] = x_sample[sl].reshape(WS, D).T
        cT = np.empty((D, 17), np.float32)
        cT[:, 0] = c_prompt[i]
        cT[:, 1:] = c_sample[sl].T
        convs = state_conv[:, sl].reshape(DEPTH, NSEQ, 3, NB, 128).transpose(4, 0, 3, 1, 2).reshape(128, -1)
        lrus = state_lru[:, sl].reshape(DEPTH, NSEQ, NB, 128).transpose(3, 0, 2, 1).reshape(128, -1)
        glas = state_gla[:, sl].transpose(0, 3, 1, 2, 4).reshape(DEPTH, 128, -1)
        in_maps.append({
            "xT": xT, "cT": cT, "convs": np.ascontiguousarray(convs), "lrus": np.ascontiguousarray(lrus),
            "glas": np.ascontiguousarray(glas), "small": small, "wa2e": wa2e, "consts": consts,
            "adaw": adaw, "wpack": wpack,
        })

    if "nc" not in _CACHE:
        _CACHE["nc"] = build_program()
    nc = _CACHE["nc"]
    res = run_bass_kernel_spmd(nc, in_maps, core_ids=list(range(NCORES)))
    R = res.results

    y_prompt = np.empty((NCORES, SEQ, D), np.float32)
    y_sample = np.empty((NCORES * NSEQ, TS, D), np.float32)
    ncp = np.empty((DEPTH, NCORES, 3, DRNN), np.float32)
    nlp = np.empty((DEPTH, NCORES, DRNN), np.float32)
    ngp = np.empty((DEPTH, NCORES, NH, DK, DVH), np.float32)
    ncs = np.empty((DEPTH, NCORES * NSEQ, 3, DRNN), np.float32)
    nls = np.empty((DEPTH, NCORES * NSEQ, DRNN), np.float32)
    ngs = np.empty((DEPTH, NCORES * NSEQ, NH, DK, DVH), np.float32)
    for i in range(NCORES):
        r = R[i]
        sl = slice(i * NSEQ, (i + 1) * NSEQ)
        yT = np.asarray(r["yT"])
        y_prompt[i] = yT[:, :SEQ].T
        y_sample[sl] = yT[:, SEQ:].T.reshape(NSEQ, TS, D)
        ncp[:, i] = np.asarray(r["ncp"]).reshape(128, DEPTH, NB, 3).transpose(1, 3, 2, 0).reshape(DEPTH, 3, DRNN)
        nlp[:, i] = np.asarray(r["nlp"]).reshape(128, DEPTH, NB).transpose(1, 2, 0).reshape(DEPTH, DRNN)
        ngp[:, i] = np.asarray(r["ngp"]).reshape(DEPTH, 128, NH, DVH).transpose(0, 2, 1, 3)
        ncs[:, sl] = np.asarray(r["ncs"]).reshape(128, DEPTH, NB, NSEQ, 3).transpose(1, 3, 4, 2, 0).reshape(DEPTH, NSEQ, 3, DRNN)
        nls[:, sl] = np.asarray(r["nls"]).reshape(128, DEPTH, NB, NSEQ).transpose(1, 3, 2, 0).reshape(DEPTH, NSEQ, DRNN)
        ngs[:, sl] = np.asarray(r["ngs"]).reshape(DEPTH, 128, NSEQ, NH, DVH).transpose(0, 2, 3, 1, 4)
    return (y_prompt, y_sample, ncp, nlp, ngp, ncs, nls, ngs)
```
